# Optimizing a Trainium2 kernel written in Bass

```python
import math
import jax
import jax.numpy as jnp
from jax import lax
import numpy as np

D_MODEL = 1024
BATCH = 8
SEQ = 4096
DEPTH = 4

GRID_W = 64
CTX_LEN = 256
NA_HEAD_DIM = 64
NA_HEADS = (D_MODEL // 2) // NA_HEAD_DIM
NA_W = NA_HEADS * NA_HEAD_DIM
NA_WIN_R = 8
NA_WIN_C = 16
GDN_HEAD_DIM = 128
GDN_HEADS = (D_MODEL - NA_W) // GDN_HEAD_DIM
GDN_W = GDN_HEADS * GDN_HEAD_DIM
MIX_W = NA_W + GDN_W
GDN_CONV = 5
GDN_CHUNK = 64
ROPE_BASE = 10000.0
D_FF = 4 * D_MODEL
IN_COLS = 3 * NA_W + 4 * GDN_W + 4 * GDN_HEADS
N_MOD = 6
LN_EPS = 1e-5
NORM_EPS = 1e-6
DEEPNORM_ALPHA = (2 * DEPTH) ** 0.25
DEEPNORM_BETA = (8 * DEPTH) ** -0.25

kernel_name = 'hybrid_na_gdn_diffusion_block'


def layer_norm(x, g, b):
    xf = x.astype(jnp.float32)
    mu = jnp.mean(xf, axis=-1, keepdims=True)
    var = jnp.mean(jnp.square(xf - mu), axis=-1, keepdims=True)
    y = (xf - mu) * lax.rsqrt(var + LN_EPS) * g.astype(jnp.float32) + b.astype(jnp.float32)
    return y.astype(x.dtype)


def l2norm(x):
    xf = x.astype(jnp.float32)
    return (xf * lax.rsqrt(jnp.sum(jnp.square(xf), axis=-1, keepdims=True) + NORM_EPS)).astype(x.dtype)


def gated_rmsnorm(o, z, w):
    of = o.astype(jnp.float32)
    of = of * lax.rsqrt(jnp.mean(jnp.square(of), axis=-1, keepdims=True) + NORM_EPS) * w.astype(jnp.float32)
    y = of.astype(o.dtype) * jax.nn.silu(z)
    return y.reshape(o.shape[0], o.shape[1], -1)


def axial_rope_tables(t):
    pos = jnp.arange(t)
    row = (pos // GRID_W).astype(jnp.float32)
    col = (pos % GRID_W).astype(jnp.float32)
    axis_dim = GDN_HEAD_DIM // 2
    inv_freq = ROPE_BASE ** (-jnp.arange(0, axis_dim, 2, dtype=jnp.float32) / axis_dim)
    ang_r = row[:, None] * inv_freq[None, :]
    ang_c = col[:, None] * inv_freq[None, :]
    return (jnp.cos(ang_r), jnp.sin(ang_r), jnp.cos(ang_c), jnp.sin(ang_c))


def rope_axis(x, cos, sin):
    x1, x2 = jnp.split(x, 2, axis=-1)
    cos = cos[:, None, :].astype(x.dtype)
    sin = sin[:, None, :].astype(x.dtype)
    return jnp.concatenate([x1 * cos - x2 * sin, x2 * cos + x1 * sin], axis=-1)


def rope_2d(x, rope):
    cos_r, sin_r, cos_c, sin_c = rope
    x_row, x_col = jnp.split(x, 2, axis=-1)
    return jnp.concatenate([rope_axis(x_row, cos_r, sin_r), rope_axis(x_col, cos_c, sin_c)], axis=-1)


def centred_conv(x, w):
    k = w.shape[0]
    return lax.conv_general_dilated(
        x, w[:, None, :], window_strides=(1,), padding=[(k // 2, k // 2)],
        dimension_numbers=('NWC', 'WIO', 'NWC'), feature_group_count=x.shape[-1])


def gdn_inputs(qkv, beta_raw, a_raw, conv_w, a_log, dt_bias, rope):
    b, t, _ = qkv.shape
    qkv = jax.nn.silu(centred_conv(qkv, conv_w))
    q, k, v = [u.reshape(b, t, GDN_HEADS, GDN_HEAD_DIM) for u in jnp.split(qkv, 3, axis=-1)]
    q, k = l2norm(q), l2norm(k)
    if rope is not None:
        q, k = rope_2d(q, rope), rope_2d(k, rope)
    q = q * (GDN_HEAD_DIM ** -0.5)
    beta = jax.nn.sigmoid(beta_raw).reshape(b, t, 2, GDN_HEADS)
    a = a_raw.reshape(b, t, 2, GDN_HEADS).astype(jnp.float32)
    g = -jnp.exp(a_log.astype(jnp.float32)) * jax.nn.softplus(a + dt_bias.astype(jnp.float32))
    return q, k, v, beta, g


def unit_lower_inverse(a):
    c = a.shape[-1]
    n = -a
    inv = jnp.eye(c, dtype=a.dtype) + n
    p = n
    for _ in range(int(math.log2(c)) - 1):
        p = p @ p
        inv = inv + inv @ p
    return inv


def gated_delta_chunked(q, k, v, g, beta, s0):
    b, t, h, _ = q.shape
    dv = v.shape[-1]
    c = GDN_CHUNK
    n = t // c

    def to_chunks(u):
        u = u.reshape((b, n, c, h) + u.shape[3:])
        return jnp.moveaxis(u, (1, 3), (0, 2))

    qc, kc, vc, bc = to_chunks(q), to_chunks(k), to_chunks(v), to_chunks(beta)
    gc = jnp.cumsum(to_chunks(g).astype(jnp.float32), axis=-1)
    idx = jnp.arange(c)
    lower = idx[:, None] >= idx[None, :]
    strict = idx[:, None] > idx[None, :]
    decay = jnp.exp(jnp.where(lower, gc[..., :, None] - gc[..., None, :], -jnp.inf)).astype(q.dtype)
    kb = kc * bc[..., None]
    a = jnp.where(strict, jnp.einsum('nbhid,nbhjd->nbhij', kb, kc) * decay, 0.0).astype(q.dtype)
    tinv = unit_lower_inverse(a)
    eg = jnp.exp(gc).astype(q.dtype)
    u = tinv @ (vc * bc[..., None])
    w = tinv @ (kb * eg[..., None])
    attn = jnp.einsum('nbhid,nbhjd->nbhij', qc, kc) * decay
    q_dec = qc * eg[..., None]
    k_dec = kc * jnp.exp(gc[..., -1:] - gc)[..., None].astype(q.dtype)
    chunk_decay = jnp.exp(gc[..., -1]).astype(q.dtype)

    def step(s, xs):
        u_i, w_i, attn_i, qd_i, kd_i, cd_i = xs
        v_new = u_i - jnp.einsum('bhck,bhkv->bhcv', w_i, s)
        o_i = jnp.einsum('bhck,bhkv->bhcv', qd_i, s) + jnp.einsum('bhij,bhjv->bhiv', attn_i, v_new)
        s = s * cd_i[..., None, None] + jnp.einsum('bhck,bhcv->bhkv', kd_i, v_new)
        return s, o_i

    s_final, o = lax.scan(step, s0, (u, w, attn, q_dec, k_dec, chunk_decay))
    o = jnp.moveaxis(o, (0, 2), (1, 3)).reshape(b, t, h, dv)
    return o, s_final


def neighbourhood_attention(q, k, v, k_ctx, v_ctx, rpb, rows):
    b, t, h, d = q.shape
    win_r = min(NA_WIN_R, rows)
    n_band = win_r * GRID_W
    cols = jnp.arange(GRID_W)
    col_start = jnp.clip(cols - NA_WIN_C // 2, 0, GRID_W - NA_WIN_C)
    col_in = (cols[None, :] >= col_start[:, None]) & (cols[None, :] < col_start[:, None] + NA_WIN_C)
    dc_idx = jnp.clip(cols[None, :] - cols[:, None], 1 - NA_WIN_C, NA_WIN_C - 1) + NA_WIN_C - 1
    rpb_cols = rpb[:, :, dc_idx].astype(jnp.float32)
    q = q * (d ** -0.5)

    def row_block(r):
        row_start = jnp.clip(r - win_r // 2, 0, rows - win_r)
        q_r = lax.dynamic_slice_in_dim(q, r * GRID_W, GRID_W, axis=1)
        k_band = lax.dynamic_slice_in_dim(k, row_start * GRID_W, n_band, axis=1)
        v_band = lax.dynamic_slice_in_dim(v, row_start * GRID_W, n_band, axis=1)
        dr_idx = row_start + jnp.arange(win_r) - r + NA_WIN_R - 1
        bias = jnp.take(rpb_cols, dr_idx, axis=1)
        bias = jnp.where(col_in[None, None], bias, -jnp.inf)
        bias = jnp.transpose(bias, (0, 2, 1, 3)).reshape(h, GRID_W, n_band)
        s_lat = jnp.einsum('bqhd,bkhd->bhqk', q_r, k_band).astype(jnp.float32) + bias
        s_ctx = jnp.einsum('bqhd,bkhd->bhqk', q_r, k_ctx).astype(jnp.float32)
        p = jax.nn.softmax(jnp.concatenate([s_lat, s_ctx], axis=-1), axis=-1).astype(v.dtype)
        return (jnp.einsum('bhqk,bkhd->bqhd', p[..., :n_band], v_band)
                + jnp.einsum('bhqk,bkhd->bqhd', p[..., n_band:], v_ctx))

    o = lax.map(row_block, jnp.arange(rows))
    return jnp.moveaxis(o, 0, 1).reshape(b, t, h * d)


def context_attention(q, k, v):
    b, l, h, d = q.shape
    s = jnp.einsum('bqhd,bkhd->bhqk', q * (d ** -0.5), k).astype(jnp.float32)
    p = jax.nn.softmax(s, axis=-1).astype(v.dtype)
    return jnp.einsum('bhqk,bkhd->bqhd', p, v).reshape(b, l, h * d)


def token_mixer(h_lat, h_ctx, w_in, conv_w, a_log, dt_bias, gdn_norm_w, rpb, w_out, rope, with_ctx_out):
    b, t, _ = h_lat.shape
    rows = t // GRID_W
    cuts = [NA_W, 2 * NA_W, 3 * NA_W, 3 * NA_W + 3 * GDN_W, 3 * NA_W + 4 * GDN_W,
            3 * NA_W + 4 * GDN_W + 2 * GDN_HEADS]
    nq, nk, nv, qkv, z, beta_raw, a_raw = jnp.split(h_lat @ w_in, cuts, axis=-1)
    cq, ck, cv, cqkv, cz, cbeta_raw, ca_raw = jnp.split(h_ctx @ w_in, cuts, axis=-1)

    def heads(u, nh, hd):
        return u.reshape(u.shape[0], u.shape[1], nh, hd)

    ck_h, cv_h = heads(ck, NA_HEADS, NA_HEAD_DIM), heads(cv, NA_HEADS, NA_HEAD_DIM)
    na_lat = neighbourhood_attention(heads(nq, NA_HEADS, NA_HEAD_DIM), heads(nk, NA_HEADS, NA_HEAD_DIM),
                                     heads(nv, NA_HEADS, NA_HEAD_DIM), ck_h, cv_h, rpb, rows)

    q, k, v, beta, g = gdn_inputs(qkv, beta_raw, a_raw, conv_w, a_log, dt_bias, rope)
    cq_g, ck_g, cv_g, cbeta, cg = gdn_inputs(cqkv, cbeta_raw, ca_raw, conv_w, a_log, dt_bias, None)
    zeros = jnp.zeros((b, GDN_HEADS, GDN_HEAD_DIM, GDN_HEAD_DIM), q.dtype)

    def flip(u):
        return jnp.flip(u, axis=1)

    co_f, s_f = gated_delta_chunked(cq_g, ck_g, cv_g, cg[:, :, 0], cbeta[:, :, 0], zeros)
    co_b, s_b = gated_delta_chunked(flip(cq_g), flip(ck_g), flip(cv_g), flip(cg[:, :, 1]), flip(cbeta[:, :, 1]), zeros)
    o_f, _ = gated_delta_chunked(q, k, v, g[:, :, 0], beta[:, :, 0], s_f)
    o_b, _ = gated_delta_chunked(flip(q), flip(k), flip(v), flip(g[:, :, 1]), flip(beta[:, :, 1]), s_b)
    gdn_lat = gated_rmsnorm(o_f + flip(o_b), heads(z, GDN_HEADS, GDN_HEAD_DIM), gdn_norm_w)

    y_lat = jnp.concatenate([na_lat, gdn_lat], axis=-1) @ w_out
    if not with_ctx_out:
        return y_lat, None
    na_ctx = context_attention(heads(cq, NA_HEADS, NA_HEAD_DIM), ck_h, cv_h)
    gdn_ctx = gated_rmsnorm(co_f + flip(co_b), heads(cz, GDN_HEADS, GDN_HEAD_DIM), gdn_norm_w)
    y_ctx = jnp.concatenate([na_ctx, gdn_ctx], axis=-1) @ w_out
    return y_lat, y_ctx


def squared_relu_mlp(h, w1, w2):
    return jnp.square(jax.nn.relu(h @ w1)) @ w2


def setup_inputs(seed: int = 0) -> dict:
    key = jax.random.key(seed)
    ks = jax.random.split(key, 20)
    f32 = jnp.float32
    nrm = lambda k, s: jax.random.normal(k, s, f32)
    x = nrm(ks[0], (BATCH, SEQ, D_MODEL))
    c = nrm(ks[1], (BATCH, D_MODEL))
    ctx = nrm(ks[2], (BATCH, CTX_LEN, D_MODEL))
    c_ctx = nrm(ks[3], (D_MODEL,))
    w_ada = nrm(ks[4], (DEPTH, D_MODEL, N_MOD * D_MODEL)) * (0.5 * D_MODEL ** -0.5)
    b_ada = 0.02 * nrm(ks[5], (DEPTH, N_MOD * D_MODEL))
    w_in = nrm(ks[6], (DEPTH, D_MODEL, IN_COLS)) * D_MODEL ** -0.5
    conv_w = nrm(ks[7], (DEPTH, GDN_CONV, 3 * GDN_W)) * GDN_CONV ** -0.5
    a_log = jnp.log(jax.random.uniform(ks[8], (DEPTH, 2, GDN_HEADS), f32, minval=1.0, maxval=16.0))
    dt = jnp.exp(jax.random.uniform(ks[9], (DEPTH, 2, GDN_HEADS), f32,
                                    minval=math.log(1e-3), maxval=math.log(1e-1)))
    dt_bias = dt + jnp.log(-jnp.expm1(-dt))
    gdn_norm_w = 1.0 + 0.02 * nrm(ks[10], (DEPTH, GDN_HEAD_DIM))
    rpb = 0.02 * nrm(ks[11], (DEPTH, NA_HEADS, 2 * NA_WIN_R - 1, 2 * NA_WIN_C - 1))
    w_out = nrm(ks[12], (DEPTH, MIX_W, D_MODEL)) * (MIX_W ** -0.5 * DEEPNORM_BETA)
    ln1_g = 1.0 + 0.02 * nrm(ks[13], (DEPTH, D_MODEL))
    ln1_b = 0.02 * nrm(ks[14], (DEPTH, D_MODEL))
    w_mlp1 = nrm(ks[15], (DEPTH, D_MODEL, D_FF)) * D_MODEL ** -0.5
    w_mlp2 = nrm(ks[16], (DEPTH, D_FF, D_MODEL)) * (D_FF ** -0.5 * DEEPNORM_BETA)
    ln2_g = 1.0 + 0.02 * nrm(ks[17], (DEPTH, D_MODEL))
    ln2_b = 0.02 * nrm(ks[18], (DEPTH, D_MODEL))
    return {'x': x, 'c': c, 'ctx': ctx, 'c_ctx': c_ctx, 'w_ada': w_ada, 'b_ada': b_ada, 'w_in': w_in,
            'conv_w': conv_w, 'a_log': a_log, 'dt_bias': dt_bias, 'gdn_norm_w': gdn_norm_w, 'rpb': rpb,
            'w_out': w_out, 'ln1_g': ln1_g, 'ln1_b': ln1_b, 'w_mlp1': w_mlp1, 'w_mlp2': w_mlp2,
            'ln2_g': ln2_g, 'ln2_b': ln2_b}


def reference(x, c, ctx, c_ctx, w_ada, b_ada, w_in, conv_w, a_log, dt_bias, gdn_norm_w, rpb,
              w_out, ln1_g, ln1_b, w_mlp1, w_mlp2, ln2_g, ln2_b):
    rope = axial_rope_tables(x.shape[1])
    silu_c = jax.nn.silu(c)
    silu_cc = jax.nn.silu(c_ctx)
    x_lat, x_ctx = x, ctx
    for l in range(DEPTH):
        ctx_out = l < DEPTH - 1
        sh1, sc1, g1, sh2, sc2, g2 = jnp.split((silu_c @ w_ada[l] + b_ada[l])[:, None, :], N_MOD, axis=-1)
        csh1, csc1, cg1, csh2, csc2, cg2 = jnp.split(silu_cc @ w_ada[l] + b_ada[l], N_MOD, axis=-1)
        y_lat, y_ctx = token_mixer(x_lat * (1.0 + sc1) + sh1, x_ctx * (1.0 + csc1) + csh1, w_in[l], conv_w[l],
                                   a_log[l], dt_bias[l], gdn_norm_w[l], rpb[l], w_out[l], rope, ctx_out)
        x_lat = layer_norm(DEEPNORM_ALPHA * x_lat + g1 * y_lat, ln1_g[l], ln1_b[l])
        x_lat = layer_norm(DEEPNORM_ALPHA * x_lat
                           + g2 * squared_relu_mlp(x_lat * (1.0 + sc2) + sh2, w_mlp1[l], w_mlp2[l]),
                           ln2_g[l], ln2_b[l])
        if ctx_out:
            x_ctx = layer_norm(DEEPNORM_ALPHA * x_ctx + cg1 * y_ctx, ln1_g[l], ln1_b[l])
            x_ctx = layer_norm(DEEPNORM_ALPHA * x_ctx
                               + cg2 * squared_relu_mlp(x_ctx * (1.0 + csc2) + csh2, w_mlp1[l], w_mlp2[l]),
                               ln2_g[l], ln2_b[l])
    return x_lat
```

```python
import math
from contextlib import ExitStack, contextmanager
import numpy as np
import concourse.bass as bass
import concourse.mybir as mybir
from concourse.bass_utils import run_bass_kernel_spmd

F32 = mybir.dt.float32
BF16 = mybir.dt.bfloat16
AF = mybir.ActivationFunctionType
ALU = mybir.AluOpType

N_DMA_SEMS = 24
SAME_ENGINE_SYNC = {"pe": False, "act": True, "dve": True, "pool": True, "sp": True}

DEPTH = 4
T_LAT = 4096
T_CTX = 256
T_ALL = T_LAT + T_CTX
NT = T_ALL // 128
DM = 1024
IN_COLS = 3600
BIG = 30000.0
ALPHA = (2 * DEPTH) ** 0.25
LN_EPS = 1e-5
NORM_EPS = 1e-6


class Tk:
    __slots__ = ("w", "r")

    def __init__(self):
        self.w = None
        self.r = {}


class KB:
    def __init__(self, nc, st):
        self.nc = nc
        self.E = {"pe": nc.tensor, "act": nc.scalar, "dve": nc.vector, "pool": nc.gpsimd, "sp": nc.sync}
        self.sem = {}
        self.cnt = {}
        for e in self.E:
            self.sem[e] = st.enter_context(nc.semaphore("s_" + e))
            self.cnt[e] = 0
        for q in ("sp", "pool", "act"):
            for i in range(N_DMA_SEMS):
                k = "d%s%d" % (q, i)
                self.sem[k] = st.enter_context(nc.semaphore("s_" + k))
                self.cnt[k] = 0
        self.known = {e: {} for e in self.E}
        self.dma_rr = {"sp": 0, "pool": 0, "act": 0}
        self.n_ins = 0
        self.tk = {}
        self.psum = set()
        self.cur = None
        self.uid = 0

    @contextmanager
    def stage(self):
        prev = self.cur
        with ExitStack() as s:
            self.cur = s
            yield s
            self.barrier()
        self.cur = prev

    def sb(self, name, shape, dt):
        self.uid += 1
        nm = "%s_%d" % (name, self.uid)
        t = self.cur.enter_context(self.nc.sbuf_tensor(nm, list(shape), dt))
        self.tk[nm] = Tk()
        return t

    def ps(self, name, shape, dt=F32):
        self.uid += 1
        nm = "%s_%d" % (name, self.uid)
        t = self.cur.enter_context(self.nc.psum_tensor(nm, list(shape), dt))
        self.tk[nm] = Tk()
        self.psum.add(nm)
        return t

    def _split(self, ins, outs):
        reads, writes = [], []
        for a in ins:
            n = a.name
            if n in self.tk:
                (writes if n in self.psum else reads).append(self.tk[n])
        for a in outs:
            n = a.name
            if n in self.tk:
                writes.append(self.tk[n])
        return reads, writes

    def _deps(self, reads, writes):
        deps = {}
        for t in reads:
            if t.w is not None and t.w[1] > deps.get(t.w[0], 0):
                deps[t.w[0]] = t.w[1]
        for t in writes:
            if t.w is not None and t.w[1] > deps.get(t.w[0], 0):
                deps[t.w[0]] = t.w[1]
            for s, c in t.r.items():
                if c > deps.get(s, 0):
                    deps[s] = c
        return deps

    def _wait(self, e, deps):
        eng = self.E[e]
        kn = self.known[e]
        for s, c in deps.items():
            if kn.get(s, 0) >= c:
                continue
            if s == e and not SAME_ENGINE_SYNC[e]:
                continue
            eng.wait_ge(self.sem[s], c)
            kn[s] = c

    def _mark(self, tag, reads, writes):
        s, c = tag
        for t in reads:
            if t.r.get(s, 0) < c:
                t.r[s] = c
        for t in writes:
            t.w = tag
            t.r = {}

    def op(self, e, fn, ins=(), outs=()):
        reads, writes = self._split(ins, outs)
        self._wait(e, self._deps(reads, writes))
        ins_ = fn(self.E[e])
        self.cnt[e] += 1
        ins_.then_inc(self.sem[e], 1)
        self._mark((e, self.cnt[e]), reads, writes)
        self.n_ins += 1

    def dma(self, q, out, in_, **kw):
        reads, writes = self._split([in_], [out])
        k = "d%s%d" % (q, self.dma_rr[q])
        self.dma_rr[q] = (self.dma_rr[q] + 1) % N_DMA_SEMS
        deps = self._deps(reads, writes)
        if self.cnt[k] > deps.get(k, 0):
            deps[k] = self.cnt[k]
        self._wait(q, deps)
        ins_ = self.E[q].dma_start(out=out, in_=in_, **kw)
        self.cnt[k] += 16
        ins_.then_inc(self.sem[k], 16)
        self._mark((k, self.cnt[k]), reads, writes)
        self.n_ins += 1

    def barrier(self):
        for e in self.E:
            self._wait(e, {s: c for s, c in self.cnt.items() if c > 0 and s != e})

    def mm(self, out, lhsT, rhs, start=True, stop=True):
        self.op("pe", lambda e: e.matmul(out, lhsT=lhsT, rhs=rhs, start=start, stop=stop), [lhsT, rhs], [out])

    def tr(self, out, in_, ident):
        self.op("pe", lambda e: e.transpose(out, in_, ident), [in_, ident], [out])

    def act(self, out, in_, func, scale=None, bias=None):
        kw = {}
        ins = [in_]
        if scale is not None:
            kw["scale"] = scale
            if not isinstance(scale, (int, float)):
                ins.append(scale)
        if bias is not None:
            kw["bias"] = bias
            if not isinstance(bias, (int, float)):
                ins.append(bias)
        self.op("act", lambda e: e.activation(out=out, in_=in_, func=func, **kw), ins, [out])

    def copy(self, e, out, in_):
        if e == "act":
            self.act(out, in_, AF.Copy)
        else:
            self.op(e, lambda g: g.tensor_copy(out=out, in_=in_), [in_], [out])

    def tt(self, e, out, a, b, op):
        self.op(e, lambda g: g.tensor_tensor(out=out, in0=a, in1=b, op=op), [a, b], [out])

    def ts(self, e, out, a, s1, op0, s2=None, op1=None):
        ins = [a] + [s for s in (s1, s2) if s is not None and not isinstance(s, (int, float))]
        if op1 is None:
            self.op(e, lambda g: g.tensor_scalar(out=out, in0=a, scalar1=s1, scalar2=None, op0=op0), ins, [out])
        else:
            self.op(e, lambda g: g.tensor_scalar(out=out, in0=a, scalar1=s1, scalar2=s2, op0=op0, op1=op1), ins, [out])

    def stt(self, e, out, a, scalar, b, op0, op1):
        ins = [a, b] + ([] if isinstance(scalar, (int, float)) else [scalar])
        self.op(e, lambda g: g.scalar_tensor_tensor(out=out, in0=a, scalar=scalar, in1=b, op0=op0, op1=op1), ins, [out])

    def memset(self, e, out, val):
        self.op(e, lambda g: g.memset(out, val), [], [out])


CONST_LAYOUT = {}


def make_consts():
    P = np.arange(128)
    parts = []
    off = 0

    def add(name, arr):
        nonlocal off
        arr = np.asarray(arr, np.float32)
        CONST_LAYOUT[name] = (off, arr.shape[1])
        parts.append(arr)
        off += arr.shape[1]

    add("ident", np.eye(128))
    add("U1", P[:, None] <= P[None, :])
    add("U1T", P[:, None] >= P[None, :])
    add("SL", P[:, None] > P[None, :])
    add("SU", P[:, None] < P[None, :])
    add("ones", np.ones((128, 128)))
    negf = np.where(P[:, None] > P[None, :], 0.0, -BIG)
    negb = np.where(P[:, None] < P[None, :], 0.0, -BIG)
    negtf = np.where(P[None, :] >= P[:, None], 0.0, -BIG)
    negtb = np.where(P[None, :] <= P[:, None], 0.0, -BIG)
    add("NEGf", np.tile(negf, (1, 4)))
    add("NEGb", np.tile(negb, (1, 4)))
    add("NEGTf", np.tile(negtf, (1, 4)))
    add("NEGTb", np.tile(negtb, (1, 4)))
    pi = np.where((P % 64) < 32, P + 32, P - 32)
    rp = np.zeros((128, 128))
    rp[pi, P] = 1.0
    add("Rperm", rp)
    kc = P % 64
    par = P // 64
    qc = np.arange(64)
    cs = np.clip(qc - 8, 0, 48)
    colmask = np.where((kc[:, None] >= cs[None, :]) & (kc[:, None] < cs[None, :] + 16), 0.0, -BIG)
    add("colmask", colmask)
    m9 = np.zeros((128, 9, 64))
    m9[par == 1, 0, :] = -BIG
    m9[par == 0, 8, :] = -BIG
    add("M9", m9.reshape(128, 576))
    dg32 = (P[:, None] // 32 == P[None, :] // 32)
    off1 = (P[:, None] // 64 == P[None, :] // 64) & (P[:, None] // 32 > P[None, :] // 32)
    off2 = (P[:, None] // 64 > P[None, :] // 64)
    add("Dg32", dg32)
    add("Off1", off1)
    add("Off1T", off1.T)
    add("Off2", off2)
    add("Off2T", off2.T)
    return np.concatenate(parts, axis=1).astype(np.float32)


def make_rope():
    t = np.arange(T_LAT)
    row = (t // 64).astype(np.float32)
    col = (t % 64).astype(np.float32)
    inv = (10000.0 ** (-np.arange(0, 64, 2, dtype=np.float32) / 64)).astype(np.float32)
    i = np.arange(128)
    f = i % 32
    pos = np.where((i < 64)[:, None], row[None, :], col[None, :]).astype(np.float32)
    ang = (pos * inv[f][:, None]).astype(np.float32)
    sgn = np.where((i % 64) < 32, -1.0, 1.0)[:, None]
    out = np.stack([np.cos(ang), sgn * np.sin(ang)], axis=1)
    return out.astype(np.float32)


def make_rpb_gather(rpb):
    P = np.arange(128)
    kc = (P % 64)[:, None, None]
    par = (P // 64)[:, None, None]
    ep = np.arange(16)[None, :, None]
    qc = np.arange(64)[None, None, :]
    dr = 7 - ep + par + 0 * qc
    dc = kc - qc + 0 * ep
    valid = (np.abs(dr) <= 7) & (np.abs(dc) <= 15)
    dri = np.clip(dr + 7, 0, 14)
    dci = np.clip(dc + 15, 0, 30)
    g = rpb[:, :, dri, dci]
    g = np.where(valid[None, None], g, np.float32(0.0))
    return np.ascontiguousarray(g.reshape(rpb.shape[0], rpb.shape[1], 128, 1024).astype(np.float32))


def build(n_layers=DEPTH, dbg=(), stages=None):
    make_consts()
    nc = bass.Bass("TRN2", target_bir_lowering=False)
    D = {}

    def din(name, shape, dt=F32):
        D[name] = nc.dram_tensor(name, list(shape), dt, kind="ExternalInput").ap()

    def dscr(name, shape, dt):
        kind = "ExternalOutput" if name in dbg else "Internal"
        D[name] = nc.dram_tensor(name, list(shape), dt, kind=kind).ap()

    ncst = sum(v[1] for v in CONST_LAYOUT.values())
    din("xin", [T_ALL, DM]); din("cvec", [2, DM]); din("consts", [128, ncst]); din("rope", [128, 2, T_LAT])
    din("Gp", [DEPTH, 8, 128, 1024])
    din("w_ada", [DEPTH, DM, 6 * DM]); din("b_ada", [DEPTH, 6 * DM]); din("w_in", [DEPTH, DM, IN_COLS])
    din("conv_w", [DEPTH, 5, 1536]); din("a_log", [DEPTH, 8]); din("dt_bias", [DEPTH, 8]); din("gdn_norm_w", [DEPTH, 128])
    din("w_out", [DEPTH, DM, DM]); din("ln1_g", [DEPTH, DM]); din("ln1_b", [DEPTH, DM])
    din("w_mlp1", [DEPTH, DM, 4 * DM]); din("w_mlp2", [DEPTH, 4 * DM, DM]); din("ln2_g", [DEPTH, DM]); din("ln2_b", [DEPTH, DM])
    D["y"] = nc.dram_tensor("y", [T_LAT, DM], F32, kind="ExternalOutput").ap()

    dscr("winb", [DEPTH, DM, IN_COLS], BF16); dscr("woutb", [DEPTH, DM, DM], BF16)
    dscr("w1b", [DEPTH, DM, 4 * DM], BF16); dscr("w2b", [DEPTH, 4 * DM, DM], BF16)
    dscr("mod_d", [DEPTH, 2, 6 * DM], F32)
    dscr("FM_d", [3072, T_ALL], BF16)
    dscr("v_d", [T_ALL, 512], BF16); dscr("ba_d", [T_ALL, 16], F32)
    dscr("GQK_d", [1024, T_ALL], BF16)
    dscr("Ktok_d", [T_ALL, 512], BF16); dscr("Vtok_d", [T_ALL, 512], BF16)
    dscr("oT_d", [2, 512, T_ALL], F32)
    dscr("aT_d", [1024, T_ALL], BF16)
    dscr("X_d", [T_ALL, DM], F32); dscr("X1_d", [T_ALL, DM], F32); dscr("h2T_d", [DM, T_ALL], BF16)
    dscr("m1T_d", [4 * DM, T_ALL], BF16)

    with ExitStack() as st:
        kb = KB(nc, st)
        if stages is None or "cast" in stages:
            cast_weights(kb, D, n_layers)
        if stages is None or "mod" in stages:
            mod_stage(kb, D, n_layers)
        for l in range(n_layers):
            last = (l == DEPTH - 1)
            xsrc = D["xin"] if l == 0 else D["X_d"]
            if stages is None or "s1" in stages:
                stage1(kb, D, l, xsrc)
            if stages is None or "na" in stages:
                stage_na(kb, D, l, last)
            if stages is None or "g1" in stages:
                stage_g1(kb, D, l)
            if stages is None or "g2" in stages:
                stage_g2(kb, D, l)
            if stages is None or "g3" in stages:
                stage_g3(kb, D, l, last)
            if stages is None or "s4a" in stages:
                stage4a(kb, D, l, xsrc, last)
            if stages is None or "s4b" in stages:
                stage4b(kb, D, l, last)
        kb.barrier()
    nc._n_ins = kb.n_ins
    return nc


def run_gens(gens):
    gens = list(gens)
    while gens:
        for g_ in list(gens):
            try:
                next(g_)
            except StopIteration:
                gens.remove(g_)


def cload(kb, D, name, dt=F32, eng="dve", rows=128):
    off, w = CONST_LAYOUT[name]
    t = kb.sb("c_" + name, [128, w], F32)
    kb.dma("sp", t[:], D["consts"][:, off:off + w])
    if dt == F32:
        return t
    tb = kb.sb("cb_" + name, [128, w], dt)
    kb.copy(eng, tb[:], t[:])
    return tb


def cast_weights(kb, D, n_layers):
    with kb.stage():
        NB = 3
        fb = [kb.sb("cf", [128, 2048], F32) for _ in range(NB)]
        bb = [kb.sb("cb", [128, 2048], BF16) for _ in range(NB)]
        jobs = []
        L = n_layers
        for src, dst, w in (("w_in", "winb", None), ("w_out", "woutb", 2), ("w_mlp1", "w1b", None), ("w_mlp2", "w2b", 2)):
            s2 = D[src][0:L].rearrange("l r c -> (l r) c")
            d2 = D[dst][0:L].rearrange("l r c -> (l r) c")
            if w:
                s2 = s2.rearrange("(a b) c -> a (b c)", b=w)
                d2 = d2.rearrange("(a b) c -> a (b c)", b=w)
            R, C = s2.shape
            for r0 in range(0, R, 128):
                for c0 in range(0, C, 2048):
                    cw = min(2048, C - c0)
                    jobs.append((s2[r0:r0 + 128, c0:c0 + cw], d2[r0:r0 + 128, c0:c0 + cw], cw))
        for i, (s, d, cw) in enumerate(jobs):
            b = i % NB
            kb.dma("sp", fb[b][:, :cw], s)
            kb.copy("dve" if i % 2 == 0 else "act", bb[b][:, :cw], fb[b][:, :cw])
            kb.dma("pool", d, bb[b][:, :cw])


def mod_stage(kb, D, n_layers):
    with kb.stage():
        cT = kb.sb("cT", [128, 2, 8], F32)
        kb.dma("sp", cT[:], D["cvec"].rearrange("r (k p) -> p r k", p=128), allow_slow_non_contiguous=True)
        scl = kb.sb("scl", [128, 8, 33], F32)
        kb.memset("dve", scl[:], 0.0)
        kb.act(scl[:, :, 0], cT[:, 0, :], AF.Silu)
        kb.act(scl[:, :, 32], cT[:, 1, :], AF.Silu)
        wt = [kb.sb("wada", [128, 8, 512], F32) for _ in range(2)]
        bt = [kb.sb("bada", [33, 512], F32) for _ in range(2)]
        mr = [kb.sb("mrow", [33, 512], F32) for _ in range(2)]
        pm = [kb.ps("pm", [128, 512]) for _ in range(2)]
        for b in bt:
            kb.memset("dve", b[:], 0.0)
        i = 0
        for l in range(n_layers):
            for cb in range(12):
                j = i % 2
                i += 1
                cols = slice(cb * 512, (cb + 1) * 512)
                kb.dma("sp", wt[j][:], D["w_ada"][l, :, cols].rearrange("(k p) c -> p k c", p=128))
                kb.dma("sp", bt[j][0:1, :], D["b_ada"][l:l + 1, cols])
                kb.dma("sp", bt[j][32:33, :], D["b_ada"][l:l + 1, cols])
                for kc in range(8):
                    kb.mm(pm[j][0:33, :], scl[:, kc, :], wt[j][:, kc, :], start=(kc == 0), stop=(kc == 7))
                kb.tt("dve", mr[j][:], pm[j][0:33, :], bt[j][:], ALU.add)
                if cb in (2, 3, 8, 9):
                    kb.ts("dve", mr[j][:], mr[j][:], 1.0, ALU.add)
                kb.dma("pool", D["mod_d"][l, 0:1, cols], mr[j][0:1, :])
                kb.dma("pool", D["mod_d"][l, 1:2, cols], mr[j][32:33, :])


def bvec(kb, name, src):
    n = src.shape[-1]
    t = kb.sb(name, [128, n], F32)
    kb.dma("sp", t[:], src.partition_broadcast(128))
    return t


def stage1(kb, D, l, xsrc):
    with kb.stage():
        w = kb.sb("win", [128, 8, IN_COLS], BF16)
        for kc in range(8):
            kb.dma("sp", w[:, kc, :], D["winb"][l, kc * 128:(kc + 1) * 128, :])
        mods = {}
        for r, nm in ((0, "l"), (1, "c")):
            mods[nm] = (bvec(kb, "sc1", D["mod_d"][l, r, 1024:2048]), bvec(kb, "sh1", D["mod_d"][l, r, 0:1024]))
        idb = cload(kb, D, "ident", BF16)
        xt = [kb.sb("xt", [128, DM], F32) for _ in range(2)]
        hb = [kb.sb("hb", [128, DM], BF16) for _ in range(2)]
        hT = [kb.sb("hT", [128, 8, 512], BF16) for _ in range(2)]
        fm = [kb.sb("fm", [128, 6, 512], BF16) for _ in range(4)]
        vt = [kb.sb("vt", [128, 512], BF16) for _ in range(2)]
        bat = [kb.sb("bat", [128, 16], F32) for _ in range(2)]
        pT = [kb.ps("pT", [128, 1024], BF16) for _ in range(2)]
        pF = [kb.ps("pF", [128, 512]) for _ in range(3)]
        pV = kb.ps("pV", [128, 512])
        pB = kb.ps("pB", [128, 512])
        FMv = D["FM_d"].rearrange("(b p) t -> p b t", p=128)
        ti = 0
        ei = 0
        fi = 0
        for g in range(9):
            ntile = 4 if g < 8 else 2
            ntok = ntile * 128
            tok0 = g * 512
            sc, sh = mods["l"] if g < 8 else mods["c"]
            hTg = hT[g % 2]
            for tt_ in range(ntile):
                j = ti % 2
                ti += 1
                rows = slice(tok0 + tt_ * 128, tok0 + (tt_ + 1) * 128)
                kb.dma("sp", xt[j][:], xsrc[rows, :])
                kb.tt("dve", xt[j][:], xt[j][:], sc[:], ALU.mult)
                kb.tt("dve", hb[j][:], xt[j][:], sh[:], ALU.add)
                for kc in range(8):
                    kb.tr(pT[j][:, kc * 128:(kc + 1) * 128], hb[j][:, kc * 128:(kc + 1) * 128], idb[:])
                kb.act(hTg[:, :, tt_ * 128:(tt_ + 1) * 128], pT[j][:].rearrange("p (k t) -> p k t", k=8), AF.Copy)
                for kc in range(8):
                    kb.mm(pV[:], hTg[:, kc, tt_ * 128:(tt_ + 1) * 128], w[:, kc, 1024:1536], start=(kc == 0), stop=(kc == 7))
                for kc in range(8):
                    kb.mm(pB[:, 0:16], hTg[:, kc, tt_ * 128:(tt_ + 1) * 128], w[:, kc, 3584:3600], start=(kc == 0), stop=(kc == 7))
                kb.copy("dve", vt[j][:], pV[:])
                kb.copy("dve", bat[j][:], pB[:, 0:16])
                kb.dma("pool", D["v_d"][rows, :], vt[j][:])
                kb.dma("pool", D["ba_d"][rows, :], bat[j][:])
            for blk in range(24):
                col0 = blk * 128 if blk < 8 else 1536 + (blk - 8) * 128
                p = pF[ei % 3]
                for kc in range(8):
                    kb.mm(p[:, :ntok], w[:, kc, col0:col0 + 128], hTg[:, kc, :ntok], start=(kc == 0), stop=(kc == 7))
                if blk % 6 == 0:
                    fcur = fm[fi % 4]
                    fi += 1
                if blk < 4:
                    kb.act(fcur[:, blk % 6, :ntok], p[:, :ntok], AF.Copy, scale=0.125)
                elif ei % 2 == 0:
                    kb.act(fcur[:, blk % 6, :ntok], p[:, :ntok], AF.Copy)
                else:
                    kb.copy("dve", fcur[:, blk % 6, :ntok], p[:, :ntok])
                ei += 1
                if blk % 6 == 5:
                    b0 = blk - 5
                    kb.dma("pool", FMv[:, b0:b0 + 6, tok0:tok0 + ntok], fcur[:, :, :ntok])


def na_pieces(c):
    out = []
    r0 = 8 * c
    for j in range(32):
        runs = []
        for r in range(r0, r0 + 8):
            rs = min(max(r - 4, 0), 56)
            if 2 * j + 1 < rs or 2 * j > rs + 7:
                continue
            edge = (r < 4) or (r > 60)
            kind = "G" if edge else "S"
            var = (7 - 2 * j + r) if edge else (r - 2 * j + 3)
            if runs and runs[-1][0] == kind and runs[-1][2] == r and runs[-1][1] + (r - runs[-1][3]) == var:
                runs[-1][2] = r + 1
            else:
                runs.append([kind, var, r + 1, r])
        for kind, var, rend, rstart in runs:
            out.append((j, (rstart - r0) * 64, (rend - rstart) * 64, (kind, var)))
    return out


def stage_na(kb, D, l, last):
    with kb.stage():
        colmask = cload(kb, D, "colmask")
        m9 = cload(kb, D, "M9")
        idf = cload(kb, D, "ident")
        vall = kb.sb("vall", [128, NT, 512], BF16)
        v3 = D["v_d"].rearrange("(t p) c -> p t c", p=128)
        for t0 in range(0, NT, 6):
            t1 = min(NT, t0 + 6)
            kb.dma("sp", vall[:, t0:t1, :], v3[:, t0:t1, :])
        vaug = [kb.sb("vaug", [128, NT, 128], BF16) for _ in range(2)]
        qk = [kb.sb("qk", [64, 2, T_ALL], BF16) for _ in range(2)]
        Gt = [kb.sb("Gt", [128, 1024], F32) for _ in range(2)]
        GM = [kb.sb("GM", [128, 1024], BF16) for _ in range(2)]
        S9 = [kb.sb("S9", [128, 576], BF16) for _ in range(2)]
        rcs = [kb.sb("rcs", [128, 512], F32) for _ in range(2)]
        for sid in range(2):
            kb.memset("pool", vaug[sid][:, :, 64:128], 1.0)
            kb.memset("dve", rcs[sid][:], 0.0)

        def mk2(name, shape, dt):
            return [[kb.sb(name, shape, dt) for _ in range(2)] for _ in range(2)]

        sc = mk2("sc", [128, 512], F32)
        pt = mk2("pt", [128, 512], BF16)
        osb = mk2("osb", [64, 512], F32)
        ob = mk2("ob", [64, 512], BF16)
        pS = [[kb.ps("pS", [128, 512]) for _ in range(2)] for _ in range(2)]
        pO = [[kb.ps("pO", [128, 512]) for _ in range(2)] for _ in range(2)]

        def headgen(h, sid):
            q_ = qk[sid]
            va = vaug[sid]
            kb.dma("sp", q_[:, 0, :], D["FM_d"][h * 64:(h + 1) * 64, :])
            kb.dma("sp", q_[:, 1, :], D["FM_d"][512 + h * 64:512 + (h + 1) * 64, :])
            kb.dma("sp", Gt[sid][:], D["Gp"][l, h])
            kb.copy("pool", va[:, :, 0:64], vall[:, :, h * 64:(h + 1) * 64])
            kb.tt("dve", GM[sid][:].rearrange("p (e q) -> p e q", e=16), Gt[sid][:].rearrange("p (e q) -> p e q", e=16),
                  colmask[:].unsqueeze(1).to_broadcast([128, 16, 64]), ALU.add)
            kb.tt("dve", S9[sid][:], GM[sid][:, 4 * 64:13 * 64], m9[:], ALU.add)
            yield
            pi = 0
            for c in range(9):
                if c == 8 and last:
                    continue
                ncols = 512 if c < 8 else 256
                q0 = c * 512
                pieces = [(32, 0, ncols, None), (33, 0, ncols, None)]
                if c < 8:
                    pieces += na_pieces(c)
                po = pO[sid][c % 2]
                for k, (j, lo, n, bias) in enumerate(pieces):
                    p = pS[sid][pi % 2]
                    e = pt[sid][pi % 2]
                    s_ = sc[sid][pi % 2]
                    pi += 1
                    kb.mm(p[:, :n], q_[:, 1, j * 128:(j + 1) * 128], q_[:, 0, q0 + lo:q0 + lo + n], start=True, stop=True)
                    yield
                    if bias is not None:
                        src = S9[sid] if bias[0] == "S" else GM[sid]
                        kb.tt("dve", s_[:, :n], p[:, :n], src[:, bias[1] * 64:bias[1] * 64 + n], ALU.add)
                        yield
                        kb.act(e[:, :n], s_[:, :n], AF.Exp)
                    else:
                        kb.act(e[:, :n], p[:, :n], AF.Exp)
                    yield
                    kb.mm(po[:, lo:lo + n], va[:, j, :], e[:, :n], start=(k == 0), stop=(k == len(pieces) - 1))
                    yield
                ci = c % 2
                r_ = rcs[sid]
                os_ = osb[sid][ci]
                o_ = ob[sid][ci]
                pb = pS[sid][pi % 2]
                pi += 1
                kb.act(r_[64:128, :ncols], po[64:128, :ncols], AF.Ln)
                kb.act(r_[64:128, :ncols], r_[64:128, :ncols], AF.Exp, scale=-1.0)
                kb.copy("act", os_[:, :ncols], po[0:64, :ncols])
                yield
                kb.mm(pb[0:64, :ncols], idf[:, 64:128], r_[:, :ncols], start=True, stop=True)
                yield
                kb.tt("dve", o_[:, :ncols], os_[:, :ncols], pb[0:64, :ncols], ALU.mult)
                kb.dma("pool", D["aT_d"][h * 64:(h + 1) * 64, q0:q0 + ncols], o_[:, :ncols])
                yield

        for h0 in range(0, 8, 2):
            run_gens([headgen(h0, 0), headgen(h0 + 1, 1)])


def stage_g1(kb, D, l):
    with kb.stage():
        idf = cload(kb, D, "ident")
        idb = kb.sb("idb", [128, 128], BF16)
        kb.copy("dve", idb[:], idf[:])
        onesb = cload(kb, D, "ones", BF16)
        rpb_ = cload(kb, D, "Rperm", BF16)
        rope = kb.sb("rope", [128, 2, T_LAT], F32)
        kb.dma("sp", rope[:, 0, :], D["rope"][:, 0, :])
        kb.dma("sp", rope[:, 1, :], D["rope"][:, 1, :])
        cw = kb.sb("cw", [128, 5, 12], F32)
        for j in range(5):
            kb.dma("sp", cw[:, j, :], D["conv_w"][l, j].rearrange("(f p) -> p f", p=128), allow_slow_non_contiguous=True)
        XW = 4360
        xin = [kb.sb("gxin", [128, XW], BF16) for _ in range(2)]
        xsh = [kb.sb("gxsh", [128, XW], BF16) for _ in range(2)]
        for x_ in xin + xsh:
            kb.memset("dve", x_[:], 0.0)
        sall = [kb.sb("sall", [128, T_ALL], F32) for _ in range(2)]
        dg = [kb.sb("dg", [128, 5, 128], BF16) for _ in range(2)]

        def mk2(name, shape, dt):
            return [[kb.sb(name, shape, dt) for _ in range(2)] for _ in range(2)]

        sq = mk2("sq", [128, 512], BF16)
        lnb = mk2("lnb", [128, 512], F32)
        qn = mk2("qn", [128, 512], BF16)
        t1 = mk2("t1", [128, 512], F32)
        t2 = mk2("t2", [128, 512], F32)
        obq = mk2("obq", [128, 512], BF16)
        tk_ = mk2("tkst", [128, 4, 128], BF16)
        pc = [kb.ps("pc", [128, 512]) for _ in range(2)]
        pn = [kb.ps("pn", [128, 512]) for _ in range(2)]
        pr = [kb.ps("pr", [128, 512]) for _ in range(2)]
        pt = [kb.ps("ptr", [128, 8, 128], BF16) for _ in range(2)]
        blocks = [(b * 512, 512, 2 + b * 512) for b in range(8)] + [(T_LAT, 256, 4102)]

        def fcgen(fc, sid):
            x_, xs_, d_, sa = xin[sid], xsh[sid], dg[sid], sall[sid]
            src = D["FM_d"][1024 + fc * 128:1024 + (fc + 1) * 128, :]
            kb.dma("sp", x_[:, 2:2 + T_LAT], src[:, 0:T_LAT])
            kb.dma("sp", x_[:, 4102:4102 + T_CTX], src[:, T_LAT:T_ALL])
            kb.dma("sp", xs_[:, 1:1 + T_LAT], src[:, 0:T_LAT])
            kb.dma("sp", xs_[:, 4101:4101 + T_CTX], src[:, T_LAT:T_ALL])
            for j in range(5):
                kb.ts("dve", d_[:, j, :], idf[:], cw[:, j, fc:fc + 1], ALU.mult)
            yield
            for (tok0, n, xc) in blocks:
                p = pc[sid]
                for j in range(5):
                    if j % 2 == 0:
                        rhs_ = x_[:, xc + j - 2:xc + j - 2 + n]
                    else:
                        rhs_ = xs_[:, xc + j - 3:xc + j - 3 + n]
                    kb.mm(p[:, :n], d_[:, j, :], rhs_, start=(j == 0), stop=(j == 4))
                yield
                kb.act(sa[:, tok0:tok0 + n], p[:, :n], AF.Silu)
                yield
            isq = fc < 4
            isv = fc >= 8
            hh = fc % 4
            for bi, (tok0, n, xc) in enumerate(blocks):
                i = bi % 2
                o_ = obq[sid][i]
                if isv:
                    kb.copy("dve", o_[:, :n], sa[:, tok0:tok0 + n])
                    yield
                else:
                    kb.tt("pool", sq[sid][i][:, :n], sa[:, tok0:tok0 + n], sa[:, tok0:tok0 + n], ALU.mult)
                    yield
                    kb.mm(pn[sid][:, :n], onesb[:], sq[sid][i][:, :n])
                    yield
                    kb.act(lnb[sid][i][:, :n], pn[sid][:, :n], AF.Ln, bias=NORM_EPS)
                    kb.act(lnb[sid][i][:, :n], lnb[sid][i][:, :n], AF.Exp, scale=-0.5, bias=(-0.5 * math.log(128.0) if isq else 0.0))
                    yield
                    if tok0 < T_LAT:
                        kb.tt("dve", qn[sid][i][:, :n], sa[:, tok0:tok0 + n], lnb[sid][i][:, :n], ALU.mult)
                        yield
                        kb.mm(pr[sid][:, :n], rpb_[:], qn[sid][i][:, :n])
                        kb.tt("pool", t1[sid][i][:, :n], qn[sid][i][:, :n], rope[:, 0, tok0:tok0 + n], ALU.mult)
                        yield
                        kb.tt("dve", t2[sid][i][:, :n], pr[sid][:, :n], rope[:, 1, tok0:tok0 + n], ALU.mult)
                        yield
                        kb.tt("dve", o_[:, :n], t1[sid][i][:, :n], t2[sid][i][:, :n], ALU.add)
                        yield
                    else:
                        kb.tt("dve", o_[:, :n], sa[:, tok0:tok0 + n], lnb[sid][i][:, :n], ALU.mult)
                        yield
                    kb.dma("pool", D["GQK_d"][fc * 128:(fc + 1) * 128, tok0:tok0 + n], o_[:, :n])
                if fc >= 4:
                    nt_ = n // 128
                    for a in range(nt_):
                        kb.tr(pt[sid][:, a, :], o_[:, a * 128:(a + 1) * 128], idb[:])
                    yield
                    kb.copy("act", tk_[sid][i][:, :nt_, :], pt[sid][:, :nt_, :])
                    dst = D["Vtok_d"] if isv else D["Ktok_d"]
                    kb.dma("pool", dst[tok0:tok0 + n, hh * 128:(hh + 1) * 128].rearrange("(a p) c -> p a c", p=128), tk_[sid][i][:, :nt_, :])
                    yield

        for fc0 in range(0, 12, 2):
            run_gens([fcgen(fc0, 0), fcgen(fc0 + 1, 1)])


def stage_g2(kb, D, l):
    with kb.stage():
        idf = cload(kb, D, "ident")
        idb = kb.sb("idb", [128, 128], BF16)
        kb.copy("dve", idb[:], idf[:])
        onesf = cload(kb, D, "ones")
        Ua = [cload(kb, D, "U1"), cload(kb, D, "U1T")]
        Sa = [cload(kb, D, "SL"), cload(kb, D, "SU")]
        NEGs = [cload(kb, D, "NEGf", BF16), cload(kb, D, "NEGb", BF16)]
        NEGT = [cload(kb, D, "NEGTf", BF16), cload(kb, D, "NEGTb", BF16)]
        ba = kb.sb("ba", [128, NT, 16], F32)
        kb.dma("sp", ba[:], D["ba_d"].rearrange("(t p) c -> p t c", p=128))
        al = bvec(kb, "alog", D["a_log"][l])
        dtb = bvec(kb, "dtb", D["dt_bias"][l])
        nA = kb.sb("nA", [128, 8], F32)
        kb.act(nA[:], al[:], AF.Exp)
        kb.ts("dve", nA[:], nA[:], -1.0, ALU.mult)
        gx = kb.sb("gx", [128, NT, 8], F32)
        g_all = kb.sb("g_all", [128, NT, 8], F32)
        beta = kb.sb("beta", [128, NT, 8], F32)
        negb = kb.sb("negb", [128, NT, 8], F32)
        kb.tt("dve", gx[:], ba[:, :, 8:16], dtb[:].unsqueeze(1).to_broadcast([128, NT, 8]), ALU.add)
        kb.act(gx[:], gx[:], AF.Exp)
        kb.act(gx[:], gx[:], AF.Ln, bias=1.0)
        kb.tt("dve", g_all[:], gx[:], nA[:].unsqueeze(1).to_broadcast([128, NT, 8]), ALU.mult)
        kb.act(beta[:], ba[:, :, 0:8], AF.Sigmoid)
        kb.ts("dve", negb[:], beta[:], -1.0, ALU.mult)

        def mk(name, shape, dt, n=2):
            return [[kb.sb(name, shape, dt) for _ in range(n)] for _ in range(2)]

        QK = mk("QK", [128, 8, 128], BF16)
        KV = mk("KV", [128, 2, 512], BF16)
        gS = mk("gS", [128, 4, 128], F32, 1)
        gU = mk("gU", [128, 4, 128], F32, 1)
        Ds = mk("Ds", [128, 4, 128], F32, 1)
        DT = mk("DT", [128, 4, 128], F32, 1)
        Er = mk("Er", [128, 4, 128], F32, 1)
        E3 = mk("E3", [128, 3, 4], F32)
        be = mk("be", [128, 4], F32, 1)
        negA = mk("negA", [128, 4, 128], F32, 1)
        MP = mk("MP", [128, 4, 2, 128], BF16)
        Rm = mk("Rm", [128, 4, 128], BF16)
        Tm = mk("Tm", [128, 4, 128], BF16)
        N1 = mk("N1", [128, 4, 128], BF16, 1)
        N1T = mk("N1T", [128, 4, 128], BF16, 1)
        N2 = mk("N2", [128, 4, 128], BF16, 1)
        Yb = mk("Yb", [128, 4, 128], BF16, 1)
        Xb = mk("Xb", [128, 4, 128], BF16, 1)
        X2b = mk("X2b", [128, 4, 128], BF16, 1)
        Rb = mk("Rb", [128, 4, 128], BF16, 1)
        Dg = cload(kb, D, "Dg32")
        O1 = [cload(kb, D, "Off1"), cload(kb, D, "Off1T")]
        O1T = [O1[1], O1[0]]
        O2 = [cload(kb, D, "Off2"), cload(kb, D, "Off2T")]
        attnT = mk("attnT", [128, 4, 128], BF16)
        QdT = mk("QdT", [128, 4, 128], BF16)
        Kd = mk("Kd", [128, 4, 128], BF16)
        Kbe = mk("Kbe", [128, 4, 128], BF16, 1)
        bV = mk("bV", [128, 4, 128], BF16, 1)
        U = mk("U", [128, 4, 128], F32)
        WT = mk("WT", [128, 4, 128], BF16)
        vnew = mk("vnew", [128, 4, 128], BF16, 1)
        osb = mk("osb", [128, 4, 128], F32)
        S = [kb.sb("S", [128, 4, 128], F32) for _ in range(2)]
        Sb = [kb.sb("Sb", [128, 4, 128], BF16) for _ in range(2)]
        for d in range(2):
            kb.memset("dve", S[d][:], 0.0)
            kb.memset("pool", Sb[d][:], 0.0)
        pY = [kb.ps("pY", [128, 4, 2, 128]) for _ in range(2)]
        pZ = [kb.ps("pZ", [128, 4, 128]) for _ in range(2)]
        pW = kb.ps("pW", [128, 4, 128])
        pOo = kb.ps("pOo", [128, 4, 128])
        GQ = D["GQK_d"].rearrange("(a p) t -> p a t", p=128)
        OT = [D["oT_d"][d].rearrange("(h v) t -> v h t", v=128) for d in range(2)]

        order = [[32, 33] + list(range(32)), [33, 32] + list(range(31, -1, -1))]
        B4 = [128, 4, 128]
        f2 = lambda a: a[:].rearrange("p h t -> p (h t)")

        def pre(s, d):
            t = order[d][s]
            b = s % 2
            tok = slice(t * 128, (t + 1) * 128)
            qk = QK[d][b]
            kv = KV[d][b]
            PY = pY[d]
            PZ = pZ[d]
            PYf = PY[:].rearrange("p h w t -> p (h w t)")
            A0f = PYf[:, 0:512]
            A1f = PYf[:, 512:1024]
            A0 = A0f.rearrange("p (h t) -> p h t", h=4)
            A1 = A1f.rearrange("p (h t) -> p h t", h=4)
            kb.dma("sp", qk[:], GQ[:, :, tok])
            kb.dma("sp", kv[:, 0, :], D["Ktok_d"][tok, :])
            kb.dma("sp", kv[:, 1, :], D["Vtok_d"][tok, :])
            gcol = g_all[:, t, d * 4:(d + 1) * 4]
            bcol = beta[:, t, d * 4:(d + 1) * 4]
            nbcol = negb[:, t, d * 4:(d + 1) * 4]
            kb.tt("dve", gS[d][0][:], gcol.unsqueeze(2).to_broadcast(B4), Sa[d][:].unsqueeze(1).to_broadcast(B4), ALU.mult)
            kb.tt("pool", gU[d][0][:], gcol.unsqueeze(2).to_broadcast(B4), Ua[d][:].unsqueeze(1).to_broadcast(B4), ALU.mult)
            yield
            kb.mm(A0f, Ua[d][:], f2(gS[d][0]), start=True, stop=False)
            kb.mm(A0f, idb[:], NEGs[d][:], start=False, stop=True)
            kb.mm(A1f, Sa[d][:], f2(gU[d][0]), start=True, stop=False)
            kb.mm(A1f, idb[:], NEGT[d][:], start=False, stop=True)
            kb.mm(f2(PZ), onesf[:], f2(gU[d][0]), start=True, stop=True)
            yield
            kb.act(f2(Ds[d][0]), A0f, AF.Exp)
            kb.act(f2(DT[d][0]), A1f, AF.Exp)
            kb.act(Er[d][0][:], PZ[:], AF.Exp)
            yield
            kb.mm(PZ[:, 0, 0:4], Ua[d][:], gcol, start=True, stop=True)
            kb.mm(PZ[:, 0, 4:8], Sa[d][:], gcol, start=True, stop=True)
            kb.mm(PZ[:, 0, 8:12], onesf[:], gcol, start=True, stop=True)
            e3 = E3[d][b]
            kb.act(e3[:].rearrange("p a h -> p (a h)"), PZ[:, 0, 0:12], AF.Exp)
            kb.tt("dve", be[d][0][:], bcol, e3[:, 0, :], ALU.mult)
            yield
            for h in range(4):
                kb.mm(A0[:, h, :], qk[:, 4 + h, :], qk[:, 4 + h, :], start=True, stop=True)
            for h in range(4):
                kb.mm(A1[:, h, :], qk[:, 4 + h, :], qk[:, h, :], start=True, stop=True)
            yield
            na_ = negA[d][0]
            for h in range(4):
                kb.stt("dve", na_[:, h, :], A0[:, h, :], nbcol[:, h:h + 1], Ds[d][0][:, h, :], ALU.mult, ALU.mult)
            kb.tt("dve", attnT[d][b][:], A1, DT[d][0][:], ALU.mult)
            yield
            bc = lambda m: m[:].unsqueeze(1).to_broadcast(B4)
            mp = MP[d][0]
            for h in range(4):
                kb.tr(PZ[:, h, :], na_[:, h, :], idf[:])
            kb.tt("pool", mp[:, :, 1, :], na_[:], bc(Dg), ALU.mult)
            kb.tt("pool", N1[d][0][:], na_[:], bc(O1[d]), ALU.mult)
            kb.tt("pool", N2[d][0][:], na_[:], bc(O2[d]), ALU.mult)
            yield
            kb.tt("dve", mp[:, :, 0, :], PZ[:], bc(Dg), ALU.mult)
            kb.tt("dve", N1T[d][0][:], PZ[:], bc(O1T[d]), ALU.mult)
            yield
            kb.tt("pool", Tm[d][0][:], mp[:, :, 1, :], bc(idf), ALU.add)
            kb.tt("pool", Rm[d][0][:], mp[:, :, 0, :], bc(idf), ALU.add)
            yield
            cur = 0
            for it in range(1, 5):
                mpn = MP[d][1 - cur]
                mpc = MP[d][cur]
                for h in range(4):
                    kb.mm(PY[:, h, 0, :], mpc[:, h, 1, :], mpc[:, h, 0, :], start=True, stop=True)
                    kb.mm(PY[:, h, 1, :], mpc[:, h, 0, :], mpc[:, h, 1, :], start=True, stop=True)
                yield
                kb.copy("act", mpn[:, 0:2], PY[:, 0:2])
                kb.copy("act", mpn[:, 2:4], PY[:, 2:4])
                yield
                rn, rc_ = Rm[d][1 - cur], Rm[d][cur]
                tn, tc_ = Tm[d][1 - cur], Tm[d][cur]
                for h in range(4):
                    kb.mm(PZ[:, h, :], mpn[:, h, 1, :], rc_[:, h, :], start=True, stop=True)
                for h in range(4):
                    kb.mm(PY[:, h, 0, :], mpn[:, h, 0, :], tc_[:, h, :], start=True, stop=True)
                yield
                kb.tt("dve", rn[:], PZ[:], rc_[:], ALU.add)
                kb.tt("dve", tn[:], PY[:, :, 0, :], tc_[:], ALU.add)
                yield
                cur = 1 - cur
            R0_, T0_ = Rm[d][cur], Tm[d][cur]
            R1_, T1_ = Rm[d][1 - cur], Tm[d][1 - cur]
            for h in range(4):
                kb.mm(PZ[:, h, :], N1T[d][0][:, h, :], T0_[:, h, :], start=True, stop=True)
            for h in range(4):
                kb.mm(PY[:, h, 0, :], N1[d][0][:, h, :], R0_[:, h, :], start=True, stop=True)
            yield
            kb.copy("act", Yb[d][0][:], PZ[:])
            kb.copy("act", Xb[d][0][:], PY[:, :, 0, :])
            yield
            for h in range(4):
                kb.mm(PZ[:, h, :], R0_[:, h, :], Yb[d][0][:, h, :], start=True, stop=True)
            for h in range(4):
                kb.mm(PY[:, h, 0, :], T0_[:, h, :], Xb[d][0][:, h, :], start=True, stop=True)
            yield
            kb.tt("dve", T1_[:], PZ[:], T0_[:], ALU.add)
            kb.tt("dve", R1_[:], PY[:, :, 0, :], R0_[:], ALU.add)
            yield
            for h in range(4):
                kb.mm(PZ[:, h, :], N2[d][0][:, h, :], R1_[:, h, :], start=True, stop=True)
            yield
            kb.copy("act", X2b[d][0][:], PZ[:])
            yield
            for h in range(4):
                kb.mm(PZ[:, h, :], T1_[:, h, :], X2b[d][0][:, h, :], start=True, stop=True)
            yield
            r = Rb[d][0]
            kb.tt("dve", r[:], PZ[:], R1_[:], ALU.add)
            kb.tt("pool", bV[d][0][:], kv[:, 1, :].rearrange("p (h v) -> p h v", h=4), bcol.unsqueeze(2).to_broadcast(B4), ALU.mult)
            kb.tt("pool", Kbe[d][0][:], kv[:, 0, :].rearrange("p (h v) -> p h v", h=4), be[d][0][:].unsqueeze(2).to_broadcast(B4), ALU.mult)
            kb.tt("pool", Kd[d][b][:], kv[:, 0, :].rearrange("p (h v) -> p h v", h=4), e3[:, 1, :].unsqueeze(2).to_broadcast(B4), ALU.mult)
            kb.tt("dve", QdT[d][b][:], qk[:, 0:4, :], Er[d][0][:], ALU.mult)
            yield
            for h in range(4):
                kb.mm(PZ[:, h, :], r[:, h, :], bV[d][0][:, h, :], start=True, stop=True)
            for h in range(4):
                kb.mm(A0[:, h, :], Kbe[d][0][:, h, :], r[:, h, :], start=True, stop=True)
            yield
            kb.copy("act", U[d][b][:], PZ[:])
            kb.copy("act", WT[d][b][:], A0)
            yield

        def scan(s, d):
            t = order[d][s]
            b = s % 2
            tok = slice(t * 128, (t + 1) * 128)
            for h in range(4):
                kb.mm(pW[:, h, :], WT[d][b][:, h, :], Sb[d][:, h, :], start=True, stop=True)
            yield
            kb.tt("dve", vnew[d][0][:], U[d][b][:], pW[:], ALU.subtract)
            yield
            for h in range(4):
                kb.mm(pOo[:, h, :], Sb[d][:, h, :], QdT[d][b][:, h, :], start=True, stop=False)
                kb.mm(pOo[:, h, :], vnew[d][0][:, h, :], attnT[d][b][:, h, :], start=False, stop=True)
            for h in range(4):
                kb.mm(pW[:, h, :], Kd[d][b][:, h, :], vnew[d][0][:, h, :], start=True, stop=True)
            yield
            kb.copy("act", osb[d][b][:], pOo[:])
            kb.dma("pool", OT[d][:, :, tok], osb[d][b][:])
            for h in range(4):
                kb.stt("dve", S[d][:, h, :], S[d][:, h, :], E3[d][b][:, 2, h:h + 1], pW[:, h, :], ALU.mult, ALU.add)
            yield
            kb.copy("act", Sb[d][:], S[d][:])
            yield

        def run(gens):
            gens = list(gens)
            while gens:
                for g_ in list(gens):
                    try:
                        next(g_)
                    except StopIteration:
                        gens.remove(g_)

        run([pre(0, 0), pre(0, 1)])
        for s in range(NT):
            def scans(s_):
                yield from scan(s_, 0)
                yield from scan(s_, 1)
            gl = [scans(s)]
            if s + 1 < NT:
                gl += [pre(s + 1, 0), pre(s + 1, 1)]
            run(gl)


def stage_g3(kb, D, l, last):
    with kb.stage():
        meanb = kb.sb("meanb", [128, 128], BF16)
        kb.memset("dve", meanb[:], 1.0 / 128.0)
        gw = kb.sb("gw", [128, 1], F32)
        kb.dma("sp", gw[:], D["gdn_norm_w"][l].rearrange("(p o) -> p o", o=1))
        ntok_all = T_LAT if last else T_ALL
        zt = [kb.sb("zt", [128, T_ALL], BF16) for _ in range(2)]
        sz = [kb.sb("sz", [128, T_ALL], F32) for _ in range(2)]

        def mk2(name, shape, dt):
            return [[kb.sb(name, shape, dt) for _ in range(2)] for _ in range(2)]

        of = mk2("of", [128, 512], F32)
        obk = mk2("obk", [128, 512], F32)
        sq = mk2("sq", [128, 512], BF16)
        rs = mk2("rs", [128, 512], F32)
        yo = mk2("yo", [128, 512], BF16)
        pn = [kb.ps("pn", [128, 512]) for _ in range(2)]
        blocks = [(b * 512, 512) for b in range(8)] + ([] if last else [(T_LAT, 256)])

        def headgen(h, sid):
            kb.dma("sp", zt[sid][:, :ntok_all], D["FM_d"][2560 + h * 128:2560 + (h + 1) * 128, 0:ntok_all])
            yield
            for c0 in range(0, ntok_all, 1088):
                c1 = min(ntok_all, c0 + 1088)
                kb.act(sz[sid][:, c0:c1], zt[sid][:, c0:c1], AF.Silu)
                yield
            for bi, (tok0, n) in enumerate(blocks):
                j = bi % 2
                o_, ob_, sq_, rs_, yo_ = of[sid][j], obk[sid][j], sq[sid][j], rs[sid][j], yo[sid][j]
                kb.dma("sp", o_[:, :n], D["oT_d"][0, h * 128:(h + 1) * 128, tok0:tok0 + n])
                kb.dma("sp", ob_[:, :n], D["oT_d"][1, h * 128:(h + 1) * 128, tok0:tok0 + n])
                yield
                kb.tt("pool", o_[:, :n], o_[:, :n], ob_[:, :n], ALU.add)
                yield
                kb.tt("pool", sq_[:, :n], o_[:, :n], o_[:, :n], ALU.mult)
                yield
                kb.mm(pn[sid][:, :n], meanb[:], sq_[:, :n])
                yield
                kb.act(rs_[:, :n], pn[sid][:, :n], AF.Ln, bias=NORM_EPS)
                kb.act(rs_[:, :n], rs_[:, :n], AF.Exp, scale=-0.5)
                yield
                kb.tt("dve", o_[:, :n], o_[:, :n], rs_[:, :n], ALU.mult)
                yield
                kb.stt("dve", yo_[:, :n], o_[:, :n], gw[:, 0:1], sz[sid][:, tok0:tok0 + n], ALU.mult, ALU.mult)
                kb.dma("pool", D["aT_d"][512 + h * 128:512 + (h + 1) * 128, tok0:tok0 + n], yo_[:, :n])
                yield

        for h0 in range(0, 4, 2):
            run_gens([headgen(h0, 0), headgen(h0 + 1, 1)])


def layer_norm_gen(kb, r, out, stt_, mv, rs, gam, bet):
    for hh in range(2):
        kb.op("dve", lambda g: g.bn_stats(out=stt_[:, hh, :], in_=r[:, hh * 512:(hh + 1) * 512]), [r[:]], [stt_[:]])
    kb.op("dve", lambda g: g.bn_aggr(out=mv[:], in_=stt_[:]), [stt_[:]], [mv[:]])
    yield
    kb.act(rs[:, 0:1], mv[:, 1:2], AF.Ln, bias=LN_EPS)
    kb.act(rs[:, 0:1], rs[:, 0:1], AF.Exp, scale=-0.5)
    yield
    kb.stt("dve", rs[:, 1:2], mv[:, 0:1], -1.0, rs[:, 0:1], ALU.mult, ALU.mult)
    yield
    kb.act(out[:], r[:], AF.Identity, scale=rs[:, 0:1], bias=rs[:, 1:2])
    yield
    kb.tt("pool", out[:], out[:], gam[:], ALU.mult)
    kb.tt("pool", out[:], out[:], bet[:], ALU.add)
    yield


def layer_norm_tile(kb, r, out, stt_, mv, rs, gam, bet):
    for _ in layer_norm_gen(kb, r, out, stt_, mv, rs, gam, bet):
        pass


def stage4a(kb, D, l, xsrc, last):
    with kb.stage():
        wo = kb.sb("wo", [128, 8, DM], BF16)
        for kc in range(8):
            kb.dma("sp", wo[:, kc, :], D["woutb"][l, kc * 128:(kc + 1) * 128, :])
        idb = cload(kb, D, "ident", BF16)
        lg = bvec(kb, "ln1g", D["ln1_g"][l])
        lb = bvec(kb, "ln1b", D["ln1_b"][l])
        mods = {}
        for r_, nm in ((0, "l"), (1, "c")):
            if nm == "c" and last:
                continue
            mods[nm] = (bvec(kb, "g1", D["mod_d"][l, r_, 2048:3072]), bvec(kb, "sc2", D["mod_d"][l, r_, 4096:5120]),
                        bvec(kb, "sh2", D["mod_d"][l, r_, 3072:4096]))
        aT = [kb.sb("aT", [128, 8, 512], BF16) for _ in range(2)]
        xt = [kb.sb("xt", [128, DM], F32) for _ in range(2)]
        rr = [kb.sb("rr", [128, DM], F32) for _ in range(2)]
        x1 = [kb.sb("x1", [128, DM], F32) for _ in range(2)]
        hb = [kb.sb("hb", [128, DM], BF16) for _ in range(2)]
        hT = [kb.sb("hT", [128, 8, 512], BF16) for _ in range(2)]
        stt_ = [kb.sb("stt", [128, 2, 6], F32) for _ in range(2)]
        mv = [kb.sb("mv", [128, 2], F32) for _ in range(2)]
        rs = [kb.sb("rs", [128, 2], F32) for _ in range(2)]
        py = [kb.ps("py", [128, 1024]) for _ in range(2)]
        pT = [kb.ps("pT", [128, 1024], BF16) for _ in range(2)]
        aTv = D["aT_d"].rearrange("(k p) t -> p k t", p=128)
        hTv = D["h2T_d"].rearrange("(k p) t -> p k t", p=128)
        for g in range(9):
            if g == 8 and last:
                continue
            ntile = 4 if g < 8 else 2
            ntok = ntile * 128
            tok0 = g * 512
            g1, sc2, sh2 = mods["l"] if g < 8 else mods["c"]
            a_ = aT[g % 2]
            hTg = hT[g % 2]
            kb.dma("sp", a_[:, :, :ntok], aTv[:, :, tok0:tok0 + ntok])

            def tilegen(tt_, j):
                rows = slice(tok0 + tt_ * 128, tok0 + (tt_ + 1) * 128)
                kb.dma("sp", xt[j][:], xsrc[rows, :])
                for hh in range(2):
                    for kc in range(8):
                        kb.mm(py[j][:, hh * 512:(hh + 1) * 512], a_[:, kc, tt_ * 128:(tt_ + 1) * 128], wo[:, kc, hh * 512:(hh + 1) * 512],
                              start=(kc == 0), stop=(kc == 7))
                yield
                kb.tt("dve", rr[j][:], py[j][:], g1[:], ALU.mult)
                kb.stt("dve", rr[j][:], xt[j][:], ALPHA, rr[j][:], ALU.mult, ALU.add)
                yield
                yield from layer_norm_gen(kb, rr[j], x1[j], stt_[j], mv[j], rs[j], lg, lb)
                kb.dma("pool", D["X1_d"][rows, :], x1[j][:])
                kb.tt("dve", rr[j][:], x1[j][:], sc2[:], ALU.mult)
                kb.tt("dve", hb[j][:], rr[j][:], sh2[:], ALU.add)
                yield
                for kc in range(8):
                    kb.tr(pT[j][:, kc * 128:(kc + 1) * 128], hb[j][:, kc * 128:(kc + 1) * 128], idb[:])
                yield
                kb.act(hTg[:, :, tt_ * 128:(tt_ + 1) * 128], pT[j][:].rearrange("p (k t) -> p k t", k=8), AF.Copy)
                yield

            for t0_ in range(0, ntile, 2):
                run_gens([tilegen(t0_, 0), tilegen(t0_ + 1, 1)])
            kb.dma("pool", hTv[:, :, tok0:tok0 + ntok], hTg[:, :, :ntok])


def stage4b(kb, D, l, last):
    ngroups = 8 if last else 9
    m1v = D["m1T_d"].rearrange("(j p) t -> p j t", p=128)
    hTv = D["h2T_d"].rearrange("(k p) t -> p k t", p=128)
    with kb.stage():
        w1 = kb.sb("w1", [128, 8, 4 * DM], BF16)
        for kc in range(8):
            kb.dma("sp", w1[:, kc, :], D["w1b"][l, kc * 128:(kc + 1) * 128, :])
        hT = [kb.sb("hT", [128, 8, 512], BF16) for _ in range(2)]
        rl = [kb.sb("rl", [128, 512], F32) for _ in range(4)]
        mo = [kb.sb("mo", [128, 8, 512], BF16) for _ in range(3)]
        pm = [kb.ps("pm", [128, 512]) for _ in range(4)]
        mi = 0
        oi = 0
        for g in range(ngroups):
            ntok = 512 if g < 8 else 256
            tok0 = g * 512
            h_ = hT[g % 2]
            kb.dma("sp", h_[:, :, :ntok], hTv[:, :, tok0:tok0 + ntok])
            for j in range(32):
                p = pm[mi % 4]
                r_ = rl[mi % 4]
                if j % 8 == 0:
                    m_ = mo[oi % 3]
                    oi += 1
                for kc in range(8):
                    kb.mm(p[:, :ntok], w1[:, kc, j * 128:(j + 1) * 128], h_[:, kc, :ntok], start=(kc == 0), stop=(kc == 7))
                kb.act(r_[:, :ntok], p[:, :ntok], AF.Relu)
                kb.tt("dve" if mi % 2 == 0 else "pool", m_[:, j % 8, :ntok], r_[:, :ntok], r_[:, :ntok], ALU.mult)
                mi += 1
                if j % 8 == 7:
                    kb.dma("pool", m1v[:, j - 7:j + 1, tok0:tok0 + ntok], m_[:, :, :ntok])
    with kb.stage():
        w2 = kb.sb("w2", [128, 32, DM], BF16)
        w2v = D["w2b"][l].rearrange("(j p) c -> p j c", p=128)
        for j0 in range(0, 32, 8):
            kb.dma("sp", w2[:, j0:j0 + 8, :], w2v[:, j0:j0 + 8, :])
        lg = bvec(kb, "ln2g", D["ln2_g"][l])
        lb = bvec(kb, "ln2b", D["ln2_b"][l])
        g2 = {"l": bvec(kb, "g2", D["mod_d"][l, 0, 5120:6144])}
        if not last:
            g2["c"] = bvec(kb, "g2c", D["mod_d"][l, 1, 5120:6144])
        mt = [kb.sb("mt", [128, 32, 512], BF16) for _ in range(2)]
        x1 = [kb.sb("x1", [128, DM], F32) for _ in range(2)]
        rr = [kb.sb("rr", [128, DM], F32) for _ in range(2)]
        x2 = [kb.sb("x2", [128, DM], F32) for _ in range(2)]
        stt_ = [kb.sb("stt", [128, 2, 6], F32) for _ in range(2)]
        mv = [kb.sb("mv", [128, 2], F32) for _ in range(2)]
        rs = [kb.sb("rs", [128, 2], F32) for _ in range(2)]
        po = [kb.ps("po", [128, 512]) for _ in range(4)]
        dst = D["y"] if last else D["X_d"]
        ui = 0
        ti = 0
        for g in range(ngroups):
            ntile = 4 if g < 8 else 2
            ntok = ntile * 128
            tok0 = g * 512
            m_ = mt[g % 2]
            for j0 in range(0, 32, 8):
                kb.dma("sp", m_[:, j0:j0 + 8, :ntok], m1v[:, j0:j0 + 8, tok0:tok0 + ntok])
            gg = g2["l"] if g < 8 else g2["c"]
            for t in range(ntile):
                i = ti % 2
                ti += 1
                rows = slice(tok0 + t * 128, tok0 + (t + 1) * 128)
                kb.dma("sp", x1[i][:], D["X1_d"][rows, :])
                for hh in range(2):
                    p = po[ui % 4]
                    ui += 1
                    cs_ = slice(hh * 512, (hh + 1) * 512)
                    for j in range(32):
                        kb.mm(p[:], m_[:, j, t * 128:(t + 1) * 128], w2[:, j, cs_], start=(j == 0), stop=(j == 31))
                    kb.tt("dve", rr[i][:, cs_], p[:], gg[:, cs_], ALU.mult)
                kb.stt("dve", rr[i][:], x1[i][:], ALPHA, rr[i][:], ALU.mult, ALU.add)
                layer_norm_tile(kb, rr[i], x2[i], stt_[i], mv[i], rs[i], lg, lb)
                kb.dma("pool", dst[rows, :], x2[i][:])


_CACHE = {}


def host_inputs(b, inp, shared):
    m = dict(shared)
    m["xin"] = np.ascontiguousarray(np.concatenate([inp["x"][b], inp["ctx"][b]], axis=0), dtype=np.float32)
    m["cvec"] = np.ascontiguousarray(np.stack([inp["c"][b], inp["c_ctx"]], axis=0), dtype=np.float32)
    return m


def shared_inputs(inp):
    f = lambda a: np.ascontiguousarray(a, dtype=np.float32)
    sh = {"consts": make_consts(), "rope": make_rope(), "Gp": make_rpb_gather(np.asarray(inp["rpb"], np.float32))}
    for k in ("w_ada", "b_ada", "w_in", "conv_w", "gdn_norm_w", "w_out", "ln1_g", "ln1_b", "w_mlp1", "w_mlp2", "ln2_g", "ln2_b"):
        sh[k] = f(inp[k])
    sh["a_log"] = f(inp["a_log"]).reshape(DEPTH, 8)
    sh["dt_bias"] = f(inp["dt_bias"]).reshape(DEPTH, 8)
    return sh


def kernel(**inputs):
    inp = {k: np.asarray(v) for k, v in inputs.items()}
    if "nc" not in _CACHE:
        _CACHE["nc"] = build()
    nc = _CACHE["nc"]
    shared = shared_inputs(inp)
    in_maps = [host_inputs(b, inp, shared) for b in range(8)]
    res = run_bass_kernel_spmd(nc, in_maps, core_ids=list(range(8)))
    return np.stack([np.asarray(r["y"], dtype=np.float32) for r in res.results], axis=0)
```

```python
import math
from contextlib import ExitStack, contextmanager
import numpy as np
import concourse.bass as bass
import concourse.mybir as mybir
from concourse.bass_utils import run_bass_kernel_spmd

F32 = mybir.dt.float32
BF16 = mybir.dt.bfloat16
AF = mybir.ActivationFunctionType
ALU = mybir.AluOpType

N_DMA_SEMS = 24
SAME_ENGINE_SYNC = {"pe": False, "act": True, "dve": True, "pool": True, "sp": True}

DEPTH = 4
T_LAT = 4096
T_CTX = 256
T_ALL = T_LAT + T_CTX
NT = T_ALL // 128
DM = 1024
IN_COLS = 3600
BIG = 30000.0
ALPHA = (2 * DEPTH) ** 0.25
LN_EPS = 1e-5
NORM_EPS = 1e-6


class Tk:
    __slots__ = ("w", "r")

    def __init__(self):
        self.w = None
        self.r = {}


class KB:
    def __init__(self, nc, st):
        self.nc = nc
        self.E = {"pe": nc.tensor, "act": nc.scalar, "dve": nc.vector, "pool": nc.gpsimd, "sp": nc.sync}
        self.sem = {}
        self.cnt = {}
        for e in self.E:
            self.sem[e] = st.enter_context(nc.semaphore("s_" + e))
            self.cnt[e] = 0
        for q in ("sp", "pool", "act"):
            for i in range(N_DMA_SEMS):
                k = "d%s%d" % (q, i)
                self.sem[k] = st.enter_context(nc.semaphore("s_" + k))
                self.cnt[k] = 0
        self.known = {e: {} for e in self.E}
        self.dma_rr = {"sp": 0, "pool": 0, "act": 0}
        self.n_ins = 0
        self.tk = {}
        self.psum = set()
        self.cur = None
        self.uid = 0

    @contextmanager
    def stage(self):
        prev = self.cur
        with ExitStack() as s:
            self.cur = s
            yield s
            self.barrier()
        self.cur = prev

    def sb(self, name, shape, dt):
        self.uid += 1
        nm = "%s_%d" % (name, self.uid)
        t = self.cur.enter_context(self.nc.sbuf_tensor(nm, list(shape), dt))
        self.tk[nm] = Tk()
        return t

    def ps(self, name, shape, dt=F32):
        self.uid += 1
        nm = "%s_%d" % (name, self.uid)
        t = self.cur.enter_context(self.nc.psum_tensor(nm, list(shape), dt))
        self.tk[nm] = Tk()
        self.psum.add(nm)
        return t

    def _split(self, ins, outs):
        reads, writes = [], []
        for a in ins:
            n = a.name
            if n in self.tk:
                (writes if n in self.psum else reads).append(self.tk[n])
        for a in outs:
            n = a.name
            if n in self.tk:
                writes.append(self.tk[n])
        return reads, writes

    def _deps(self, reads, writes):
        deps = {}
        for t in reads:
            if t.w is not None and t.w[1] > deps.get(t.w[0], 0):
                deps[t.w[0]] = t.w[1]
        for t in writes:
            if t.w is not None and t.w[1] > deps.get(t.w[0], 0):
                deps[t.w[0]] = t.w[1]
            for s, c in t.r.items():
                if c > deps.get(s, 0):
                    deps[s] = c
        return deps

    def _wait(self, e, deps):
        eng = self.E[e]
        kn = self.known[e]
        for s, c in deps.items():
            if kn.get(s, 0) >= c:
                continue
            if s == e and not SAME_ENGINE_SYNC[e]:
                continue
            eng.wait_ge(self.sem[s], c)
            kn[s] = c

    def _mark(self, tag, reads, writes):
        s, c = tag
        for t in reads:
            if t.r.get(s, 0) < c:
                t.r[s] = c
        for t in writes:
            t.w = tag
            t.r = {}

    def op(self, e, fn, ins=(), outs=()):
        reads, writes = self._split(ins, outs)
        self._wait(e, self._deps(reads, writes))
        ins_ = fn(self.E[e])
        self.cnt[e] += 1
        ins_.then_inc(self.sem[e], 1)
        self._mark((e, self.cnt[e]), reads, writes)
        self.n_ins += 1

    def dma(self, q, out, in_, **kw):
        reads, writes = self._split([in_], [out])
        k = "d%s%d" % (q, self.dma_rr[q])
        self.dma_rr[q] = (self.dma_rr[q] + 1) % N_DMA_SEMS
        deps = self._deps(reads, writes)
        if self.cnt[k] > deps.get(k, 0):
            deps[k] = self.cnt[k]
        self._wait(q, deps)
        ins_ = self.E[q].dma_start(out=out, in_=in_, **kw)
        self.cnt[k] += 16
        ins_.then_inc(self.sem[k], 16)
        self._mark((k, self.cnt[k]), reads, writes)
        self.n_ins += 1

    def barrier(self):
        for e in self.E:
            self._wait(e, {s: c for s, c in self.cnt.items() if c > 0 and s != e})

    def mm(self, out, lhsT, rhs, start=True, stop=True):
        self.op("pe", lambda e: e.matmul(out, lhsT=lhsT, rhs=rhs, start=start, stop=stop), [lhsT, rhs], [out])

    def tr(self, out, in_, ident):
        self.op("pe", lambda e: e.transpose(out, in_, ident), [in_, ident], [out])

    def act(self, out, in_, func, scale=None, bias=None):
        kw = {}
        ins = [in_]
        if scale is not None:
            kw["scale"] = scale
            if not isinstance(scale, (int, float)):
                ins.append(scale)
        if bias is not None:
            kw["bias"] = bias
            if not isinstance(bias, (int, float)):
                ins.append(bias)
        self.op("act", lambda e: e.activation(out=out, in_=in_, func=func, **kw), ins, [out])

    def copy(self, e, out, in_):
        if e == "act":
            self.act(out, in_, AF.Copy)
        else:
            self.op(e, lambda g: g.tensor_copy(out=out, in_=in_), [in_], [out])

    def tt(self, e, out, a, b, op):
        self.op(e, lambda g: g.tensor_tensor(out=out, in0=a, in1=b, op=op), [a, b], [out])

    def ts(self, e, out, a, s1, op0, s2=None, op1=None):
        ins = [a] + [s for s in (s1, s2) if s is not None and not isinstance(s, (int, float))]
        if op1 is None:
            self.op(e, lambda g: g.tensor_scalar(out=out, in0=a, scalar1=s1, scalar2=None, op0=op0), ins, [out])
        else:
            self.op(e, lambda g: g.tensor_scalar(out=out, in0=a, scalar1=s1, scalar2=s2, op0=op0, op1=op1), ins, [out])

    def stt(self, e, out, a, scalar, b, op0, op1):
        ins = [a, b] + ([] if isinstance(scalar, (int, float)) else [scalar])
        self.op(e, lambda g: g.scalar_tensor_tensor(out=out, in0=a, scalar=scalar, in1=b, op0=op0, op1=op1), ins, [out])

    def memset(self, e, out, val):
        self.op(e, lambda g: g.memset(out, val), [], [out])


CONST_LAYOUT = {}


def make_consts():
    P = np.arange(128)
    parts = []
    off = 0

    def add(name, arr):
        nonlocal off
        arr = np.asarray(arr, np.float32)
        CONST_LAYOUT[name] = (off, arr.shape[1])
        parts.append(arr)
        off += arr.shape[1]

    add("ident", np.eye(128))
    add("U1", P[:, None] <= P[None, :])
    add("U1T", P[:, None] >= P[None, :])
    add("SL", P[:, None] > P[None, :])
    add("SU", P[:, None] < P[None, :])
    add("ones", np.ones((128, 128)))
    negf = np.where(P[:, None] > P[None, :], 0.0, -BIG)
    negb = np.where(P[:, None] < P[None, :], 0.0, -BIG)
    negtf = np.where(P[None, :] >= P[:, None], 0.0, -BIG)
    negtb = np.where(P[None, :] <= P[:, None], 0.0, -BIG)
    add("NEGf", np.tile(negf, (1, 4)))
    add("NEGb", np.tile(negb, (1, 4)))
    add("NEGTf", np.tile(negtf, (1, 4)))
    add("NEGTb", np.tile(negtb, (1, 4)))
    pi = np.where((P % 64) < 32, P + 32, P - 32)
    rp = np.zeros((128, 128))
    rp[pi, P] = 1.0
    add("Rperm", rp)
    kc = P % 64
    par = P // 64
    qc = np.arange(64)
    cs = np.clip(qc - 8, 0, 48)
    colmask = np.where((kc[:, None] >= cs[None, :]) & (kc[:, None] < cs[None, :] + 16), 0.0, -BIG)
    add("colmask", colmask)
    m9 = np.zeros((128, 9, 64))
    m9[par == 1, 0, :] = -BIG
    m9[par == 0, 8, :] = -BIG
    add("M9", m9.reshape(128, 576))
    dg32 = (P[:, None] // 32 == P[None, :] // 32)
    off1 = (P[:, None] // 64 == P[None, :] // 64) & (P[:, None] // 32 > P[None, :] // 32)
    off2 = (P[:, None] // 64 > P[None, :] // 64)
    add("Dg32", dg32)
    add("Off1", off1)
    add("Off1T", off1.T)
    add("Off2", off2)
    add("Off2T", off2.T)
    return np.concatenate(parts, axis=1).astype(np.float32)


def make_rope():
    t = np.arange(T_LAT)
    row = (t // 64).astype(np.float32)
    col = (t % 64).astype(np.float32)
    inv = (10000.0 ** (-np.arange(0, 64, 2, dtype=np.float32) / 64)).astype(np.float32)
    i = np.arange(128)
    f = i % 32
    pos = np.where((i < 64)[:, None], row[None, :], col[None, :]).astype(np.float32)
    ang = (pos * inv[f][:, None]).astype(np.float32)
    sgn = np.where((i % 64) < 32, -1.0, 1.0)[:, None]
    out = np.stack([np.cos(ang), sgn * np.sin(ang)], axis=1)
    return out.astype(np.float32)


def make_rpb_gather(rpb):
    P = np.arange(128)
    kc = (P % 64)[:, None, None]
    par = (P // 64)[:, None, None]
    ep = np.arange(16)[None, :, None]
    qc = np.arange(64)[None, None, :]
    dr = 7 - ep + par + 0 * qc
    dc = kc - qc + 0 * ep
    valid = (np.abs(dr) <= 7) & (np.abs(dc) <= 15)
    dri = np.clip(dr + 7, 0, 14)
    dci = np.clip(dc + 15, 0, 30)
    g = rpb[:, :, dri, dci]
    g = np.where(valid[None, None], g, np.float32(0.0))
    return np.ascontiguousarray(g.reshape(rpb.shape[0], rpb.shape[1], 128, 1024).astype(np.float32))


def build(n_layers=DEPTH, dbg=(), stages=None):
    make_consts()
    nc = bass.Bass("TRN2", target_bir_lowering=False)
    D = {}

    def din(name, shape, dt=F32):
        D[name] = nc.dram_tensor(name, list(shape), dt, kind="ExternalInput").ap()

    def dscr(name, shape, dt):
        kind = "ExternalOutput" if name in dbg else "Internal"
        D[name] = nc.dram_tensor(name, list(shape), dt, kind=kind).ap()

    ncst = sum(v[1] for v in CONST_LAYOUT.values())
    din("xin", [T_ALL, DM]); din("cvec", [2, DM]); din("consts", [128, ncst]); din("rope", [128, 2, T_LAT])
    din("Gp", [DEPTH, 8, 128, 1024])
    din("w_ada", [DEPTH, DM, 6 * DM]); din("b_ada", [DEPTH, 6 * DM]); din("w_in", [DEPTH, DM, IN_COLS])
    din("conv_w", [DEPTH, 5, 1536]); din("a_log", [DEPTH, 8]); din("dt_bias", [DEPTH, 8]); din("gdn_norm_w", [DEPTH, 128])
    din("w_out", [DEPTH, DM, DM]); din("ln1_g", [DEPTH, DM]); din("ln1_b", [DEPTH, DM])
    din("w_mlp1", [DEPTH, DM, 4 * DM]); din("w_mlp2", [DEPTH, 4 * DM, DM]); din("ln2_g", [DEPTH, DM]); din("ln2_b", [DEPTH, DM])
    D["y"] = nc.dram_tensor("y", [T_LAT, DM], F32, kind="ExternalOutput").ap()

    dscr("winb", [DEPTH, DM, IN_COLS], BF16); dscr("woutb", [DEPTH, DM, DM], BF16)
    dscr("w1b", [DEPTH, DM, 4 * DM], BF16); dscr("w2b", [DEPTH, 4 * DM, DM], BF16)
    dscr("mod_d", [DEPTH, 2, 6 * DM], F32)
    dscr("FM_d", [3072, T_ALL], BF16)
    dscr("v_d", [T_ALL, 512], BF16); dscr("ba_d", [T_ALL, 16], F32)
    dscr("GQK_d", [1024, T_ALL], BF16)
    dscr("Ktok_d", [T_ALL, 512], BF16); dscr("Vtok_d", [T_ALL, 512], BF16)
    dscr("oT_d", [2, 512, T_ALL], F32)
    dscr("aT_d", [1024, T_ALL], BF16)
    dscr("X_d", [T_ALL, DM], F32); dscr("X1_d", [T_ALL, DM], F32); dscr("h2T_d", [DM, T_ALL], BF16)
    dscr("m1T_d", [4 * DM, T_ALL], BF16)

    with ExitStack() as st:
        kb = KB(nc, st)
        bg = (stages is None) or ("g2" in stages)
        if stages is None or "cast" in stages:
            items = [("w_in", "winb", 0, None)]
            if not bg:
                for l_ in range(n_layers):
                    items += layer_cast_items(l_, n_layers)
            cast_weights(kb, D, items)
        if stages is None or "mod" in stages:
            mod_stage(kb, D, n_layers)
        for l in range(n_layers):
            last = (l == DEPTH - 1)
            xsrc = D["xin"] if l == 0 else D["X_d"]
            if stages is None or "s1" in stages:
                stage1(kb, D, l, xsrc)
            if stages is None or "na" in stages:
                stage_na(kb, D, l, last)
            if stages is None or "g1" in stages:
                stage_g1(kb, D, l)
            if stages is None or "g2" in stages:
                stage_g2(kb, D, l, layer_cast_items(l, n_layers))
            if stages is None or "g3" in stages:
                stage_g3(kb, D, l, last)
            if stages is None or "s4a" in stages:
                stage4a(kb, D, l, xsrc, last)
            if stages is None or "s4b" in stages:
                stage4b(kb, D, l, last)
        kb.barrier()
    nc._n_ins = kb.n_ins
    return nc


def run_gens(gens):
    gens = list(gens)
    while gens:
        for g_ in list(gens):
            try:
                next(g_)
            except StopIteration:
                gens.remove(g_)


def cload(kb, D, name, dt=F32, eng="dve", rows=128):
    off, w = CONST_LAYOUT[name]
    t = kb.sb("c_" + name, [128, w], F32)
    kb.dma("sp", t[:], D["consts"][:, off:off + w])
    if dt == F32:
        return t
    tb = kb.sb("cb_" + name, [128, w], dt)
    kb.copy(eng, tb[:], t[:])
    return tb


def cast_jobs(D, items):
    jobs = []
    for src, dst, l, w in items:
        s2 = D[src][l]
        d2 = D[dst][l]
        if w:
            s2 = s2.rearrange("(a b) c -> a (b c)", b=w)
            d2 = d2.rearrange("(a b) c -> a (b c)", b=w)
        R, C = s2.shape
        for r0 in range(0, R, 128):
            for c0 in range(0, C, 2048):
                cw = min(2048, C - c0)
                jobs.append((s2[r0:r0 + 128, c0:c0 + cw], d2[r0:r0 + 128, c0:c0 + cw], cw))
    return jobs


def layer_cast_items(l, n_layers):
    it = [("w_out", "woutb", l, 2), ("w_mlp1", "w1b", l, None), ("w_mlp2", "w2b", l, 2)]
    if l + 1 < n_layers:
        it.append(("w_in", "winb", l + 1, None))
    return it


def cast_weights(kb, D, items):
    with kb.stage():
        NB = 3
        fb = [kb.sb("cf", [128, 2048], F32) for _ in range(NB)]
        bb = [kb.sb("cb", [128, 2048], BF16) for _ in range(NB)]
        for i, (s, d, cw) in enumerate(cast_jobs(D, items)):
            b = i % NB
            kb.dma("sp", fb[b][:, :cw], s)
            kb.copy("dve" if i % 2 == 0 else "act", bb[b][:, :cw], fb[b][:, :cw])
            kb.dma("pool", d, bb[b][:, :cw])


def cast_gen(kb, D, items, fb, bb):
    NB = len(fb)
    for i, (s, d, cw) in enumerate(cast_jobs(D, items)):
        b = i % NB
        kb.dma("sp", fb[b][:, :cw], s)
        yield
        kb.copy("act", bb[b][:, :cw], fb[b][:, :cw])
        kb.dma("act", d, bb[b][:, :cw])
        yield


def mod_stage(kb, D, n_layers):
    with kb.stage():
        cT = kb.sb("cT", [128, 2, 8], F32)
        kb.dma("sp", cT[:], D["cvec"].rearrange("r (k p) -> p r k", p=128), allow_slow_non_contiguous=True)
        scl = kb.sb("scl", [128, 8, 33], F32)
        kb.memset("dve", scl[:], 0.0)
        kb.act(scl[:, :, 0], cT[:, 0, :], AF.Silu)
        kb.act(scl[:, :, 32], cT[:, 1, :], AF.Silu)
        wt = [kb.sb("wada", [128, 8, 512], F32) for _ in range(2)]
        bt = [kb.sb("bada", [33, 512], F32) for _ in range(2)]
        mr = [kb.sb("mrow", [33, 512], F32) for _ in range(2)]
        pm = [kb.ps("pm", [128, 512]) for _ in range(2)]
        for b in bt:
            kb.memset("dve", b[:], 0.0)
        i = 0
        for l in range(n_layers):
            for cb in range(12):
                j = i % 2
                i += 1
                cols = slice(cb * 512, (cb + 1) * 512)
                kb.dma("sp", wt[j][:], D["w_ada"][l, :, cols].rearrange("(k p) c -> p k c", p=128))
                kb.dma("sp", bt[j][0:1, :], D["b_ada"][l:l + 1, cols])
                kb.dma("sp", bt[j][32:33, :], D["b_ada"][l:l + 1, cols])
                for kc in range(8):
                    kb.mm(pm[j][0:33, :], scl[:, kc, :], wt[j][:, kc, :], start=(kc == 0), stop=(kc == 7))
                kb.tt("dve", mr[j][:], pm[j][0:33, :], bt[j][:], ALU.add)
                if cb in (2, 3, 8, 9):
                    kb.ts("dve", mr[j][:], mr[j][:], 1.0, ALU.add)
                kb.dma("pool", D["mod_d"][l, 0:1, cols], mr[j][0:1, :])
                kb.dma("pool", D["mod_d"][l, 1:2, cols], mr[j][32:33, :])


def bvec(kb, name, src):
    n = src.shape[-1]
    t = kb.sb(name, [128, n], F32)
    kb.dma("sp", t[:], src.partition_broadcast(128))
    return t


def stage1(kb, D, l, xsrc):
    with kb.stage():
        w = kb.sb("win", [128, 8, IN_COLS], BF16)
        for kc in range(8):
            kb.dma("sp", w[:, kc, :], D["winb"][l, kc * 128:(kc + 1) * 128, :])
        mods = {}
        for r, nm in ((0, "l"), (1, "c")):
            mods[nm] = (bvec(kb, "sc1", D["mod_d"][l, r, 1024:2048]), bvec(kb, "sh1", D["mod_d"][l, r, 0:1024]))
        idb = cload(kb, D, "ident", BF16)
        xt = [kb.sb("xt", [128, DM], F32) for _ in range(2)]
        hb = [kb.sb("hb", [128, DM], BF16) for _ in range(2)]
        hT = [kb.sb("hT", [128, 8, 512], BF16) for _ in range(2)]
        fm = [kb.sb("fm", [128, 6, 512], BF16) for _ in range(4)]
        vt = [kb.sb("vt", [128, 512], BF16) for _ in range(2)]
        bat = [kb.sb("bat", [128, 16], F32) for _ in range(2)]
        pT = [kb.ps("pT", [128, 1024], BF16) for _ in range(2)]
        pF = [kb.ps("pF", [128, 512]) for _ in range(3)]
        pV = kb.ps("pV", [128, 512])
        pB = kb.ps("pB", [128, 512])
        FMv = D["FM_d"].rearrange("(b p) t -> p b t", p=128)
        ti = 0
        ei = 0
        fi = 0
        for g in range(9):
            ntile = 4 if g < 8 else 2
            ntok = ntile * 128
            tok0 = g * 512
            sc, sh = mods["l"] if g < 8 else mods["c"]
            hTg = hT[g % 2]
            for tt_ in range(ntile):
                j = ti % 2
                ti += 1
                rows = slice(tok0 + tt_ * 128, tok0 + (tt_ + 1) * 128)
                kb.dma("sp", xt[j][:], xsrc[rows, :])
                kb.tt("dve", xt[j][:], xt[j][:], sc[:], ALU.mult)
                kb.tt("dve", hb[j][:], xt[j][:], sh[:], ALU.add)
                for kc in range(8):
                    kb.tr(pT[j][:, kc * 128:(kc + 1) * 128], hb[j][:, kc * 128:(kc + 1) * 128], idb[:])
                kb.act(hTg[:, :, tt_ * 128:(tt_ + 1) * 128], pT[j][:].rearrange("p (k t) -> p k t", k=8), AF.Copy)
                for kc in range(8):
                    kb.mm(pV[:], hTg[:, kc, tt_ * 128:(tt_ + 1) * 128], w[:, kc, 1024:1536], start=(kc == 0), stop=(kc == 7))
                for kc in range(8):
                    kb.mm(pB[:, 0:16], hTg[:, kc, tt_ * 128:(tt_ + 1) * 128], w[:, kc, 3584:3600], start=(kc == 0), stop=(kc == 7))
                kb.copy("dve", vt[j][:], pV[:])
                kb.copy("dve", bat[j][:], pB[:, 0:16])
                kb.dma("pool", D["v_d"][rows, :], vt[j][:])
                kb.dma("pool", D["ba_d"][rows, :], bat[j][:])
            for blk in range(24):
                col0 = blk * 128 if blk < 8 else 1536 + (blk - 8) * 128
                p = pF[ei % 3]
                for kc in range(8):
                    kb.mm(p[:, :ntok], w[:, kc, col0:col0 + 128], hTg[:, kc, :ntok], start=(kc == 0), stop=(kc == 7))
                if blk % 6 == 0:
                    fcur = fm[fi % 4]
                    fi += 1
                if blk < 4:
                    kb.act(fcur[:, blk % 6, :ntok], p[:, :ntok], AF.Copy, scale=0.125)
                elif ei % 2 == 0:
                    kb.act(fcur[:, blk % 6, :ntok], p[:, :ntok], AF.Copy)
                else:
                    kb.copy("dve", fcur[:, blk % 6, :ntok], p[:, :ntok])
                ei += 1
                if blk % 6 == 5:
                    b0 = blk - 5
                    kb.dma("pool", FMv[:, b0:b0 + 6, tok0:tok0 + ntok], fcur[:, :, :ntok])


def na_pieces(c):
    out = []
    r0 = 8 * c
    for j in range(32):
        runs = []
        for r in range(r0, r0 + 8):
            rs = min(max(r - 4, 0), 56)
            if 2 * j + 1 < rs or 2 * j > rs + 7:
                continue
            edge = (r < 4) or (r > 60)
            kind = "G" if edge else "S"
            var = (7 - 2 * j + r) if edge else (r - 2 * j + 3)
            if runs and runs[-1][0] == kind and runs[-1][2] == r and runs[-1][1] + (r - runs[-1][3]) == var:
                runs[-1][2] = r + 1
            else:
                runs.append([kind, var, r + 1, r])
        for kind, var, rend, rstart in runs:
            out.append((j, (rstart - r0) * 64, (rend - rstart) * 64, (kind, var)))
    return out


def stage_na(kb, D, l, last):
    with kb.stage():
        colmask = cload(kb, D, "colmask")
        m9 = cload(kb, D, "M9")
        idf = cload(kb, D, "ident")
        vall = kb.sb("vall", [128, NT, 512], BF16)
        v3 = D["v_d"].rearrange("(t p) c -> p t c", p=128)
        for t0 in range(0, NT, 6):
            t1 = min(NT, t0 + 6)
            kb.dma("sp", vall[:, t0:t1, :], v3[:, t0:t1, :])
        vaug = [kb.sb("vaug", [128, NT, 128], BF16) for _ in range(2)]
        qk = [kb.sb("qk", [64, 2, T_ALL], BF16) for _ in range(2)]
        Gt = [kb.sb("Gt", [128, 1024], F32) for _ in range(2)]
        GM = [kb.sb("GM", [128, 1024], BF16) for _ in range(2)]
        S9 = [kb.sb("S9", [128, 576], BF16) for _ in range(2)]
        rcs = [kb.sb("rcs", [128, 512], F32) for _ in range(2)]
        for sid in range(2):
            kb.memset("pool", vaug[sid][:, :, 64:128], 1.0)
            kb.memset("dve", rcs[sid][:], 0.0)

        def mk2(name, shape, dt):
            return [[kb.sb(name, shape, dt) for _ in range(2)] for _ in range(2)]

        sc = mk2("sc", [128, 512], F32)
        pt = mk2("pt", [128, 512], BF16)
        osb = mk2("osb", [64, 512], F32)
        ob = mk2("ob", [64, 512], BF16)
        pS = [[kb.ps("pS", [128, 512]) for _ in range(2)] for _ in range(2)]
        pO = [[kb.ps("pO", [128, 512]) for _ in range(2)] for _ in range(2)]

        def headgen(h, sid):
            q_ = qk[sid]
            va = vaug[sid]
            kb.dma("sp", q_[:, 0, :], D["FM_d"][h * 64:(h + 1) * 64, :])
            kb.dma("sp", q_[:, 1, :], D["FM_d"][512 + h * 64:512 + (h + 1) * 64, :])
            kb.dma("sp", Gt[sid][:], D["Gp"][l, h])
            kb.copy("pool", va[:, :, 0:64], vall[:, :, h * 64:(h + 1) * 64])
            kb.tt("dve", GM[sid][:].rearrange("p (e q) -> p e q", e=16), Gt[sid][:].rearrange("p (e q) -> p e q", e=16),
                  colmask[:].unsqueeze(1).to_broadcast([128, 16, 64]), ALU.add)
            kb.tt("dve", S9[sid][:], GM[sid][:, 4 * 64:13 * 64], m9[:], ALU.add)
            yield
            pi = 0
            for c in range(9):
                if c == 8 and last:
                    continue
                ncols = 512 if c < 8 else 256
                q0 = c * 512
                pieces = [(32, 0, ncols, None), (33, 0, ncols, None)]
                if c < 8:
                    pieces += na_pieces(c)
                po = pO[sid][c % 2]
                for k, (j, lo, n, bias) in enumerate(pieces):
                    p = pS[sid][pi % 2]
                    e = pt[sid][pi % 2]
                    s_ = sc[sid][pi % 2]
                    pi += 1
                    kb.mm(p[:, :n], q_[:, 1, j * 128:(j + 1) * 128], q_[:, 0, q0 + lo:q0 + lo + n], start=True, stop=True)
                    yield
                    if bias is not None:
                        src = S9[sid] if bias[0] == "S" else GM[sid]
                        kb.tt("dve", s_[:, :n], p[:, :n], src[:, bias[1] * 64:bias[1] * 64 + n], ALU.add)
                        yield
                        kb.act(e[:, :n], s_[:, :n], AF.Exp)
                    else:
                        kb.act(e[:, :n], p[:, :n], AF.Exp)
                    yield
                    kb.mm(po[:, lo:lo + n], va[:, j, :], e[:, :n], start=(k == 0), stop=(k == len(pieces) - 1))
                    yield
                ci = c % 2
                r_ = rcs[sid]
                os_ = osb[sid][ci]
                o_ = ob[sid][ci]
                pb = pS[sid][pi % 2]
                pi += 1
                kb.act(r_[64:128, :ncols], po[64:128, :ncols], AF.Ln)
                kb.act(r_[64:128, :ncols], r_[64:128, :ncols], AF.Exp, scale=-1.0)
                kb.copy("act", os_[:, :ncols], po[0:64, :ncols])
                yield
                kb.mm(pb[0:64, :ncols], idf[:, 64:128], r_[:, :ncols], start=True, stop=True)
                yield
                kb.tt("dve", o_[:, :ncols], os_[:, :ncols], pb[0:64, :ncols], ALU.mult)
                kb.dma("pool", D["aT_d"][h * 64:(h + 1) * 64, q0:q0 + ncols], o_[:, :ncols])
                yield

        for h0 in range(0, 8, 2):
            run_gens([headgen(h0, 0), headgen(h0 + 1, 1)])


def stage_g1(kb, D, l):
    with kb.stage():
        idf = cload(kb, D, "ident")
        idb = kb.sb("idb", [128, 128], BF16)
        kb.copy("dve", idb[:], idf[:])
        onesb = cload(kb, D, "ones", BF16)
        rpb_ = cload(kb, D, "Rperm", BF16)
        rope = kb.sb("rope", [128, 2, T_LAT], F32)
        kb.dma("sp", rope[:, 0, :], D["rope"][:, 0, :])
        kb.dma("sp", rope[:, 1, :], D["rope"][:, 1, :])
        cw = kb.sb("cw", [128, 5, 12], F32)
        for j in range(5):
            kb.dma("sp", cw[:, j, :], D["conv_w"][l, j].rearrange("(f p) -> p f", p=128), allow_slow_non_contiguous=True)
        XW = 4360
        xin = [kb.sb("gxin", [128, XW], BF16) for _ in range(2)]
        xsh = [kb.sb("gxsh", [128, XW], BF16) for _ in range(2)]
        for x_ in xin + xsh:
            kb.memset("dve", x_[:], 0.0)
        sall = [kb.sb("sall", [128, T_ALL], F32) for _ in range(2)]
        dg = [kb.sb("dg", [128, 5, 128], BF16) for _ in range(2)]

        def mk2(name, shape, dt):
            return [[kb.sb(name, shape, dt) for _ in range(2)] for _ in range(2)]

        sq = mk2("sq", [128, 512], BF16)
        lnb = mk2("lnb", [128, 512], F32)
        qn = mk2("qn", [128, 512], BF16)
        t1 = mk2("t1", [128, 512], F32)
        t2 = mk2("t2", [128, 512], F32)
        obq = mk2("obq", [128, 512], BF16)
        tk_ = mk2("tkst", [128, 4, 128], BF16)
        pc = [kb.ps("pc", [128, 512]) for _ in range(2)]
        pn = [kb.ps("pn", [128, 512]) for _ in range(2)]
        pr = [kb.ps("pr", [128, 512]) for _ in range(2)]
        pt = [kb.ps("ptr", [128, 8, 128], BF16) for _ in range(2)]
        blocks = [(b * 512, 512, 2 + b * 512) for b in range(8)] + [(T_LAT, 256, 4102)]

        def fcgen(fc, sid):
            x_, xs_, d_, sa = xin[sid], xsh[sid], dg[sid], sall[sid]
            src = D["FM_d"][1024 + fc * 128:1024 + (fc + 1) * 128, :]
            kb.dma("sp", x_[:, 2:2 + T_LAT], src[:, 0:T_LAT])
            kb.dma("sp", x_[:, 4102:4102 + T_CTX], src[:, T_LAT:T_ALL])
            kb.dma("sp", xs_[:, 1:1 + T_LAT], src[:, 0:T_LAT])
            kb.dma("sp", xs_[:, 4101:4101 + T_CTX], src[:, T_LAT:T_ALL])
            for j in range(5):
                kb.ts("dve", d_[:, j, :], idf[:], cw[:, j, fc:fc + 1], ALU.mult)
            yield
            for (tok0, n, xc) in blocks:
                p = pc[sid]
                for j in range(5):
                    if j % 2 == 0:
                        rhs_ = x_[:, xc + j - 2:xc + j - 2 + n]
                    else:
                        rhs_ = xs_[:, xc + j - 3:xc + j - 3 + n]
                    kb.mm(p[:, :n], d_[:, j, :], rhs_, start=(j == 0), stop=(j == 4))
                yield
                kb.act(sa[:, tok0:tok0 + n], p[:, :n], AF.Silu)
                yield
            isq = fc < 4
            isv = fc >= 8
            hh = fc % 4
            for bi, (tok0, n, xc) in enumerate(blocks):
                i = bi % 2
                o_ = obq[sid][i]
                if isv:
                    kb.copy("dve", o_[:, :n], sa[:, tok0:tok0 + n])
                    yield
                else:
                    kb.tt("pool", sq[sid][i][:, :n], sa[:, tok0:tok0 + n], sa[:, tok0:tok0 + n], ALU.mult)
                    yield
                    kb.mm(pn[sid][:, :n], onesb[:], sq[sid][i][:, :n])
                    yield
                    kb.act(lnb[sid][i][:, :n], pn[sid][:, :n], AF.Ln, bias=NORM_EPS)
                    kb.act(lnb[sid][i][:, :n], lnb[sid][i][:, :n], AF.Exp, scale=-0.5, bias=(-0.5 * math.log(128.0) if isq else 0.0))
                    yield
                    if tok0 < T_LAT:
                        kb.tt("dve", qn[sid][i][:, :n], sa[:, tok0:tok0 + n], lnb[sid][i][:, :n], ALU.mult)
                        yield
                        kb.mm(pr[sid][:, :n], rpb_[:], qn[sid][i][:, :n])
                        kb.tt("pool", t1[sid][i][:, :n], qn[sid][i][:, :n], rope[:, 0, tok0:tok0 + n], ALU.mult)
                        yield
                        kb.tt("dve", t2[sid][i][:, :n], pr[sid][:, :n], rope[:, 1, tok0:tok0 + n], ALU.mult)
                        yield
                        kb.tt("dve", o_[:, :n], t1[sid][i][:, :n], t2[sid][i][:, :n], ALU.add)
                        yield
                    else:
                        kb.tt("dve", o_[:, :n], sa[:, tok0:tok0 + n], lnb[sid][i][:, :n], ALU.mult)
                        yield
                    kb.dma("pool", D["GQK_d"][fc * 128:(fc + 1) * 128, tok0:tok0 + n], o_[:, :n])
                if fc >= 4:
                    nt_ = n // 128
                    for a in range(nt_):
                        kb.tr(pt[sid][:, a, :], o_[:, a * 128:(a + 1) * 128], idb[:])
                    yield
                    kb.copy("act", tk_[sid][i][:, :nt_, :], pt[sid][:, :nt_, :])
                    dst = D["Vtok_d"] if isv else D["Ktok_d"]
                    kb.dma("pool", dst[tok0:tok0 + n, hh * 128:(hh + 1) * 128].rearrange("(a p) c -> p a c", p=128), tk_[sid][i][:, :nt_, :])
                    yield

        for fc0 in range(0, 12, 2):
            run_gens([fcgen(fc0, 0), fcgen(fc0 + 1, 1)])


def stage_g2(kb, D, l, cast_items=()):
    with kb.stage():
        cfb = [kb.sb("cf", [128, 2048], F32) for _ in range(3)]
        cbb = [kb.sb("cb", [128, 2048], BF16) for _ in range(3)]
        castg = cast_gen(kb, D, cast_items, cfb, cbb)
        idf = cload(kb, D, "ident")
        idb = kb.sb("idb", [128, 128], BF16)
        kb.copy("dve", idb[:], idf[:])
        onesf = cload(kb, D, "ones")
        Ua = [cload(kb, D, "U1"), cload(kb, D, "U1T")]
        Sa = [cload(kb, D, "SL"), cload(kb, D, "SU")]
        NEGs = [cload(kb, D, "NEGf", BF16), cload(kb, D, "NEGb", BF16)]
        NEGT = [cload(kb, D, "NEGTf", BF16), cload(kb, D, "NEGTb", BF16)]
        ba = kb.sb("ba", [128, NT, 16], F32)
        kb.dma("sp", ba[:], D["ba_d"].rearrange("(t p) c -> p t c", p=128))
        al = bvec(kb, "alog", D["a_log"][l])
        dtb = bvec(kb, "dtb", D["dt_bias"][l])
        nA = kb.sb("nA", [128, 8], F32)
        kb.act(nA[:], al[:], AF.Exp)
        kb.ts("dve", nA[:], nA[:], -1.0, ALU.mult)
        gx = kb.sb("gx", [128, NT, 8], F32)
        g_all = kb.sb("g_all", [128, NT, 8], F32)
        beta = kb.sb("beta", [128, NT, 8], F32)
        negb = kb.sb("negb", [128, NT, 8], F32)
        kb.tt("dve", gx[:], ba[:, :, 8:16], dtb[:].unsqueeze(1).to_broadcast([128, NT, 8]), ALU.add)
        kb.act(gx[:], gx[:], AF.Exp)
        kb.act(gx[:], gx[:], AF.Ln, bias=1.0)
        kb.tt("dve", g_all[:], gx[:], nA[:].unsqueeze(1).to_broadcast([128, NT, 8]), ALU.mult)
        kb.act(beta[:], ba[:, :, 0:8], AF.Sigmoid)
        kb.ts("dve", negb[:], beta[:], -1.0, ALU.mult)

        def mk(name, shape, dt, n=2):
            return [[kb.sb(name, shape, dt) for _ in range(n)] for _ in range(2)]

        QK = mk("QK", [128, 8, 128], BF16)
        KV = mk("KV", [128, 2, 512], BF16)
        gS = mk("gS", [128, 4, 128], F32, 1)
        gU = mk("gU", [128, 4, 128], F32, 1)
        Ds = mk("Ds", [128, 4, 128], F32, 1)
        DT = mk("DT", [128, 4, 128], F32, 1)
        Er = mk("Er", [128, 4, 128], F32, 1)
        E3 = mk("E3", [128, 3, 4], F32)
        be = mk("be", [128, 4], F32, 1)
        negA = mk("negA", [128, 4, 128], F32, 1)
        MP = mk("MP", [128, 4, 2, 128], BF16)
        Rm = mk("Rm", [128, 4, 128], BF16)
        Tm = mk("Tm", [128, 4, 128], BF16)
        N1 = mk("N1", [128, 4, 128], BF16, 1)
        N1T = mk("N1T", [128, 4, 128], BF16, 1)
        N2 = mk("N2", [128, 4, 128], BF16, 1)
        Yb = mk("Yb", [128, 4, 128], BF16, 1)
        Xb = mk("Xb", [128, 4, 128], BF16, 1)
        X2b = mk("X2b", [128, 4, 128], BF16, 1)
        Rb = mk("Rb", [128, 4, 128], BF16, 1)
        Dg = cload(kb, D, "Dg32")
        O1 = [cload(kb, D, "Off1"), cload(kb, D, "Off1T")]
        O1T = [O1[1], O1[0]]
        O2 = [cload(kb, D, "Off2"), cload(kb, D, "Off2T")]
        attnT = mk("attnT", [128, 4, 128], BF16)
        QdT = mk("QdT", [128, 4, 128], BF16)
        Kd = mk("Kd", [128, 4, 128], BF16)
        Kbe = mk("Kbe", [128, 4, 128], BF16, 1)
        bV = mk("bV", [128, 4, 128], BF16, 1)
        U = mk("U", [128, 4, 128], F32)
        WT = mk("WT", [128, 4, 128], BF16)
        vnew = mk("vnew", [128, 4, 128], BF16, 1)
        osb = mk("osb", [128, 4, 128], F32)
        S = [kb.sb("S", [128, 4, 128], F32) for _ in range(2)]
        Sb = [kb.sb("Sb", [128, 4, 128], BF16) for _ in range(2)]
        for d in range(2):
            kb.memset("dve", S[d][:], 0.0)
            kb.memset("pool", Sb[d][:], 0.0)
        pY = [kb.ps("pY", [128, 4, 2, 128]) for _ in range(2)]
        pZ = [kb.ps("pZ", [128, 4, 128]) for _ in range(2)]
        pW = kb.ps("pW", [128, 4, 128])
        pOo = kb.ps("pOo", [128, 4, 128])
        GQ = D["GQK_d"].rearrange("(a p) t -> p a t", p=128)
        OT = [D["oT_d"][d].rearrange("(h v) t -> v h t", v=128) for d in range(2)]

        order = [[32, 33] + list(range(32)), [33, 32] + list(range(31, -1, -1))]
        B4 = [128, 4, 128]
        f2 = lambda a: a[:].rearrange("p h t -> p (h t)")

        def pre(s, d):
            t = order[d][s]
            b = s % 2
            tok = slice(t * 128, (t + 1) * 128)
            qk = QK[d][b]
            kv = KV[d][b]
            PY = pY[d]
            PZ = pZ[d]
            PYf = PY[:].rearrange("p h w t -> p (h w t)")
            A0f = PYf[:, 0:512]
            A1f = PYf[:, 512:1024]
            A0 = A0f.rearrange("p (h t) -> p h t", h=4)
            A1 = A1f.rearrange("p (h t) -> p h t", h=4)
            kb.dma("sp", qk[:], GQ[:, :, tok])
            kb.dma("sp", kv[:, 0, :], D["Ktok_d"][tok, :])
            kb.dma("sp", kv[:, 1, :], D["Vtok_d"][tok, :])
            gcol = g_all[:, t, d * 4:(d + 1) * 4]
            bcol = beta[:, t, d * 4:(d + 1) * 4]
            nbcol = negb[:, t, d * 4:(d + 1) * 4]
            kb.tt("dve", gS[d][0][:], gcol.unsqueeze(2).to_broadcast(B4), Sa[d][:].unsqueeze(1).to_broadcast(B4), ALU.mult)
            kb.tt("pool", gU[d][0][:], gcol.unsqueeze(2).to_broadcast(B4), Ua[d][:].unsqueeze(1).to_broadcast(B4), ALU.mult)
            yield
            kb.mm(A0f, Ua[d][:], f2(gS[d][0]), start=True, stop=False)
            kb.mm(A0f, idb[:], NEGs[d][:], start=False, stop=True)
            kb.mm(A1f, Sa[d][:], f2(gU[d][0]), start=True, stop=False)
            kb.mm(A1f, idb[:], NEGT[d][:], start=False, stop=True)
            kb.mm(f2(PZ), onesf[:], f2(gU[d][0]), start=True, stop=True)
            yield
            kb.act(f2(Ds[d][0]), A0f, AF.Exp)
            kb.act(f2(DT[d][0]), A1f, AF.Exp)
            kb.act(Er[d][0][:], PZ[:], AF.Exp)
            yield
            kb.mm(PZ[:, 0, 0:4], Ua[d][:], gcol, start=True, stop=True)
            kb.mm(PZ[:, 0, 4:8], Sa[d][:], gcol, start=True, stop=True)
            kb.mm(PZ[:, 0, 8:12], onesf[:], gcol, start=True, stop=True)
            e3 = E3[d][b]
            kb.act(e3[:].rearrange("p a h -> p (a h)"), PZ[:, 0, 0:12], AF.Exp)
            kb.tt("dve", be[d][0][:], bcol, e3[:, 0, :], ALU.mult)
            yield
            for h in range(4):
                kb.mm(A0[:, h, :], qk[:, 4 + h, :], qk[:, 4 + h, :], start=True, stop=True)
            for h in range(4):
                kb.mm(A1[:, h, :], qk[:, 4 + h, :], qk[:, h, :], start=True, stop=True)
            yield
            na_ = negA[d][0]
            for h in range(4):
                kb.stt("dve", na_[:, h, :], A0[:, h, :], nbcol[:, h:h + 1], Ds[d][0][:, h, :], ALU.mult, ALU.mult)
            kb.tt("dve", attnT[d][b][:], A1, DT[d][0][:], ALU.mult)
            yield
            bc = lambda m: m[:].unsqueeze(1).to_broadcast(B4)
            mp = MP[d][0]
            for h in range(4):
                kb.tr(PZ[:, h, :], na_[:, h, :], idf[:])
            kb.tt("pool", mp[:, :, 1, :], na_[:], bc(Dg), ALU.mult)
            kb.tt("pool", N1[d][0][:], na_[:], bc(O1[d]), ALU.mult)
            kb.tt("pool", N2[d][0][:], na_[:], bc(O2[d]), ALU.mult)
            yield
            kb.tt("dve", mp[:, :, 0, :], PZ[:], bc(Dg), ALU.mult)
            kb.tt("dve", N1T[d][0][:], PZ[:], bc(O1T[d]), ALU.mult)
            yield
            kb.tt("pool", Tm[d][0][:], mp[:, :, 1, :], bc(idf), ALU.add)
            kb.tt("pool", Rm[d][0][:], mp[:, :, 0, :], bc(idf), ALU.add)
            yield
            cur = 0
            for it in range(1, 5):
                mpn = MP[d][1 - cur]
                mpc = MP[d][cur]
                for h in range(4):
                    kb.mm(PY[:, h, 0, :], mpc[:, h, 1, :], mpc[:, h, 0, :], start=True, stop=True)
                    kb.mm(PY[:, h, 1, :], mpc[:, h, 0, :], mpc[:, h, 1, :], start=True, stop=True)
                yield
                kb.copy("act", mpn[:, 0:2], PY[:, 0:2])
                kb.copy("act", mpn[:, 2:4], PY[:, 2:4])
                yield
                rn, rc_ = Rm[d][1 - cur], Rm[d][cur]
                tn, tc_ = Tm[d][1 - cur], Tm[d][cur]
                for h in range(4):
                    kb.mm(PZ[:, h, :], mpn[:, h, 1, :], rc_[:, h, :], start=True, stop=True)
                for h in range(4):
                    kb.mm(PY[:, h, 0, :], mpn[:, h, 0, :], tc_[:, h, :], start=True, stop=True)
                yield
                kb.tt("dve", rn[:], PZ[:], rc_[:], ALU.add)
                kb.tt("dve", tn[:], PY[:, :, 0, :], tc_[:], ALU.add)
                yield
                cur = 1 - cur
            R0_, T0_ = Rm[d][cur], Tm[d][cur]
            R1_, T1_ = Rm[d][1 - cur], Tm[d][1 - cur]
            for h in range(4):
                kb.mm(PZ[:, h, :], N1T[d][0][:, h, :], T0_[:, h, :], start=True, stop=True)
            for h in range(4):
                kb.mm(PY[:, h, 0, :], N1[d][0][:, h, :], R0_[:, h, :], start=True, stop=True)
            yield
            kb.copy("act", Yb[d][0][:], PZ[:])
            kb.copy("act", Xb[d][0][:], PY[:, :, 0, :])
            yield
            for h in range(4):
                kb.mm(PZ[:, h, :], R0_[:, h, :], Yb[d][0][:, h, :], start=True, stop=True)
            for h in range(4):
                kb.mm(PY[:, h, 0, :], T0_[:, h, :], Xb[d][0][:, h, :], start=True, stop=True)
            yield
            kb.tt("dve", T1_[:], PZ[:], T0_[:], ALU.add)
            kb.tt("dve", R1_[:], PY[:, :, 0, :], R0_[:], ALU.add)
            yield
            for h in range(4):
                kb.mm(PZ[:, h, :], N2[d][0][:, h, :], R1_[:, h, :], start=True, stop=True)
            yield
            kb.copy("act", X2b[d][0][:], PZ[:])
            yield
            for h in range(4):
                kb.mm(PZ[:, h, :], T1_[:, h, :], X2b[d][0][:, h, :], start=True, stop=True)
            yield
            r = Rb[d][0]
            kb.tt("dve", r[:], PZ[:], R1_[:], ALU.add)
            kb.tt("pool", bV[d][0][:], kv[:, 1, :].rearrange("p (h v) -> p h v", h=4), bcol.unsqueeze(2).to_broadcast(B4), ALU.mult)
            kb.tt("pool", Kbe[d][0][:], kv[:, 0, :].rearrange("p (h v) -> p h v", h=4), be[d][0][:].unsqueeze(2).to_broadcast(B4), ALU.mult)
            kb.tt("pool", Kd[d][b][:], kv[:, 0, :].rearrange("p (h v) -> p h v", h=4), e3[:, 1, :].unsqueeze(2).to_broadcast(B4), ALU.mult)
            kb.tt("dve", QdT[d][b][:], qk[:, 0:4, :], Er[d][0][:], ALU.mult)
            yield
            for h in range(4):
                kb.mm(PZ[:, h, :], r[:, h, :], bV[d][0][:, h, :], start=True, stop=True)
            for h in range(4):
                kb.mm(A0[:, h, :], Kbe[d][0][:, h, :], r[:, h, :], start=True, stop=True)
            yield
            kb.copy("act", U[d][b][:], PZ[:])
            kb.copy("act", WT[d][b][:], A0)
            yield

        def scan(s, d):
            t = order[d][s]
            b = s % 2
            tok = slice(t * 128, (t + 1) * 128)
            for h in range(4):
                kb.mm(pW[:, h, :], WT[d][b][:, h, :], Sb[d][:, h, :], start=True, stop=True)
            yield
            kb.tt("dve", vnew[d][0][:], U[d][b][:], pW[:], ALU.subtract)
            yield
            for h in range(4):
                kb.mm(pOo[:, h, :], Sb[d][:, h, :], QdT[d][b][:, h, :], start=True, stop=False)
                kb.mm(pOo[:, h, :], vnew[d][0][:, h, :], attnT[d][b][:, h, :], start=False, stop=True)
            for h in range(4):
                kb.mm(pW[:, h, :], Kd[d][b][:, h, :], vnew[d][0][:, h, :], start=True, stop=True)
            yield
            kb.copy("act", osb[d][b][:], pOo[:])
            kb.dma("pool", OT[d][:, :, tok], osb[d][b][:])
            for h in range(4):
                kb.stt("dve", S[d][:, h, :], S[d][:, h, :], E3[d][b][:, 2, h:h + 1], pW[:, h, :], ALU.mult, ALU.add)
            yield
            kb.copy("act", Sb[d][:], S[d][:])
            yield

        def run(gens):
            gens = list(gens)
            while gens:
                for g_ in list(gens):
                    try:
                        next(g_)
                    except StopIteration:
                        gens.remove(g_)

        run([pre(0, 0), pre(0, 1)])
        for s in range(NT):
            def scans(s_):
                yield from scan(s_, 0)
                yield from scan(s_, 1)
            def limited(g_, n_):
                for _ in range(n_):
                    try:
                        next(g_)
                    except StopIteration:
                        return
                    yield
            gl = [scans(s), limited(castg, 6)]
            if s + 1 < NT:
                gl += [pre(s + 1, 0), pre(s + 1, 1)]
            run(gl)
        for _ in castg:
            pass


def stage_g3(kb, D, l, last):
    with kb.stage():
        meanb = kb.sb("meanb", [128, 128], BF16)
        kb.memset("dve", meanb[:], 1.0 / 128.0)
        gw = kb.sb("gw", [128, 1], F32)
        kb.dma("sp", gw[:], D["gdn_norm_w"][l].rearrange("(p o) -> p o", o=1))
        ntok_all = T_LAT if last else T_ALL
        zt = [kb.sb("zt", [128, T_ALL], BF16) for _ in range(2)]
        sz = [kb.sb("sz", [128, T_ALL], F32) for _ in range(2)]

        def mk2(name, shape, dt):
            return [[kb.sb(name, shape, dt) for _ in range(2)] for _ in range(2)]

        of = mk2("of", [128, 512], F32)
        obk = mk2("obk", [128, 512], F32)
        sq = mk2("sq", [128, 512], BF16)
        rs = mk2("rs", [128, 512], F32)
        yo = mk2("yo", [128, 512], BF16)
        pn = [kb.ps("pn", [128, 512]) for _ in range(2)]
        blocks = [(b * 512, 512) for b in range(8)] + ([] if last else [(T_LAT, 256)])

        def headgen(h, sid):
            kb.dma("sp", zt[sid][:, :ntok_all], D["FM_d"][2560 + h * 128:2560 + (h + 1) * 128, 0:ntok_all])
            yield
            for c0 in range(0, ntok_all, 1088):
                c1 = min(ntok_all, c0 + 1088)
                kb.act(sz[sid][:, c0:c1], zt[sid][:, c0:c1], AF.Silu)
                yield
            for bi, (tok0, n) in enumerate(blocks):
                j = bi % 2
                o_, ob_, sq_, rs_, yo_ = of[sid][j], obk[sid][j], sq[sid][j], rs[sid][j], yo[sid][j]
                kb.dma("sp", o_[:, :n], D["oT_d"][0, h * 128:(h + 1) * 128, tok0:tok0 + n])
                kb.dma("sp", ob_[:, :n], D["oT_d"][1, h * 128:(h + 1) * 128, tok0:tok0 + n])
                yield
                kb.tt("pool", o_[:, :n], o_[:, :n], ob_[:, :n], ALU.add)
                yield
                kb.tt("pool", sq_[:, :n], o_[:, :n], o_[:, :n], ALU.mult)
                yield
                kb.mm(pn[sid][:, :n], meanb[:], sq_[:, :n])
                yield
                kb.act(rs_[:, :n], pn[sid][:, :n], AF.Ln, bias=NORM_EPS)
                kb.act(rs_[:, :n], rs_[:, :n], AF.Exp, scale=-0.5)
                yield
                kb.tt("dve", o_[:, :n], o_[:, :n], rs_[:, :n], ALU.mult)
                yield
                kb.stt("dve", yo_[:, :n], o_[:, :n], gw[:, 0:1], sz[sid][:, tok0:tok0 + n], ALU.mult, ALU.mult)
                kb.dma("pool", D["aT_d"][512 + h * 128:512 + (h + 1) * 128, tok0:tok0 + n], yo_[:, :n])
                yield

        for h0 in range(0, 4, 2):
            run_gens([headgen(h0, 0), headgen(h0 + 1, 1)])


def layer_norm_gen(kb, r, out, stt_, mv, rs, gam, bet):
    for hh in range(2):
        kb.op("dve", lambda g: g.bn_stats(out=stt_[:, hh, :], in_=r[:, hh * 512:(hh + 1) * 512]), [r[:]], [stt_[:]])
    kb.op("dve", lambda g: g.bn_aggr(out=mv[:], in_=stt_[:]), [stt_[:]], [mv[:]])
    yield
    kb.act(rs[:, 0:1], mv[:, 1:2], AF.Ln, bias=LN_EPS)
    kb.act(rs[:, 0:1], rs[:, 0:1], AF.Exp, scale=-0.5)
    yield
    kb.stt("dve", rs[:, 1:2], mv[:, 0:1], -1.0, rs[:, 0:1], ALU.mult, ALU.mult)
    yield
    kb.act(out[:], r[:], AF.Identity, scale=rs[:, 0:1], bias=rs[:, 1:2])
    yield
    kb.tt("pool", out[:], out[:], gam[:], ALU.mult)
    kb.tt("pool", out[:], out[:], bet[:], ALU.add)
    yield


def layer_norm_tile(kb, r, out, stt_, mv, rs, gam, bet):
    for _ in layer_norm_gen(kb, r, out, stt_, mv, rs, gam, bet):
        pass


def stage4a(kb, D, l, xsrc, last):
    with kb.stage():
        wo = kb.sb("wo", [128, 8, DM], BF16)
        for kc in range(8):
            kb.dma("sp", wo[:, kc, :], D["woutb"][l, kc * 128:(kc + 1) * 128, :])
        idb = cload(kb, D, "ident", BF16)
        lg = bvec(kb, "ln1g", D["ln1_g"][l])
        lb = bvec(kb, "ln1b", D["ln1_b"][l])
        mods = {}
        for r_, nm in ((0, "l"), (1, "c")):
            if nm == "c" and last:
                continue
            mods[nm] = (bvec(kb, "g1", D["mod_d"][l, r_, 2048:3072]), bvec(kb, "sc2", D["mod_d"][l, r_, 4096:5120]),
                        bvec(kb, "sh2", D["mod_d"][l, r_, 3072:4096]))
        aT = [kb.sb("aT", [128, 8, 512], BF16) for _ in range(2)]
        xt = [kb.sb("xt", [128, DM], F32) for _ in range(2)]
        rr = [kb.sb("rr", [128, DM], F32) for _ in range(2)]
        x1 = [kb.sb("x1", [128, DM], F32) for _ in range(2)]
        hb = [kb.sb("hb", [128, DM], BF16) for _ in range(2)]
        hT = [kb.sb("hT", [128, 8, 512], BF16) for _ in range(2)]
        stt_ = [kb.sb("stt", [128, 2, 6], F32) for _ in range(2)]
        mv = [kb.sb("mv", [128, 2], F32) for _ in range(2)]
        rs = [kb.sb("rs", [128, 2], F32) for _ in range(2)]
        py = [kb.ps("py", [128, 1024]) for _ in range(2)]
        pT = [kb.ps("pT", [128, 1024], BF16) for _ in range(2)]
        aTv = D["aT_d"].rearrange("(k p) t -> p k t", p=128)
        hTv = D["h2T_d"].rearrange("(k p) t -> p k t", p=128)
        for g in range(9):
            if g == 8 and last:
                continue
            ntile = 4 if g < 8 else 2
            ntok = ntile * 128
            tok0 = g * 512
            g1, sc2, sh2 = mods["l"] if g < 8 else mods["c"]
            a_ = aT[g % 2]
            hTg = hT[g % 2]
            kb.dma("sp", a_[:, :, :ntok], aTv[:, :, tok0:tok0 + ntok])

            def tilegen(tt_, j):
                rows = slice(tok0 + tt_ * 128, tok0 + (tt_ + 1) * 128)
                kb.dma("sp", xt[j][:], xsrc[rows, :])
                for hh in range(2):
                    for kc in range(8):
                        kb.mm(py[j][:, hh * 512:(hh + 1) * 512], a_[:, kc, tt_ * 128:(tt_ + 1) * 128], wo[:, kc, hh * 512:(hh + 1) * 512],
                              start=(kc == 0), stop=(kc == 7))
                yield
                kb.tt("dve", rr[j][:], py[j][:], g1[:], ALU.mult)
                kb.stt("dve", rr[j][:], xt[j][:], ALPHA, rr[j][:], ALU.mult, ALU.add)
                yield
                yield from layer_norm_gen(kb, rr[j], x1[j], stt_[j], mv[j], rs[j], lg, lb)
                kb.dma("pool", D["X1_d"][rows, :], x1[j][:])
                kb.tt("dve", rr[j][:], x1[j][:], sc2[:], ALU.mult)
                kb.tt("dve", hb[j][:], rr[j][:], sh2[:], ALU.add)
                yield
                for kc in range(8):
                    kb.tr(pT[j][:, kc * 128:(kc + 1) * 128], hb[j][:, kc * 128:(kc + 1) * 128], idb[:])
                yield
                kb.act(hTg[:, :, tt_ * 128:(tt_ + 1) * 128], pT[j][:].rearrange("p (k t) -> p k t", k=8), AF.Copy)
                yield

            for t0_ in range(0, ntile, 2):
                run_gens([tilegen(t0_, 0), tilegen(t0_ + 1, 1)])
            kb.dma("pool", hTv[:, :, tok0:tok0 + ntok], hTg[:, :, :ntok])


def stage4b(kb, D, l, last):
    ngroups = 8 if last else 9
    m1v = D["m1T_d"].rearrange("(j p) t -> p j t", p=128)
    hTv = D["h2T_d"].rearrange("(k p) t -> p k t", p=128)
    with kb.stage():
        w1 = kb.sb("w1", [128, 8, 4 * DM], BF16)
        for kc in range(8):
            kb.dma("sp", w1[:, kc, :], D["w1b"][l, kc * 128:(kc + 1) * 128, :])
        hT = [kb.sb("hT", [128, 8, 512], BF16) for _ in range(2)]
        rl = [kb.sb("rl", [128, 512], F32) for _ in range(4)]
        mo = [kb.sb("mo", [128, 8, 512], BF16) for _ in range(3)]
        pm = [kb.ps("pm", [128, 512]) for _ in range(4)]
        mi = 0
        oi = 0
        for g in range(ngroups):
            ntok = 512 if g < 8 else 256
            tok0 = g * 512
            h_ = hT[g % 2]
            kb.dma("sp", h_[:, :, :ntok], hTv[:, :, tok0:tok0 + ntok])
            for j in range(32):
                p = pm[mi % 4]
                r_ = rl[mi % 4]
                if j % 8 == 0:
                    m_ = mo[oi % 3]
                    oi += 1
                for kc in range(8):
                    kb.mm(p[:, :ntok], w1[:, kc, j * 128:(j + 1) * 128], h_[:, kc, :ntok], start=(kc == 0), stop=(kc == 7))
                kb.act(r_[:, :ntok], p[:, :ntok], AF.Relu)
                kb.tt("dve" if mi % 2 == 0 else "pool", m_[:, j % 8, :ntok], r_[:, :ntok], r_[:, :ntok], ALU.mult)
                mi += 1
                if j % 8 == 7:
                    kb.dma("pool", m1v[:, j - 7:j + 1, tok0:tok0 + ntok], m_[:, :, :ntok])
    with kb.stage():
        w2 = kb.sb("w2", [128, 32, DM], BF16)
        w2v = D["w2b"][l].rearrange("(j p) c -> p j c", p=128)
        for j0 in range(0, 32, 8):
            kb.dma("sp", w2[:, j0:j0 + 8, :], w2v[:, j0:j0 + 8, :])
        lg = bvec(kb, "ln2g", D["ln2_g"][l])
        lb = bvec(kb, "ln2b", D["ln2_b"][l])
        g2 = {"l": bvec(kb, "g2", D["mod_d"][l, 0, 5120:6144])}
        if not last:
            g2["c"] = bvec(kb, "g2c", D["mod_d"][l, 1, 5120:6144])
        mt = [kb.sb("mt", [128, 32, 512], BF16) for _ in range(2)]
        x1 = [kb.sb("x1", [128, DM], F32) for _ in range(2)]
        rr = [kb.sb("rr", [128, DM], F32) for _ in range(2)]
        x2 = [kb.sb("x2", [128, DM], F32) for _ in range(2)]
        stt_ = [kb.sb("stt", [128, 2, 6], F32) for _ in range(2)]
        mv = [kb.sb("mv", [128, 2], F32) for _ in range(2)]
        rs = [kb.sb("rs", [128, 2], F32) for _ in range(2)]
        po = [kb.ps("po", [128, 512]) for _ in range(4)]
        dst = D["y"] if last else D["X_d"]
        ui = 0
        ti = 0
        for g in range(ngroups):
            ntile = 4 if g < 8 else 2
            ntok = ntile * 128
            tok0 = g * 512
            m_ = mt[g % 2]
            for j0 in range(0, 32, 8):
                kb.dma("sp", m_[:, j0:j0 + 8, :ntok], m1v[:, j0:j0 + 8, tok0:tok0 + ntok])
            gg = g2["l"] if g < 8 else g2["c"]
            for t in range(ntile):
                i = ti % 2
                ti += 1
                rows = slice(tok0 + t * 128, tok0 + (t + 1) * 128)
                kb.dma("sp", x1[i][:], D["X1_d"][rows, :])
                for hh in range(2):
                    p = po[ui % 4]
                    ui += 1
                    cs_ = slice(hh * 512, (hh + 1) * 512)
                    for j in range(32):
                        kb.mm(p[:], m_[:, j, t * 128:(t + 1) * 128], w2[:, j, cs_], start=(j == 0), stop=(j == 31))
                    kb.tt("dve", rr[i][:, cs_], p[:], gg[:, cs_], ALU.mult)
                kb.stt("dve", rr[i][:], x1[i][:], ALPHA, rr[i][:], ALU.mult, ALU.add)
                layer_norm_tile(kb, rr[i], x2[i], stt_[i], mv[i], rs[i], lg, lb)
                kb.dma("pool", dst[rows, :], x2[i][:])


_CACHE = {}


def host_inputs(b, inp, shared):
    m = dict(shared)
    m["xin"] = np.ascontiguousarray(np.concatenate([inp["x"][b], inp["ctx"][b]], axis=0), dtype=np.float32)
    m["cvec"] = np.ascontiguousarray(np.stack([inp["c"][b], inp["c_ctx"]], axis=0), dtype=np.float32)
    return m


def shared_inputs(inp):
    f = lambda a: np.ascontiguousarray(a, dtype=np.float32)
    sh = {"consts": make_consts(), "rope": make_rope(), "Gp": make_rpb_gather(np.asarray(inp["rpb"], np.float32))}
    for k in ("w_ada", "b_ada", "w_in", "conv_w", "gdn_norm_w", "w_out", "ln1_g", "ln1_b", "w_mlp1", "w_mlp2", "ln2_g", "ln2_b"):
        sh[k] = f(inp[k])
    sh["a_log"] = f(inp["a_log"]).reshape(DEPTH, 8)
    sh["dt_bias"] = f(inp["dt_bias"]).reshape(DEPTH, 8)
    return sh


def kernel(**inputs):
    inp = {k: np.asarray(v) for k, v in inputs.items()}
    if "nc" not in _CACHE:
        _CACHE["nc"] = build()
    nc = _CACHE["nc"]
    shared = shared_inputs(inp)
    in_maps = [host_inputs(b, inp, shared) for b in range(8)]
    res = run_bass_kernel_spmd(nc, in_maps, core_ids=list(range(8)))
    return np.stack([np.asarray(r["y"], dtype=np.float32) for r in res.results], axis=0)
```

```python
import math
from contextlib import ExitStack, contextmanager
import numpy as np
import concourse.bass as bass
import concourse.mybir as mybir
from concourse.bass_utils import run_bass_kernel_spmd

F32 = mybir.dt.float32
BF16 = mybir.dt.bfloat16
AF = mybir.ActivationFunctionType
ALU = mybir.AluOpType

N_DMA_SEMS = 24
SAME_ENGINE_SYNC = {"pe": False, "act": True, "dve": True, "pool": True, "sp": True}

DEPTH = 4
T_LAT = 4096
T_CTX = 256
T_ALL = T_LAT + T_CTX
NT = T_ALL // 128
DM = 1024
IN_COLS = 3600
BIG = 30000.0
ALPHA = (2 * DEPTH) ** 0.25
LN_EPS = 1e-5
NORM_EPS = 1e-6


class Tk:
    __slots__ = ("w", "r")

    def __init__(self):
        self.w = None
        self.r = {}


class KB:
    def __init__(self, nc, st):
        self.nc = nc
        self.E = {"pe": nc.tensor, "act": nc.scalar, "dve": nc.vector, "pool": nc.gpsimd, "sp": nc.sync}
        self.sem = {}
        self.cnt = {}
        for e in self.E:
            self.sem[e] = st.enter_context(nc.semaphore("s_" + e))
            self.cnt[e] = 0
        for q in ("sp", "pool", "act"):
            for i in range(N_DMA_SEMS):
                k = "d%s%d" % (q, i)
                self.sem[k] = st.enter_context(nc.semaphore("s_" + k))
                self.cnt[k] = 0
        self.known = {e: {} for e in self.E}
        self.dma_rr = {"sp": 0, "pool": 0, "act": 0}
        self.n_ins = 0
        self.tk = {}
        self.psum = set()
        self.cur = None
        self.uid = 0

    @contextmanager
    def stage(self):
        prev = self.cur
        with ExitStack() as s:
            self.cur = s
            yield s
            self.barrier()
        self.cur = prev

    def sb(self, name, shape, dt):
        self.uid += 1
        nm = "%s_%d" % (name, self.uid)
        t = self.cur.enter_context(self.nc.sbuf_tensor(nm, list(shape), dt))
        self.tk[nm] = Tk()
        return t

    def ps(self, name, shape, dt=F32):
        self.uid += 1
        nm = "%s_%d" % (name, self.uid)
        t = self.cur.enter_context(self.nc.psum_tensor(nm, list(shape), dt))
        self.tk[nm] = Tk()
        self.psum.add(nm)
        return t

    def _split(self, ins, outs):
        reads, writes = [], []
        for a in ins:
            n = a.name
            if n in self.tk:
                (writes if n in self.psum else reads).append(self.tk[n])
        for a in outs:
            n = a.name
            if n in self.tk:
                writes.append(self.tk[n])
        return reads, writes

    def _deps(self, reads, writes):
        deps = {}
        for t in reads:
            if t.w is not None and t.w[1] > deps.get(t.w[0], 0):
                deps[t.w[0]] = t.w[1]
        for t in writes:
            if t.w is not None and t.w[1] > deps.get(t.w[0], 0):
                deps[t.w[0]] = t.w[1]
            for s, c in t.r.items():
                if c > deps.get(s, 0):
                    deps[s] = c
        return deps

    def _wait(self, e, deps):
        eng = self.E[e]
        kn = self.known[e]
        for s, c in deps.items():
            if kn.get(s, 0) >= c:
                continue
            if s == e and not SAME_ENGINE_SYNC[e]:
                continue
            eng.wait_ge(self.sem[s], c)
            kn[s] = c

    def _mark(self, tag, reads, writes):
        s, c = tag
        for t in reads:
            if t.r.get(s, 0) < c:
                t.r[s] = c
        for t in writes:
            t.w = tag
            t.r = {}

    def op(self, e, fn, ins=(), outs=()):
        reads, writes = self._split(ins, outs)
        self._wait(e, self._deps(reads, writes))
        ins_ = fn(self.E[e])
        self.cnt[e] += 1
        ins_.then_inc(self.sem[e], 1)
        self._mark((e, self.cnt[e]), reads, writes)
        self.n_ins += 1

    def dma(self, q, out, in_, **kw):
        reads, writes = self._split([in_], [out])
        k = "d%s%d" % (q, self.dma_rr[q])
        self.dma_rr[q] = (self.dma_rr[q] + 1) % N_DMA_SEMS
        deps = self._deps(reads, writes)
        if self.cnt[k] > deps.get(k, 0):
            deps[k] = self.cnt[k]
        self._wait(q, deps)
        ins_ = self.E[q].dma_start(out=out, in_=in_, **kw)
        self.cnt[k] += 16
        ins_.then_inc(self.sem[k], 16)
        self._mark((k, self.cnt[k]), reads, writes)
        self.n_ins += 1

    def barrier(self):
        for e in self.E:
            self._wait(e, {s: c for s, c in self.cnt.items() if c > 0 and s != e})

    def mm(self, out, lhsT, rhs, start=True, stop=True):
        self.op("pe", lambda e: e.matmul(out, lhsT=lhsT, rhs=rhs, start=start, stop=stop), [lhsT, rhs], [out])

    def tr(self, out, in_, ident):
        self.op("pe", lambda e: e.transpose(out, in_, ident), [in_, ident], [out])

    def act(self, out, in_, func, scale=None, bias=None):
        kw = {}
        ins = [in_]
        if scale is not None:
            kw["scale"] = scale
            if not isinstance(scale, (int, float)):
                ins.append(scale)
        if bias is not None:
            kw["bias"] = bias
            if not isinstance(bias, (int, float)):
                ins.append(bias)
        self.op("act", lambda e: e.activation(out=out, in_=in_, func=func, **kw), ins, [out])

    def copy(self, e, out, in_):
        if e == "act":
            self.act(out, in_, AF.Copy)
        else:
            self.op(e, lambda g: g.tensor_copy(out=out, in_=in_), [in_], [out])

    def tt(self, e, out, a, b, op):
        self.op(e, lambda g: g.tensor_tensor(out=out, in0=a, in1=b, op=op), [a, b], [out])

    def ts(self, e, out, a, s1, op0, s2=None, op1=None):
        ins = [a] + [s for s in (s1, s2) if s is not None and not isinstance(s, (int, float))]
        if op1 is None:
            self.op(e, lambda g: g.tensor_scalar(out=out, in0=a, scalar1=s1, scalar2=None, op0=op0), ins, [out])
        else:
            self.op(e, lambda g: g.tensor_scalar(out=out, in0=a, scalar1=s1, scalar2=s2, op0=op0, op1=op1), ins, [out])

    def stt(self, e, out, a, scalar, b, op0, op1):
        ins = [a, b] + ([] if isinstance(scalar, (int, float)) else [scalar])
        self.op(e, lambda g: g.scalar_tensor_tensor(out=out, in0=a, scalar=scalar, in1=b, op0=op0, op1=op1), ins, [out])

    def memset(self, e, out, val):
        self.op(e, lambda g: g.memset(out, val), [], [out])


CONST_LAYOUT = {}


def make_consts():
    P = np.arange(128)
    parts = []
    off = 0

    def add(name, arr):
        nonlocal off
        arr = np.asarray(arr, np.float32)
        CONST_LAYOUT[name] = (off, arr.shape[1])
        parts.append(arr)
        off += arr.shape[1]

    add("ident", np.eye(128))
    add("U1", P[:, None] <= P[None, :])
    add("U1T", P[:, None] >= P[None, :])
    add("SL", P[:, None] > P[None, :])
    add("SU", P[:, None] < P[None, :])
    add("ones", np.ones((128, 128)))
    negf = np.where(P[:, None] > P[None, :], 0.0, -BIG)
    negb = np.where(P[:, None] < P[None, :], 0.0, -BIG)
    negtf = np.where(P[None, :] >= P[:, None], 0.0, -BIG)
    negtb = np.where(P[None, :] <= P[:, None], 0.0, -BIG)
    add("NEGf", np.tile(negf, (1, 4)))
    add("NEGb", np.tile(negb, (1, 4)))
    add("NEGTf", np.tile(negtf, (1, 4)))
    add("NEGTb", np.tile(negtb, (1, 4)))
    pi = np.where((P % 64) < 32, P + 32, P - 32)
    rp = np.zeros((128, 128))
    rp[pi, P] = 1.0
    add("Rperm", rp)
    kc = P % 64
    par = P // 64
    qc = np.arange(64)
    cs = np.clip(qc - 8, 0, 48)
    colmask = np.where((kc[:, None] >= cs[None, :]) & (kc[:, None] < cs[None, :] + 16), 0.0, -BIG)
    add("colmask", colmask)
    m9 = np.zeros((128, 9, 64))
    m9[par == 1, 0, :] = -BIG
    m9[par == 0, 8, :] = -BIG
    add("M9", m9.reshape(128, 576))
    dg32 = (P[:, None] // 32 == P[None, :] // 32)
    off1 = (P[:, None] // 64 == P[None, :] // 64) & (P[:, None] // 32 > P[None, :] // 32)
    off2 = (P[:, None] // 64 > P[None, :] // 64)
    add("Dg32", dg32)
    add("Off1", off1)
    add("Off1T", off1.T)
    add("Off2", off2)
    add("Off2T", off2.T)
    return np.concatenate(parts, axis=1).astype(np.float32)


def make_rope():
    t = np.arange(T_LAT)
    row = (t // 64).astype(np.float32)
    col = (t % 64).astype(np.float32)
    inv = (10000.0 ** (-np.arange(0, 64, 2, dtype=np.float32) / 64)).astype(np.float32)
    i = np.arange(128)
    f = i % 32
    pos = np.where((i < 64)[:, None], row[None, :], col[None, :]).astype(np.float32)
    ang = (pos * inv[f][:, None]).astype(np.float32)
    sgn = np.where((i % 64) < 32, -1.0, 1.0)[:, None]
    out = np.stack([np.cos(ang), sgn * np.sin(ang)], axis=1)
    return out.astype(np.float32)


def make_rpb_gather(rpb):
    P = np.arange(128)
    kc = (P % 64)[:, None, None]
    par = (P // 64)[:, None, None]
    ep = np.arange(16)[None, :, None]
    qc = np.arange(64)[None, None, :]
    dr = 7 - ep + par + 0 * qc
    dc = kc - qc + 0 * ep
    valid = (np.abs(dr) <= 7) & (np.abs(dc) <= 15)
    dri = np.clip(dr + 7, 0, 14)
    dci = np.clip(dc + 15, 0, 30)
    g = rpb[:, :, dri, dci]
    g = np.where(valid[None, None], g, np.float32(0.0))
    return np.ascontiguousarray(g.reshape(rpb.shape[0], rpb.shape[1], 128, 1024).astype(np.float32))


def build(n_layers=DEPTH, dbg=(), stages=None):
    make_consts()
    nc = bass.Bass("TRN2", target_bir_lowering=False)
    D = {}

    def din(name, shape, dt=F32):
        D[name] = nc.dram_tensor(name, list(shape), dt, kind="ExternalInput").ap()

    def dscr(name, shape, dt):
        kind = "ExternalOutput" if name in dbg else "Internal"
        D[name] = nc.dram_tensor(name, list(shape), dt, kind=kind).ap()

    ncst = sum(v[1] for v in CONST_LAYOUT.values())
    din("xin", [T_ALL, DM]); din("cvec", [2, DM]); din("consts", [128, ncst]); din("rope", [128, 2, T_LAT])
    din("Gp", [DEPTH, 8, 128, 1024])
    din("w_ada", [DEPTH, DM, 6 * DM]); din("b_ada", [DEPTH, 6 * DM]); din("w_in", [DEPTH, DM, IN_COLS])
    din("conv_w", [DEPTH, 5, 1536]); din("a_log", [DEPTH, 8]); din("dt_bias", [DEPTH, 8]); din("gdn_norm_w", [DEPTH, 128])
    din("w_out", [DEPTH, DM, DM]); din("ln1_g", [DEPTH, DM]); din("ln1_b", [DEPTH, DM])
    din("w_mlp1", [DEPTH, DM, 4 * DM]); din("w_mlp2", [DEPTH, 4 * DM, DM]); din("ln2_g", [DEPTH, DM]); din("ln2_b", [DEPTH, DM])
    D["y"] = nc.dram_tensor("y", [T_LAT, DM], F32, kind="ExternalOutput").ap()

    dscr("winb", [DEPTH, DM, IN_COLS], BF16); dscr("woutb", [DEPTH, DM, DM], BF16)
    dscr("w1b", [DEPTH, DM, 4 * DM], BF16); dscr("w2b", [DEPTH, 4 * DM, DM], BF16)
    dscr("mod_d", [DEPTH, 2, 6 * DM], F32)
    dscr("FM_d", [3072, T_ALL], BF16)
    dscr("v_d", [T_ALL, 512], BF16); dscr("ba_d", [T_ALL, 16], F32)
    dscr("GQK_d", [1024, T_ALL], BF16)
    dscr("Ktok_d", [T_ALL, 512], BF16); dscr("Vtok_d", [T_ALL, 512], BF16)
    dscr("oT_d", [2, 512, T_ALL], F32)
    dscr("aT_d", [1024, T_ALL], BF16)
    dscr("X_d", [T_ALL, DM], F32); dscr("X1_d", [T_ALL, DM], F32); dscr("h2T_d", [DM, T_ALL], BF16)
    dscr("m1T_d", [4 * DM, T_ALL], BF16)

    with ExitStack() as st:
        kb = KB(nc, st)
        bg = (stages is None) or ("g2" in stages)
        if stages is None or "cast" in stages:
            items = [("w_in", "winb", 0, None)]
            if not bg:
                for l_ in range(n_layers):
                    items += layer_cast_items(l_, n_layers)
            cast_weights(kb, D, items)
        if stages is None or "mod" in stages:
            mod_stage(kb, D, n_layers)
        for l in range(n_layers):
            last = (l == DEPTH - 1)
            xsrc = D["xin"] if l == 0 else D["X_d"]
            if stages is None or "s1" in stages:
                stage1(kb, D, l, xsrc)
            if stages is None or "na" in stages:
                stage_na(kb, D, l, last)
            if stages is None or "g1" in stages:
                stage_g1(kb, D, l)
            if stages is None or "g2" in stages:
                stage_g2(kb, D, l, layer_cast_items(l, n_layers))
            if stages is None or "g3" in stages:
                stage_g3(kb, D, l, last)
            if stages is None or "s4a" in stages:
                stage4a(kb, D, l, xsrc, last)
            if stages is None or "s4b" in stages:
                stage4b(kb, D, l, last)
        kb.barrier()
    nc._n_ins = kb.n_ins
    return nc


def run_gens(gens):
    gens = list(gens)
    while gens:
        for g_ in list(gens):
            try:
                next(g_)
            except StopIteration:
                gens.remove(g_)


def cload(kb, D, name, dt=F32, eng="dve", rows=128):
    off, w = CONST_LAYOUT[name]
    t = kb.sb("c_" + name, [128, w], F32)
    kb.dma("sp", t[:], D["consts"][:, off:off + w])
    if dt == F32:
        return t
    tb = kb.sb("cb_" + name, [128, w], dt)
    kb.copy(eng, tb[:], t[:])
    return tb


def cast_jobs(D, items):
    jobs = []
    for src, dst, l, w in items:
        s2 = D[src][l]
        d2 = D[dst][l]
        if w:
            s2 = s2.rearrange("(a b) c -> a (b c)", b=w)
            d2 = d2.rearrange("(a b) c -> a (b c)", b=w)
        R, C = s2.shape
        for r0 in range(0, R, 128):
            for c0 in range(0, C, 2048):
                cw = min(2048, C - c0)
                jobs.append((s2[r0:r0 + 128, c0:c0 + cw], d2[r0:r0 + 128, c0:c0 + cw], cw))
    return jobs


def layer_cast_items(l, n_layers):
    it = [("w_out", "woutb", l, 2), ("w_mlp1", "w1b", l, None), ("w_mlp2", "w2b", l, 2)]
    if l + 1 < n_layers:
        it.append(("w_in", "winb", l + 1, None))
    return it


def cast_weights(kb, D, items):
    with kb.stage():
        NB = 3
        fb = [kb.sb("cf", [128, 2048], F32) for _ in range(NB)]
        bb = [kb.sb("cb", [128, 2048], BF16) for _ in range(NB)]
        for i, (s, d, cw) in enumerate(cast_jobs(D, items)):
            b = i % NB
            kb.dma("sp", fb[b][:, :cw], s)
            kb.copy("dve" if i % 2 == 0 else "act", bb[b][:, :cw], fb[b][:, :cw])
            kb.dma("pool", d, bb[b][:, :cw])


def cast_gen(kb, D, items, fb, bb):
    NB = len(fb)
    for i, (s, d, cw) in enumerate(cast_jobs(D, items)):
        b = i % NB
        kb.dma("sp", fb[b][:, :cw], s)
        yield
        kb.copy("act", bb[b][:, :cw], fb[b][:, :cw])
        kb.dma("act", d, bb[b][:, :cw])
        yield


def mod_stage(kb, D, n_layers):
    with kb.stage():
        cT = kb.sb("cT", [128, 2, 8], F32)
        kb.dma("sp", cT[:], D["cvec"].rearrange("r (k p) -> p r k", p=128), allow_slow_non_contiguous=True)
        scl = kb.sb("scl", [128, 8, 33], F32)
        kb.memset("dve", scl[:], 0.0)
        kb.act(scl[:, :, 0], cT[:, 0, :], AF.Silu)
        kb.act(scl[:, :, 32], cT[:, 1, :], AF.Silu)
        wt = [kb.sb("wada", [128, 8, 512], F32) for _ in range(2)]
        bt = [kb.sb("bada", [33, 512], F32) for _ in range(2)]
        mr = [kb.sb("mrow", [33, 512], F32) for _ in range(2)]
        pm = [kb.ps("pm", [128, 512]) for _ in range(2)]
        for b in bt:
            kb.memset("dve", b[:], 0.0)
        i = 0
        for l in range(n_layers):
            for cb in range(12):
                j = i % 2
                i += 1
                cols = slice(cb * 512, (cb + 1) * 512)
                kb.dma("sp", wt[j][:], D["w_ada"][l, :, cols].rearrange("(k p) c -> p k c", p=128))
                kb.dma("sp", bt[j][0:1, :], D["b_ada"][l:l + 1, cols])
                kb.dma("sp", bt[j][32:33, :], D["b_ada"][l:l + 1, cols])
                for kc in range(8):
                    kb.mm(pm[j][0:33, :], scl[:, kc, :], wt[j][:, kc, :], start=(kc == 0), stop=(kc == 7))
                kb.tt("dve", mr[j][:], pm[j][0:33, :], bt[j][:], ALU.add)
                if cb in (2, 3, 8, 9):
                    kb.ts("dve", mr[j][:], mr[j][:], 1.0, ALU.add)
                kb.dma("pool", D["mod_d"][l, 0:1, cols], mr[j][0:1, :])
                kb.dma("pool", D["mod_d"][l, 1:2, cols], mr[j][32:33, :])


def bvec(kb, name, src):
    n = src.shape[-1]
    t = kb.sb(name, [128, n], F32)
    kb.dma("sp", t[:], src.partition_broadcast(128))
    return t


def stage1(kb, D, l, xsrc):
    with kb.stage():
        w = kb.sb("win", [128, 8, IN_COLS], BF16)
        for kc in range(8):
            kb.dma("sp", w[:, kc, :], D["winb"][l, kc * 128:(kc + 1) * 128, :])
        mods = {}
        for r, nm in ((0, "l"), (1, "c")):
            mods[nm] = (bvec(kb, "sc1", D["mod_d"][l, r, 1024:2048]), bvec(kb, "sh1", D["mod_d"][l, r, 0:1024]))
        idb = cload(kb, D, "ident", BF16)
        xt = [kb.sb("xt", [128, DM], F32) for _ in range(2)]
        hb = [kb.sb("hb", [128, DM], BF16) for _ in range(2)]
        hT = [kb.sb("hT", [128, 8, 512], BF16) for _ in range(2)]
        fm = [kb.sb("fm", [128, 6, 512], BF16) for _ in range(4)]
        vt = [kb.sb("vt", [128, 512], BF16) for _ in range(2)]
        bat = [kb.sb("bat", [128, 16], F32) for _ in range(2)]
        pT = [kb.ps("pT", [128, 1024], BF16) for _ in range(2)]
        pF = [kb.ps("pF", [128, 512]) for _ in range(3)]
        pV = kb.ps("pV", [128, 512])
        pB = kb.ps("pB", [128, 512])
        FMv = D["FM_d"].rearrange("(b p) t -> p b t", p=128)
        ti = 0
        ei = 0
        fi = 0
        for g in range(9):
            ntile = 4 if g < 8 else 2
            ntok = ntile * 128
            tok0 = g * 512
            sc, sh = mods["l"] if g < 8 else mods["c"]
            hTg = hT[g % 2]
            for tt_ in range(ntile):
                j = ti % 2
                ti += 1
                rows = slice(tok0 + tt_ * 128, tok0 + (tt_ + 1) * 128)
                kb.dma("sp", xt[j][:], xsrc[rows, :])
                kb.tt("dve", xt[j][:], xt[j][:], sc[:], ALU.mult)
                kb.tt("dve", hb[j][:], xt[j][:], sh[:], ALU.add)
                for kc in range(8):
                    kb.tr(pT[j][:, kc * 128:(kc + 1) * 128], hb[j][:, kc * 128:(kc + 1) * 128], idb[:])
                kb.act(hTg[:, :, tt_ * 128:(tt_ + 1) * 128], pT[j][:].rearrange("p (k t) -> p k t", k=8), AF.Copy)
                for kc in range(8):
                    kb.mm(pV[:], hTg[:, kc, tt_ * 128:(tt_ + 1) * 128], w[:, kc, 1024:1536], start=(kc == 0), stop=(kc == 7))
                for kc in range(8):
                    kb.mm(pB[:, 0:16], hTg[:, kc, tt_ * 128:(tt_ + 1) * 128], w[:, kc, 3584:3600], start=(kc == 0), stop=(kc == 7))
                kb.copy("dve", vt[j][:], pV[:])
                kb.copy("dve", bat[j][:], pB[:, 0:16])
                kb.dma("pool", D["v_d"][rows, :], vt[j][:])
                kb.dma("pool", D["ba_d"][rows, :], bat[j][:])
            for blk in range(24):
                col0 = blk * 128 if blk < 8 else 1536 + (blk - 8) * 128
                p = pF[ei % 3]
                for kc in range(8):
                    kb.mm(p[:, :ntok], w[:, kc, col0:col0 + 128], hTg[:, kc, :ntok], start=(kc == 0), stop=(kc == 7))
                if blk % 6 == 0:
                    fcur = fm[fi % 4]
                    fi += 1
                if blk < 4:
                    kb.act(fcur[:, blk % 6, :ntok], p[:, :ntok], AF.Copy, scale=0.125)
                elif ei % 2 == 0:
                    kb.act(fcur[:, blk % 6, :ntok], p[:, :ntok], AF.Copy)
                else:
                    kb.copy("dve", fcur[:, blk % 6, :ntok], p[:, :ntok])
                ei += 1
                if blk % 6 == 5:
                    b0 = blk - 5
                    kb.dma("pool", FMv[:, b0:b0 + 6, tok0:tok0 + ntok], fcur[:, :, :ntok])


def na_pieces(c):
    out = []
    r0 = 8 * c
    for j in range(32):
        runs = []
        for r in range(r0, r0 + 8):
            rs = min(max(r - 4, 0), 56)
            if 2 * j + 1 < rs or 2 * j > rs + 7:
                continue
            edge = (r < 4) or (r > 60)
            kind = "G" if edge else "S"
            var = (7 - 2 * j + r) if edge else (r - 2 * j + 3)
            if runs and runs[-1][0] == kind and runs[-1][2] == r and runs[-1][1] + (r - runs[-1][3]) == var:
                runs[-1][2] = r + 1
            else:
                runs.append([kind, var, r + 1, r])
        for kind, var, rend, rstart in runs:
            out.append((j, (rstart - r0) * 64, (rend - rstart) * 64, (kind, var)))
    return out


def stage_na(kb, D, l, last):
    with kb.stage():
        colmask = cload(kb, D, "colmask")
        m9 = cload(kb, D, "M9")
        idf = cload(kb, D, "ident")
        vall = kb.sb("vall", [128, NT, 512], BF16)
        v3 = D["v_d"].rearrange("(t p) c -> p t c", p=128)
        for t0 in range(0, NT, 6):
            t1 = min(NT, t0 + 6)
            kb.dma("sp", vall[:, t0:t1, :], v3[:, t0:t1, :])
        vaug = [kb.sb("vaug", [128, NT, 128], BF16) for _ in range(2)]
        qk = [kb.sb("qk", [64, 2, T_ALL], BF16) for _ in range(2)]
        Gt = [kb.sb("Gt", [128, 1024], F32) for _ in range(2)]
        GM = [kb.sb("GM", [128, 1024], BF16) for _ in range(2)]
        S9 = [kb.sb("S9", [128, 576], BF16) for _ in range(2)]
        rcs = [kb.sb("rcs", [128, 512], F32) for _ in range(2)]
        for sid in range(2):
            kb.memset("pool", vaug[sid][:, :, 64:128], 1.0)
            kb.memset("dve", rcs[sid][:], 0.0)

        def mk2(name, shape, dt):
            return [[kb.sb(name, shape, dt) for _ in range(2)] for _ in range(2)]

        def mk3(name, shape, dt):
            return [[kb.sb(name, shape, dt) for _ in range(3)] for _ in range(2)]

        sc = mk3("sc", [128, 512], F32)
        pt = mk3("pt", [128, 512], BF16)
        osb = mk2("osb", [64, 512], F32)
        ob = mk2("ob", [64, 512], BF16)
        pS = [[kb.ps("pS", [128, 512]) for _ in range(3)] for _ in range(2)]
        pO = [[kb.ps("pO", [128, 512]) for _ in range(1)] for _ in range(2)]

        def headgen(h, sid):
            q_ = qk[sid]
            va = vaug[sid]
            kb.dma("sp", q_[:, 0, :], D["FM_d"][h * 64:(h + 1) * 64, :])
            kb.dma("sp", q_[:, 1, :], D["FM_d"][512 + h * 64:512 + (h + 1) * 64, :])
            kb.dma("sp", Gt[sid][:], D["Gp"][l, h])
            kb.copy("pool", va[:, :, 0:64], vall[:, :, h * 64:(h + 1) * 64])
            kb.tt("dve", GM[sid][:].rearrange("p (e q) -> p e q", e=16), Gt[sid][:].rearrange("p (e q) -> p e q", e=16),
                  colmask[:].unsqueeze(1).to_broadcast([128, 16, 64]), ALU.add)
            kb.tt("dve", S9[sid][:], GM[sid][:, 4 * 64:13 * 64], m9[:], ALU.add)
            yield
            pi = 0
            for c in range(9):
                if c == 8 and last:
                    continue
                ncols = 512 if c < 8 else 256
                q0 = c * 512
                pieces = [(32, 0, ncols, None), (33, 0, ncols, None)]
                if c < 8:
                    pieces += na_pieces(c)
                po = pO[sid][0]
                for k, (j, lo, n, bias) in enumerate(pieces):
                    p = pS[sid][pi % 3]
                    e = pt[sid][pi % 3]
                    s_ = sc[sid][pi % 3]
                    pi += 1
                    kb.mm(p[:, :n], q_[:, 1, j * 128:(j + 1) * 128], q_[:, 0, q0 + lo:q0 + lo + n], start=True, stop=True)
                    yield
                    if bias is not None:
                        src = S9[sid] if bias[0] == "S" else GM[sid]
                        kb.tt("dve", s_[:, :n], p[:, :n], src[:, bias[1] * 64:bias[1] * 64 + n], ALU.add)
                        yield
                        kb.act(e[:, :n], s_[:, :n], AF.Exp)
                    else:
                        kb.act(e[:, :n], p[:, :n], AF.Exp)
                    yield
                    kb.mm(po[:, lo:lo + n], va[:, j, :], e[:, :n], start=(k == 0), stop=(k == len(pieces) - 1))
                    yield
                ci = c % 2
                r_ = rcs[sid]
                os_ = osb[sid][ci]
                o_ = ob[sid][ci]
                pb = pS[sid][pi % 3]
                pi += 1
                kb.act(r_[64:128, :ncols], po[64:128, :ncols], AF.Ln)
                kb.act(r_[64:128, :ncols], r_[64:128, :ncols], AF.Exp, scale=-1.0)
                kb.copy("act", os_[:, :ncols], po[0:64, :ncols])
                yield
                kb.mm(pb[0:64, :ncols], idf[:, 64:128], r_[:, :ncols], start=True, stop=True)
                yield
                kb.tt("dve", o_[:, :ncols], os_[:, :ncols], pb[0:64, :ncols], ALU.mult)
                kb.dma("pool", D["aT_d"][h * 64:(h + 1) * 64, q0:q0 + ncols], o_[:, :ncols])
                yield

        for h0 in range(0, 8, 2):
            run_gens([headgen(h0, 0), headgen(h0 + 1, 1)])


def stage_g1(kb, D, l):
    with kb.stage():
        idf = cload(kb, D, "ident")
        idb = kb.sb("idb", [128, 128], BF16)
        kb.copy("dve", idb[:], idf[:])
        onesb = cload(kb, D, "ones", BF16)
        rpb_ = cload(kb, D, "Rperm", BF16)
        rope = kb.sb("rope", [128, 2, T_LAT], F32)
        kb.dma("sp", rope[:, 0, :], D["rope"][:, 0, :])
        kb.dma("sp", rope[:, 1, :], D["rope"][:, 1, :])
        cw = kb.sb("cw", [128, 5, 12], F32)
        for j in range(5):
            kb.dma("sp", cw[:, j, :], D["conv_w"][l, j].rearrange("(f p) -> p f", p=128), allow_slow_non_contiguous=True)
        XW = 4360
        xin = [kb.sb("gxin", [128, XW], BF16) for _ in range(2)]
        xsh = [kb.sb("gxsh", [128, XW], BF16) for _ in range(2)]
        for x_ in xin + xsh:
            kb.memset("dve", x_[:], 0.0)
        sall = [kb.sb("sall", [128, T_ALL], F32) for _ in range(2)]
        dg = [kb.sb("dg", [128, 5, 128], BF16) for _ in range(2)]

        def mk2(name, shape, dt):
            return [[kb.sb(name, shape, dt) for _ in range(2)] for _ in range(2)]

        sq = mk2("sq", [128, 512], BF16)
        lnb = mk2("lnb", [128, 512], F32)
        qn = mk2("qn", [128, 512], BF16)
        t1 = mk2("t1", [128, 512], F32)
        t2 = mk2("t2", [128, 512], F32)
        obq = mk2("obq", [128, 512], BF16)
        tk_ = mk2("tkst", [128, 4, 128], BF16)
        pc = [kb.ps("pc", [128, 512]) for _ in range(2)]
        pn = [kb.ps("pn", [128, 512]) for _ in range(2)]
        pr = [kb.ps("pr", [128, 512]) for _ in range(2)]
        pt = [kb.ps("ptr", [128, 8, 128], BF16) for _ in range(2)]
        blocks = [(b * 512, 512, 2 + b * 512) for b in range(8)] + [(T_LAT, 256, 4102)]

        def fcgen(fc, sid):
            x_, xs_, d_, sa = xin[sid], xsh[sid], dg[sid], sall[sid]
            src = D["FM_d"][1024 + fc * 128:1024 + (fc + 1) * 128, :]
            kb.dma("sp", x_[:, 2:2 + T_LAT], src[:, 0:T_LAT])
            kb.dma("sp", x_[:, 4102:4102 + T_CTX], src[:, T_LAT:T_ALL])
            kb.dma("sp", xs_[:, 1:1 + T_LAT], src[:, 0:T_LAT])
            kb.dma("sp", xs_[:, 4101:4101 + T_CTX], src[:, T_LAT:T_ALL])
            for j in range(5):
                kb.ts("dve", d_[:, j, :], idf[:], cw[:, j, fc:fc + 1], ALU.mult)
            yield
            for (tok0, n, xc) in blocks:
                p = pc[sid]
                for j in range(5):
                    if j % 2 == 0:
                        rhs_ = x_[:, xc + j - 2:xc + j - 2 + n]
                    else:
                        rhs_ = xs_[:, xc + j - 3:xc + j - 3 + n]
                    kb.mm(p[:, :n], d_[:, j, :], rhs_, start=(j == 0), stop=(j == 4))
                yield
                kb.act(sa[:, tok0:tok0 + n], p[:, :n], AF.Silu)
                yield
            isq = fc < 4
            isv = fc >= 8
            hh = fc % 4
            for bi, (tok0, n, xc) in enumerate(blocks):
                i = bi % 2
                o_ = obq[sid][i]
                if isv:
                    kb.copy("dve", o_[:, :n], sa[:, tok0:tok0 + n])
                    yield
                else:
                    kb.tt("pool", sq[sid][i][:, :n], sa[:, tok0:tok0 + n], sa[:, tok0:tok0 + n], ALU.mult)
                    yield
                    kb.mm(pn[sid][:, :n], onesb[:], sq[sid][i][:, :n])
                    yield
                    kb.act(lnb[sid][i][:, :n], pn[sid][:, :n], AF.Ln, bias=NORM_EPS)
                    kb.act(lnb[sid][i][:, :n], lnb[sid][i][:, :n], AF.Exp, scale=-0.5, bias=(-0.5 * math.log(128.0) if isq else 0.0))
                    yield
                    if tok0 < T_LAT:
                        kb.tt("dve", qn[sid][i][:, :n], sa[:, tok0:tok0 + n], lnb[sid][i][:, :n], ALU.mult)
                        yield
                        kb.mm(pr[sid][:, :n], rpb_[:], qn[sid][i][:, :n])
                        kb.tt("pool", t1[sid][i][:, :n], qn[sid][i][:, :n], rope[:, 0, tok0:tok0 + n], ALU.mult)
                        yield
                        kb.tt("dve", t2[sid][i][:, :n], pr[sid][:, :n], rope[:, 1, tok0:tok0 + n], ALU.mult)
                        yield
                        kb.tt("dve", o_[:, :n], t1[sid][i][:, :n], t2[sid][i][:, :n], ALU.add)
                        yield
                    else:
                        kb.tt("dve", o_[:, :n], sa[:, tok0:tok0 + n], lnb[sid][i][:, :n], ALU.mult)
                        yield
                    kb.dma("pool", D["GQK_d"][fc * 128:(fc + 1) * 128, tok0:tok0 + n], o_[:, :n])
                if fc >= 4:
                    nt_ = n // 128
                    for a in range(nt_):
                        kb.tr(pt[sid][:, a, :], o_[:, a * 128:(a + 1) * 128], idb[:])
                    yield
                    kb.copy("act", tk_[sid][i][:, :nt_, :], pt[sid][:, :nt_, :])
                    dst = D["Vtok_d"] if isv else D["Ktok_d"]
                    kb.dma("pool", dst[tok0:tok0 + n, hh * 128:(hh + 1) * 128].rearrange("(a p) c -> p a c", p=128), tk_[sid][i][:, :nt_, :])
                    yield

        for fc0 in range(0, 12, 2):
            run_gens([fcgen(fc0, 0), fcgen(fc0 + 1, 1)])


def stage_g2(kb, D, l, cast_items=()):
    with kb.stage():
        cfb = [kb.sb("cf", [128, 2048], F32) for _ in range(3)]
        cbb = [kb.sb("cb", [128, 2048], BF16) for _ in range(3)]
        castg = cast_gen(kb, D, cast_items, cfb, cbb)
        idf = cload(kb, D, "ident")
        idb = kb.sb("idb", [128, 128], BF16)
        kb.copy("dve", idb[:], idf[:])
        onesf = cload(kb, D, "ones")
        Ua = [cload(kb, D, "U1"), cload(kb, D, "U1T")]
        Sa = [cload(kb, D, "SL"), cload(kb, D, "SU")]
        NEGs = [cload(kb, D, "NEGf", BF16), cload(kb, D, "NEGb", BF16)]
        NEGT = [cload(kb, D, "NEGTf", BF16), cload(kb, D, "NEGTb", BF16)]
        ba = kb.sb("ba", [128, NT, 16], F32)
        kb.dma("sp", ba[:], D["ba_d"].rearrange("(t p) c -> p t c", p=128))
        al = bvec(kb, "alog", D["a_log"][l])
        dtb = bvec(kb, "dtb", D["dt_bias"][l])
        nA = kb.sb("nA", [128, 8], F32)
        kb.act(nA[:], al[:], AF.Exp)
        kb.ts("dve", nA[:], nA[:], -1.0, ALU.mult)
        gx = kb.sb("gx", [128, NT, 8], F32)
        g_all = kb.sb("g_all", [128, NT, 8], F32)
        beta = kb.sb("beta", [128, NT, 8], F32)
        negb = kb.sb("negb", [128, NT, 8], F32)
        kb.tt("dve", gx[:], ba[:, :, 8:16], dtb[:].unsqueeze(1).to_broadcast([128, NT, 8]), ALU.add)
        kb.act(gx[:], gx[:], AF.Exp)
        kb.act(gx[:], gx[:], AF.Ln, bias=1.0)
        kb.tt("dve", g_all[:], gx[:], nA[:].unsqueeze(1).to_broadcast([128, NT, 8]), ALU.mult)
        kb.act(beta[:], ba[:, :, 0:8], AF.Sigmoid)
        kb.ts("dve", negb[:], beta[:], -1.0, ALU.mult)

        def mk(name, shape, dt, n=2):
            return [[kb.sb(name, shape, dt) for _ in range(n)] for _ in range(2)]

        QK = mk("QK", [128, 8, 128], BF16)
        KV = mk("KV", [128, 2, 512], BF16)
        gS = mk("gS", [128, 4, 128], F32, 1)
        gU = mk("gU", [128, 4, 128], F32, 1)
        Ds = mk("Ds", [128, 4, 128], F32, 1)
        DT = mk("DT", [128, 4, 128], F32, 1)
        Er = mk("Er", [128, 4, 128], F32, 1)
        E3 = mk("E3", [128, 3, 4], F32)
        be = mk("be", [128, 4], F32, 1)
        negA = mk("negA", [128, 4, 128], F32, 1)
        MP = mk("MP", [128, 4, 2, 128], BF16)
        Rm = mk("Rm", [128, 4, 128], BF16)
        Tm = mk("Tm", [128, 4, 128], BF16)
        N1 = mk("N1", [128, 4, 128], BF16, 1)
        N1T = mk("N1T", [128, 4, 128], BF16, 1)
        N2 = mk("N2", [128, 4, 128], BF16, 1)
        Yb = mk("Yb", [128, 4, 128], BF16, 1)
        Xb = mk("Xb", [128, 4, 128], BF16, 1)
        X2b = mk("X2b", [128, 4, 128], BF16, 1)
        Rb = mk("Rb", [128, 4, 128], BF16, 1)
        Dg = cload(kb, D, "Dg32")
        O1 = [cload(kb, D, "Off1"), cload(kb, D, "Off1T")]
        O1T = [O1[1], O1[0]]
        O2 = [cload(kb, D, "Off2"), cload(kb, D, "Off2T")]
        attnT = mk("attnT", [128, 4, 128], BF16)
        QdT = mk("QdT", [128, 4, 128], BF16)
        Kd = mk("Kd", [128, 4, 128], BF16)
        Kbe = mk("Kbe", [128, 4, 128], BF16, 1)
        bV = mk("bV", [128, 4, 128], BF16, 1)
        U = mk("U", [128, 4, 128], F32)
        WT = mk("WT", [128, 4, 128], BF16)
        vnew = mk("vnew", [128, 4, 128], BF16, 1)
        osb = mk("osb", [128, 4, 128], F32)
        S = [kb.sb("S", [128, 4, 128], F32) for _ in range(2)]
        Sb = [kb.sb("Sb", [128, 4, 128], BF16) for _ in range(2)]
        for d in range(2):
            kb.memset("dve", S[d][:], 0.0)
            kb.memset("pool", Sb[d][:], 0.0)
        pY = [kb.ps("pY", [128, 4, 2, 128]) for _ in range(2)]
        pZ = [kb.ps("pZ", [128, 4, 128]) for _ in range(2)]
        pW = kb.ps("pW", [128, 4, 128])
        pOo = kb.ps("pOo", [128, 4, 128])
        GQ = D["GQK_d"].rearrange("(a p) t -> p a t", p=128)
        OT = [D["oT_d"][d].rearrange("(h v) t -> v h t", v=128) for d in range(2)]

        order = [[32, 33] + list(range(32)), [33, 32] + list(range(31, -1, -1))]
        B4 = [128, 4, 128]
        f2 = lambda a: a[:].rearrange("p h t -> p (h t)")

        def pre(s, d):
            t = order[d][s]
            b = s % 2
            tok = slice(t * 128, (t + 1) * 128)
            qk = QK[d][b]
            kv = KV[d][b]
            PY = pY[d]
            PZ = pZ[d]
            PYf = PY[:].rearrange("p h w t -> p (h w t)")
            A0f = PYf[:, 0:512]
            A1f = PYf[:, 512:1024]
            A0 = A0f.rearrange("p (h t) -> p h t", h=4)
            A1 = A1f.rearrange("p (h t) -> p h t", h=4)
            kb.dma("sp", qk[:], GQ[:, :, tok])
            kb.dma("sp", kv[:, 0, :], D["Ktok_d"][tok, :])
            kb.dma("sp", kv[:, 1, :], D["Vtok_d"][tok, :])
            gcol = g_all[:, t, d * 4:(d + 1) * 4]
            bcol = beta[:, t, d * 4:(d + 1) * 4]
            nbcol = negb[:, t, d * 4:(d + 1) * 4]
            kb.tt("dve", gS[d][0][:], gcol.unsqueeze(2).to_broadcast(B4), Sa[d][:].unsqueeze(1).to_broadcast(B4), ALU.mult)
            kb.tt("pool", gU[d][0][:], gcol.unsqueeze(2).to_broadcast(B4), Ua[d][:].unsqueeze(1).to_broadcast(B4), ALU.mult)
            yield
            kb.mm(A0f, Ua[d][:], f2(gS[d][0]), start=True, stop=False)
            kb.mm(A0f, idb[:], NEGs[d][:], start=False, stop=True)
            kb.mm(A1f, Sa[d][:], f2(gU[d][0]), start=True, stop=False)
            kb.mm(A1f, idb[:], NEGT[d][:], start=False, stop=True)
            kb.mm(f2(PZ), onesf[:], f2(gU[d][0]), start=True, stop=True)
            yield
            kb.act(f2(Ds[d][0]), A0f, AF.Exp)
            kb.act(f2(DT[d][0]), A1f, AF.Exp)
            kb.act(Er[d][0][:], PZ[:], AF.Exp)
            yield
            kb.mm(PZ[:, 0, 0:4], Ua[d][:], gcol, start=True, stop=True)
            kb.mm(PZ[:, 0, 4:8], Sa[d][:], gcol, start=True, stop=True)
            kb.mm(PZ[:, 0, 8:12], onesf[:], gcol, start=True, stop=True)
            e3 = E3[d][b]
            kb.act(e3[:].rearrange("p a h -> p (a h)"), PZ[:, 0, 0:12], AF.Exp)
            kb.tt("dve", be[d][0][:], bcol, e3[:, 0, :], ALU.mult)
            yield
            for h in range(4):
                kb.mm(A0[:, h, :], qk[:, 4 + h, :], qk[:, 4 + h, :], start=True, stop=True)
            for h in range(4):
                kb.mm(A1[:, h, :], qk[:, 4 + h, :], qk[:, h, :], start=True, stop=True)
            yield
            na_ = negA[d][0]
            for h in range(4):
                kb.stt("dve", na_[:, h, :], A0[:, h, :], nbcol[:, h:h + 1], Ds[d][0][:, h, :], ALU.mult, ALU.mult)
            kb.tt("dve", attnT[d][b][:], A1, DT[d][0][:], ALU.mult)
            yield
            bc = lambda m: m[:].unsqueeze(1).to_broadcast(B4)
            mp = MP[d][0]
            for h in range(4):
                kb.tr(PZ[:, h, :], na_[:, h, :], idf[:])
            kb.tt("pool", mp[:, :, 1, :], na_[:], bc(Dg), ALU.mult)
            kb.tt("pool", N1[d][0][:], na_[:], bc(O1[d]), ALU.mult)
            kb.tt("pool", N2[d][0][:], na_[:], bc(O2[d]), ALU.mult)
            yield
            kb.tt("dve", mp[:, :, 0, :], PZ[:], bc(Dg), ALU.mult)
            kb.tt("dve", N1T[d][0][:], PZ[:], bc(O1T[d]), ALU.mult)
            yield
            kb.tt("pool", Tm[d][0][:], mp[:, :, 1, :], bc(idf), ALU.add)
            kb.tt("pool", Rm[d][0][:], mp[:, :, 0, :], bc(idf), ALU.add)
            yield
            cur = 0
            for it in range(1, 5):
                mpn = MP[d][1 - cur]
                mpc = MP[d][cur]
                for h in range(4):
                    kb.mm(PY[:, h, 0, :], mpc[:, h, 1, :], mpc[:, h, 0, :], start=True, stop=True)
                    kb.mm(PY[:, h, 1, :], mpc[:, h, 0, :], mpc[:, h, 1, :], start=True, stop=True)
                yield
                kb.copy("act", mpn[:, 0:2], PY[:, 0:2])
                kb.copy("act", mpn[:, 2:4], PY[:, 2:4])
                yield
                rn, rc_ = Rm[d][1 - cur], Rm[d][cur]
                tn, tc_ = Tm[d][1 - cur], Tm[d][cur]
                for h in range(4):
                    kb.mm(PZ[:, h, :], mpn[:, h, 1, :], rc_[:, h, :], start=True, stop=True)
                for h in range(4):
                    kb.mm(PY[:, h, 0, :], mpn[:, h, 0, :], tc_[:, h, :], start=True, stop=True)
                yield
                kb.tt("dve", rn[:], PZ[:], rc_[:], ALU.add)
                kb.tt("dve", tn[:], PY[:, :, 0, :], tc_[:], ALU.add)
                yield
                cur = 1 - cur
            R0_, T0_ = Rm[d][cur], Tm[d][cur]
            R1_, T1_ = Rm[d][1 - cur], Tm[d][1 - cur]
            for h in range(4):
                kb.mm(PZ[:, h, :], N1T[d][0][:, h, :], T0_[:, h, :], start=True, stop=True)
            for h in range(4):
                kb.mm(PY[:, h, 0, :], N1[d][0][:, h, :], R0_[:, h, :], start=True, stop=True)
            yield
            kb.copy("act", Yb[d][0][:], PZ[:])
            kb.copy("act", Xb[d][0][:], PY[:, :, 0, :])
            yield
            for h in range(4):
                kb.mm(PZ[:, h, :], R0_[:, h, :], Yb[d][0][:, h, :], start=True, stop=True)
            for h in range(4):
                kb.mm(PY[:, h, 0, :], T0_[:, h, :], Xb[d][0][:, h, :], start=True, stop=True)
            yield
            kb.tt("dve", T1_[:], PZ[:], T0_[:], ALU.add)
            kb.tt("dve", R1_[:], PY[:, :, 0, :], R0_[:], ALU.add)
            yield
            for h in range(4):
                kb.mm(PZ[:, h, :], N2[d][0][:, h, :], R1_[:, h, :], start=True, stop=True)
            yield
            kb.copy("act", X2b[d][0][:], PZ[:])
            yield
            for h in range(4):
                kb.mm(PZ[:, h, :], T1_[:, h, :], X2b[d][0][:, h, :], start=True, stop=True)
            yield
            r = Rb[d][0]
            kb.tt("dve", r[:], PZ[:], R1_[:], ALU.add)
            kb.tt("pool", bV[d][0][:], kv[:, 1, :].rearrange("p (h v) -> p h v", h=4), bcol.unsqueeze(2).to_broadcast(B4), ALU.mult)
            kb.tt("pool", Kbe[d][0][:], kv[:, 0, :].rearrange("p (h v) -> p h v", h=4), be[d][0][:].unsqueeze(2).to_broadcast(B4), ALU.mult)
            kb.tt("pool", Kd[d][b][:], kv[:, 0, :].rearrange("p (h v) -> p h v", h=4), e3[:, 1, :].unsqueeze(2).to_broadcast(B4), ALU.mult)
            kb.tt("dve", QdT[d][b][:], qk[:, 0:4, :], Er[d][0][:], ALU.mult)
            yield
            for h in range(4):
                kb.mm(PZ[:, h, :], r[:, h, :], bV[d][0][:, h, :], start=True, stop=True)
            for h in range(4):
                kb.mm(A0[:, h, :], Kbe[d][0][:, h, :], r[:, h, :], start=True, stop=True)
            yield
            kb.copy("act", U[d][b][:], PZ[:])
            kb.copy("act", WT[d][b][:], A0)
            yield

        def scan(s, d):
            t = order[d][s]
            b = s % 2
            tok = slice(t * 128, (t + 1) * 128)
            for h in range(4):
                kb.mm(pW[:, h, :], WT[d][b][:, h, :], Sb[d][:, h, :], start=True, stop=True)
            yield
            kb.tt("dve", vnew[d][0][:], U[d][b][:], pW[:], ALU.subtract)
            yield
            for h in range(4):
                kb.mm(pOo[:, h, :], Sb[d][:, h, :], QdT[d][b][:, h, :], start=True, stop=False)
                kb.mm(pOo[:, h, :], vnew[d][0][:, h, :], attnT[d][b][:, h, :], start=False, stop=True)
            for h in range(4):
                kb.mm(pW[:, h, :], Kd[d][b][:, h, :], vnew[d][0][:, h, :], start=True, stop=True)
            yield
            kb.copy("act", osb[d][b][:], pOo[:])
            kb.dma("pool", OT[d][:, :, tok], osb[d][b][:])
            for h in range(4):
                kb.stt("dve", S[d][:, h, :], S[d][:, h, :], E3[d][b][:, 2, h:h + 1], pW[:, h, :], ALU.mult, ALU.add)
            yield
            kb.copy("act", Sb[d][:], S[d][:])
            yield

        def run(gens):
            gens = list(gens)
            while gens:
                for g_ in list(gens):
                    try:
                        next(g_)
                    except StopIteration:
                        gens.remove(g_)

        run([pre(0, 0), pre(0, 1)])
        for s in range(NT):
            def scans(s_):
                yield from scan(s_, 0)
                yield from scan(s_, 1)
            def limited(g_, n_):
                for _ in range(n_):
                    try:
                        next(g_)
                    except StopIteration:
                        return
                    yield
            gl = [scans(s), limited(castg, 6)]
            if s + 1 < NT:
                gl += [pre(s + 1, 0), pre(s + 1, 1)]
            run(gl)
        for _ in castg:
            pass


def stage_g3(kb, D, l, last):
    with kb.stage():
        meanb = kb.sb("meanb", [128, 128], BF16)
        kb.memset("dve", meanb[:], 1.0 / 128.0)
        gw = kb.sb("gw", [128, 1], F32)
        kb.dma("sp", gw[:], D["gdn_norm_w"][l].rearrange("(p o) -> p o", o=1))
        ntok_all = T_LAT if last else T_ALL
        zt = [kb.sb("zt", [128, T_ALL], BF16) for _ in range(2)]
        sz = [kb.sb("sz", [128, T_ALL], F32) for _ in range(2)]

        def mk2(name, shape, dt):
            return [[kb.sb(name, shape, dt) for _ in range(2)] for _ in range(2)]

        of = mk2("of", [128, 512], F32)
        obk = mk2("obk", [128, 512], F32)
        sq = mk2("sq", [128, 512], BF16)
        rs = mk2("rs", [128, 512], F32)
        yo = mk2("yo", [128, 512], BF16)
        pn = [kb.ps("pn", [128, 512]) for _ in range(2)]
        blocks = [(b * 512, 512) for b in range(8)] + ([] if last else [(T_LAT, 256)])

        def headgen(h, sid):
            kb.dma("sp", zt[sid][:, :ntok_all], D["FM_d"][2560 + h * 128:2560 + (h + 1) * 128, 0:ntok_all])
            yield
            for c0 in range(0, ntok_all, 1088):
                c1 = min(ntok_all, c0 + 1088)
                kb.act(sz[sid][:, c0:c1], zt[sid][:, c0:c1], AF.Silu)
                yield
            for bi, (tok0, n) in enumerate(blocks):
                j = bi % 2
                o_, ob_, sq_, rs_, yo_ = of[sid][j], obk[sid][j], sq[sid][j], rs[sid][j], yo[sid][j]
                kb.dma("sp", o_[:, :n], D["oT_d"][0, h * 128:(h + 1) * 128, tok0:tok0 + n])
                kb.dma("sp", ob_[:, :n], D["oT_d"][1, h * 128:(h + 1) * 128, tok0:tok0 + n])
                yield
                kb.tt("pool", o_[:, :n], o_[:, :n], ob_[:, :n], ALU.add)
                yield
                kb.tt("pool", sq_[:, :n], o_[:, :n], o_[:, :n], ALU.mult)
                yield
                kb.mm(pn[sid][:, :n], meanb[:], sq_[:, :n])
                yield
                kb.act(rs_[:, :n], pn[sid][:, :n], AF.Ln, bias=NORM_EPS)
                kb.act(rs_[:, :n], rs_[:, :n], AF.Exp, scale=-0.5)
                yield
                kb.tt("dve", o_[:, :n], o_[:, :n], rs_[:, :n], ALU.mult)
                yield
                kb.stt("dve", yo_[:, :n], o_[:, :n], gw[:, 0:1], sz[sid][:, tok0:tok0 + n], ALU.mult, ALU.mult)
                kb.dma("pool", D["aT_d"][512 + h * 128:512 + (h + 1) * 128, tok0:tok0 + n], yo_[:, :n])
                yield

        for h0 in range(0, 4, 2):
            run_gens([headgen(h0, 0), headgen(h0 + 1, 1)])


def layer_norm_gen(kb, r, out, stt_, mv, rs, gam, bet):
    for hh in range(2):
        kb.op("dve", lambda g: g.bn_stats(out=stt_[:, hh, :], in_=r[:, hh * 512:(hh + 1) * 512]), [r[:]], [stt_[:]])
    kb.op("dve", lambda g: g.bn_aggr(out=mv[:], in_=stt_[:]), [stt_[:]], [mv[:]])
    yield
    kb.act(rs[:, 0:1], mv[:, 1:2], AF.Ln, bias=LN_EPS)
    kb.act(rs[:, 0:1], rs[:, 0:1], AF.Exp, scale=-0.5)
    yield
    kb.stt("dve", rs[:, 1:2], mv[:, 0:1], -1.0, rs[:, 0:1], ALU.mult, ALU.mult)
    yield
    kb.act(out[:], r[:], AF.Identity, scale=rs[:, 0:1], bias=rs[:, 1:2])
    yield
    kb.tt("pool", out[:], out[:], gam[:], ALU.mult)
    kb.tt("pool", out[:], out[:], bet[:], ALU.add)
    yield


def layer_norm_tile(kb, r, out, stt_, mv, rs, gam, bet):
    for _ in layer_norm_gen(kb, r, out, stt_, mv, rs, gam, bet):
        pass


def stage4a(kb, D, l, xsrc, last):
    with kb.stage():
        wo = kb.sb("wo", [128, 8, DM], BF16)
        for kc in range(8):
            kb.dma("sp", wo[:, kc, :], D["woutb"][l, kc * 128:(kc + 1) * 128, :])
        idb = cload(kb, D, "ident", BF16)
        lg = bvec(kb, "ln1g", D["ln1_g"][l])
        lb = bvec(kb, "ln1b", D["ln1_b"][l])
        mods = {}
        for r_, nm in ((0, "l"), (1, "c")):
            if nm == "c" and last:
                continue
            mods[nm] = (bvec(kb, "g1", D["mod_d"][l, r_, 2048:3072]), bvec(kb, "sc2", D["mod_d"][l, r_, 4096:5120]),
                        bvec(kb, "sh2", D["mod_d"][l, r_, 3072:4096]))
        aT = [kb.sb("aT", [128, 8, 512], BF16) for _ in range(2)]
        NS = 4
        xt = [kb.sb("xt", [128, DM], F32) for _ in range(NS)]
        rr = [kb.sb("rr", [128, DM], F32) for _ in range(NS)]
        x1 = [kb.sb("x1", [128, DM], F32) for _ in range(NS)]
        hb = [kb.sb("hb", [128, DM], BF16) for _ in range(NS)]
        hT = [kb.sb("hT", [128, 8, 512], BF16) for _ in range(2)]
        stt_ = [kb.sb("stt", [128, 2, 6], F32) for _ in range(NS)]
        mv = [kb.sb("mv", [128, 2], F32) for _ in range(NS)]
        rs = [kb.sb("rs", [128, 2], F32) for _ in range(NS)]
        py = [kb.ps("py", [128, 512]) for _ in range(NS)]
        pT = [kb.ps("pT", [128, 1024], BF16) for _ in range(NS)]
        aTv = D["aT_d"].rearrange("(k p) t -> p k t", p=128)
        hTv = D["h2T_d"].rearrange("(k p) t -> p k t", p=128)
        for g in range(9):
            if g == 8 and last:
                continue
            ntile = 4 if g < 8 else 2
            ntok = ntile * 128
            tok0 = g * 512
            g1, sc2, sh2 = mods["l"] if g < 8 else mods["c"]
            a_ = aT[g % 2]
            hTg = hT[g % 2]
            kb.dma("sp", a_[:, :, :ntok], aTv[:, :, tok0:tok0 + ntok])

            def tilegen(tt_, j):
                rows = slice(tok0 + tt_ * 128, tok0 + (tt_ + 1) * 128)
                kb.dma("sp", xt[j][:], xsrc[rows, :])
                for hh in range(2):
                    cs_ = slice(hh * 512, (hh + 1) * 512)
                    for kc in range(8):
                        kb.mm(py[j][:], a_[:, kc, tt_ * 128:(tt_ + 1) * 128], wo[:, kc, cs_], start=(kc == 0), stop=(kc == 7))
                    yield
                    kb.tt("dve", rr[j][:, cs_], py[j][:], g1[:, cs_], ALU.mult)
                    yield
                kb.stt("dve", rr[j][:], xt[j][:], ALPHA, rr[j][:], ALU.mult, ALU.add)
                yield
                yield from layer_norm_gen(kb, rr[j], x1[j], stt_[j], mv[j], rs[j], lg, lb)
                kb.dma("pool", D["X1_d"][rows, :], x1[j][:])
                kb.tt("dve", rr[j][:], x1[j][:], sc2[:], ALU.mult)
                kb.tt("dve", hb[j][:], rr[j][:], sh2[:], ALU.add)
                yield
                for kc in range(8):
                    kb.tr(pT[j][:, kc * 128:(kc + 1) * 128], hb[j][:, kc * 128:(kc + 1) * 128], idb[:])
                yield
                kb.act(hTg[:, :, tt_ * 128:(tt_ + 1) * 128], pT[j][:].rearrange("p (k t) -> p k t", k=8), AF.Copy)
                yield

            run_gens([tilegen(t_, t_) for t_ in range(ntile)])
            kb.dma("pool", hTv[:, :, tok0:tok0 + ntok], hTg[:, :, :ntok])


def stage4b(kb, D, l, last):
    ngroups = 8 if last else 9
    m1v = D["m1T_d"].rearrange("(j p) t -> p j t", p=128)
    hTv = D["h2T_d"].rearrange("(k p) t -> p k t", p=128)
    with kb.stage():
        w1 = kb.sb("w1", [128, 8, 4 * DM], BF16)
        for kc in range(8):
            kb.dma("sp", w1[:, kc, :], D["w1b"][l, kc * 128:(kc + 1) * 128, :])
        hT = [kb.sb("hT", [128, 8, 512], BF16) for _ in range(2)]
        rl = [kb.sb("rl", [128, 512], F32) for _ in range(4)]
        mo = [kb.sb("mo", [128, 8, 512], BF16) for _ in range(3)]
        pm = [kb.ps("pm", [128, 512]) for _ in range(4)]
        mi = 0
        oi = 0
        for g in range(ngroups):
            ntok = 512 if g < 8 else 256
            tok0 = g * 512
            h_ = hT[g % 2]
            kb.dma("sp", h_[:, :, :ntok], hTv[:, :, tok0:tok0 + ntok])
            for j in range(32):
                p = pm[mi % 4]
                r_ = rl[mi % 4]
                if j % 8 == 0:
                    m_ = mo[oi % 3]
                    oi += 1
                for kc in range(8):
                    kb.mm(p[:, :ntok], w1[:, kc, j * 128:(j + 1) * 128], h_[:, kc, :ntok], start=(kc == 0), stop=(kc == 7))
                kb.act(r_[:, :ntok], p[:, :ntok], AF.Relu)
                kb.tt("dve" if mi % 2 == 0 else "pool", m_[:, j % 8, :ntok], r_[:, :ntok], r_[:, :ntok], ALU.mult)
                mi += 1
                if j % 8 == 7:
                    kb.dma("pool", m1v[:, j - 7:j + 1, tok0:tok0 + ntok], m_[:, :, :ntok])
    with kb.stage():
        w2 = kb.sb("w2", [128, 32, DM], BF16)
        w2v = D["w2b"][l].rearrange("(j p) c -> p j c", p=128)
        for j0 in range(0, 32, 8):
            kb.dma("sp", w2[:, j0:j0 + 8, :], w2v[:, j0:j0 + 8, :])
        lg = bvec(kb, "ln2g", D["ln2_g"][l])
        lb = bvec(kb, "ln2b", D["ln2_b"][l])
        g2 = {"l": bvec(kb, "g2", D["mod_d"][l, 0, 5120:6144])}
        if not last:
            g2["c"] = bvec(kb, "g2c", D["mod_d"][l, 1, 5120:6144])
        mt = [kb.sb("mt", [128, 32, 512], BF16) for _ in range(2)]
        x1 = [kb.sb("x1", [128, DM], F32) for _ in range(2)]
        rr = [kb.sb("rr", [128, DM], F32) for _ in range(2)]
        x2 = [kb.sb("x2", [128, DM], F32) for _ in range(2)]
        stt_ = [kb.sb("stt", [128, 2, 6], F32) for _ in range(2)]
        mv = [kb.sb("mv", [128, 2], F32) for _ in range(2)]
        rs = [kb.sb("rs", [128, 2], F32) for _ in range(2)]
        po = [kb.ps("po", [128, 512]) for _ in range(4)]
        dst = D["y"] if last else D["X_d"]
        ui = 0
        ti = 0
        for g in range(ngroups):
            ntile = 4 if g < 8 else 2
            ntok = ntile * 128
            tok0 = g * 512
            m_ = mt[g % 2]
            for j0 in range(0, 32, 8):
                kb.dma("sp", m_[:, j0:j0 + 8, :ntok], m1v[:, j0:j0 + 8, tok0:tok0 + ntok])
            gg = g2["l"] if g < 8 else g2["c"]
            for t in range(ntile):
                i = ti % 2
                ti += 1
                rows = slice(tok0 + t * 128, tok0 + (t + 1) * 128)
                kb.dma("sp", x1[i][:], D["X1_d"][rows, :])
                for hh in range(2):
                    p = po[ui % 4]
                    ui += 1
                    cs_ = slice(hh * 512, (hh + 1) * 512)
                    for j in range(32):
                        kb.mm(p[:], m_[:, j, t * 128:(t + 1) * 128], w2[:, j, cs_], start=(j == 0), stop=(j == 31))
                    kb.tt("dve", rr[i][:, cs_], p[:], gg[:, cs_], ALU.mult)
                kb.stt("dve", rr[i][:], x1[i][:], ALPHA, rr[i][:], ALU.mult, ALU.add)
                layer_norm_tile(kb, rr[i], x2[i], stt_[i], mv[i], rs[i], lg, lb)
                kb.dma("pool", dst[rows, :], x2[i][:])


_CACHE = {}


def host_inputs(b, inp, shared):
    m = dict(shared)
    m["xin"] = np.ascontiguousarray(np.concatenate([inp["x"][b], inp["ctx"][b]], axis=0), dtype=np.float32)
    m["cvec"] = np.ascontiguousarray(np.stack([inp["c"][b], inp["c_ctx"]], axis=0), dtype=np.float32)
    return m


def shared_inputs(inp):
    f = lambda a: np.ascontiguousarray(a, dtype=np.float32)
    sh = {"consts": make_consts(), "rope": make_rope(), "Gp": make_rpb_gather(np.asarray(inp["rpb"], np.float32))}
    for k in ("w_ada", "b_ada", "w_in", "conv_w", "gdn_norm_w", "w_out", "ln1_g", "ln1_b", "w_mlp1", "w_mlp2", "ln2_g", "ln2_b"):
        sh[k] = f(inp[k])
    sh["a_log"] = f(inp["a_log"]).reshape(DEPTH, 8)
    sh["dt_bias"] = f(inp["dt_bias"]).reshape(DEPTH, 8)
    return sh


def kernel(**inputs):
    inp = {k: np.asarray(v) for k, v in inputs.items()}
    if "nc" not in _CACHE:
        _CACHE["nc"] = build()
    nc = _CACHE["nc"]
    shared = shared_inputs(inp)
    in_maps = [host_inputs(b, inp, shared) for b in range(8)]
    res = run_bass_kernel_spmd(nc, in_maps, core_ids=list(range(8)))
    return np.stack([np.asarray(r["y"], dtype=np.float32) for r in res.results], axis=0)
```

```python
import math
from contextlib import ExitStack, contextmanager
import numpy as np
import concourse.bass as bass
import concourse.mybir as mybir
from concourse.bass_utils import run_bass_kernel_spmd

F32 = mybir.dt.float32
BF16 = mybir.dt.bfloat16
AF = mybir.ActivationFunctionType
ALU = mybir.AluOpType

N_DMA_SEMS = 24
SAME_ENGINE_SYNC = {"pe": False, "act": True, "dve": True, "pool": True, "sp": True}

DEPTH = 4
T_LAT = 4096
T_CTX = 256
T_ALL = T_LAT + T_CTX
NT = T_ALL // 128
DM = 1024
IN_COLS = 3600
BIG = 30000.0
ALPHA = (2 * DEPTH) ** 0.25
LN_EPS = 1e-5
NORM_EPS = 1e-6


class Tk:
    __slots__ = ("w", "r")

    def __init__(self):
        self.w = None
        self.r = {}


class KB:
    def __init__(self, nc, st):
        self.nc = nc
        self.E = {"pe": nc.tensor, "act": nc.scalar, "dve": nc.vector, "pool": nc.gpsimd, "sp": nc.sync}
        self.sem = {}
        self.cnt = {}
        for e in self.E:
            self.sem[e] = st.enter_context(nc.semaphore("s_" + e))
            self.cnt[e] = 0
        for q in ("sp", "pool", "act"):
            for i in range(N_DMA_SEMS):
                k = "d%s%d" % (q, i)
                self.sem[k] = st.enter_context(nc.semaphore("s_" + k))
                self.cnt[k] = 0
        self.known = {e: {} for e in self.E}
        self.dma_rr = {"sp": 0, "pool": 0, "act": 0}
        self.n_ins = 0
        self.tk = {}
        self.psum = set()
        self.cur = None
        self.uid = 0

    @contextmanager
    def stage(self):
        prev = self.cur
        with ExitStack() as s:
            self.cur = s
            yield s
            self.barrier()
        self.cur = prev

    def sb(self, name, shape, dt):
        self.uid += 1
        nm = "%s_%d" % (name, self.uid)
        t = self.cur.enter_context(self.nc.sbuf_tensor(nm, list(shape), dt))
        self.tk[nm] = Tk()
        return t

    def ps(self, name, shape, dt=F32):
        self.uid += 1
        nm = "%s_%d" % (name, self.uid)
        t = self.cur.enter_context(self.nc.psum_tensor(nm, list(shape), dt))
        self.tk[nm] = Tk()
        self.psum.add(nm)
        return t

    def _split(self, ins, outs):
        reads, writes = [], []
        for a in ins:
            n = a.name
            if n in self.tk:
                (writes if n in self.psum else reads).append(self.tk[n])
        for a in outs:
            n = a.name
            if n in self.tk:
                writes.append(self.tk[n])
        return reads, writes

    def _deps(self, reads, writes):
        deps = {}
        for t in reads:
            if t.w is not None and t.w[1] > deps.get(t.w[0], 0):
                deps[t.w[0]] = t.w[1]
        for t in writes:
            if t.w is not None and t.w[1] > deps.get(t.w[0], 0):
                deps[t.w[0]] = t.w[1]
            for s, c in t.r.items():
                if c > deps.get(s, 0):
                    deps[s] = c
        return deps

    def _wait(self, e, deps):
        eng = self.E[e]
        kn = self.known[e]
        for s, c in deps.items():
            if kn.get(s, 0) >= c:
                continue
            if s == e and not SAME_ENGINE_SYNC[e]:
                continue
            eng.wait_ge(self.sem[s], c)
            kn[s] = c

    def _mark(self, tag, reads, writes):
        s, c = tag
        for t in reads:
            if t.r.get(s, 0) < c:
                t.r[s] = c
        for t in writes:
            t.w = tag
            t.r = {}

    def op(self, e, fn, ins=(), outs=()):
        reads, writes = self._split(ins, outs)
        self._wait(e, self._deps(reads, writes))
        ins_ = fn(self.E[e])
        self.cnt[e] += 1
        ins_.then_inc(self.sem[e], 1)
        self._mark((e, self.cnt[e]), reads, writes)
        self.n_ins += 1

    def dma(self, q, out, in_, **kw):
        reads, writes = self._split([in_], [out])
        k = "d%s%d" % (q, self.dma_rr[q])
        self.dma_rr[q] = (self.dma_rr[q] + 1) % N_DMA_SEMS
        deps = self._deps(reads, writes)
        if self.cnt[k] > deps.get(k, 0):
            deps[k] = self.cnt[k]
        self._wait(q, deps)
        ins_ = self.E[q].dma_start(out=out, in_=in_, **kw)
        self.cnt[k] += 16
        ins_.then_inc(self.sem[k], 16)
        self._mark((k, self.cnt[k]), reads, writes)
        self.n_ins += 1

    def barrier(self):
        for e in self.E:
            self._wait(e, {s: c for s, c in self.cnt.items() if c > 0 and s != e})

    def mm(self, out, lhsT, rhs, start=True, stop=True):
        self.op("pe", lambda e: e.matmul(out, lhsT=lhsT, rhs=rhs, start=start, stop=stop), [lhsT, rhs], [out])

    def tr(self, out, in_, ident):
        self.op("pe", lambda e: e.transpose(out, in_, ident), [in_, ident], [out])

    def act(self, out, in_, func, scale=None, bias=None):
        kw = {}
        ins = [in_]
        if scale is not None:
            kw["scale"] = scale
            if not isinstance(scale, (int, float)):
                ins.append(scale)
        if bias is not None:
            kw["bias"] = bias
            if not isinstance(bias, (int, float)):
                ins.append(bias)
        self.op("act", lambda e: e.activation(out=out, in_=in_, func=func, **kw), ins, [out])

    def copy(self, e, out, in_):
        if e == "act":
            self.act(out, in_, AF.Copy)
        else:
            self.op(e, lambda g: g.tensor_copy(out=out, in_=in_), [in_], [out])

    def tt(self, e, out, a, b, op):
        self.op(e, lambda g: g.tensor_tensor(out=out, in0=a, in1=b, op=op), [a, b], [out])

    def ts(self, e, out, a, s1, op0, s2=None, op1=None):
        ins = [a] + [s for s in (s1, s2) if s is not None and not isinstance(s, (int, float))]
        if op1 is None:
            self.op(e, lambda g: g.tensor_scalar(out=out, in0=a, scalar1=s1, scalar2=None, op0=op0), ins, [out])
        else:
            self.op(e, lambda g: g.tensor_scalar(out=out, in0=a, scalar1=s1, scalar2=s2, op0=op0, op1=op1), ins, [out])

    def stt(self, e, out, a, scalar, b, op0, op1):
        ins = [a, b] + ([] if isinstance(scalar, (int, float)) else [scalar])
        self.op(e, lambda g: g.scalar_tensor_tensor(out=out, in0=a, scalar=scalar, in1=b, op0=op0, op1=op1), ins, [out])

    def memset(self, e, out, val):
        self.op(e, lambda g: g.memset(out, val), [], [out])


CONST_LAYOUT = {}


def make_consts():
    P = np.arange(128)
    parts = []
    off = 0

    def add(name, arr):
        nonlocal off
        arr = np.asarray(arr, np.float32)
        CONST_LAYOUT[name] = (off, arr.shape[1])
        parts.append(arr)
        off += arr.shape[1]

    add("ident", np.eye(128))
    add("U1", P[:, None] <= P[None, :])
    add("U1T", P[:, None] >= P[None, :])
    add("SL", P[:, None] > P[None, :])
    add("SU", P[:, None] < P[None, :])
    add("ones", np.ones((128, 128)))
    negf = np.where(P[:, None] > P[None, :], 0.0, -BIG)
    negb = np.where(P[:, None] < P[None, :], 0.0, -BIG)
    negtf = np.where(P[None, :] >= P[:, None], 0.0, -BIG)
    negtb = np.where(P[None, :] <= P[:, None], 0.0, -BIG)
    add("NEGf", np.tile(negf, (1, 4)))
    add("NEGb", np.tile(negb, (1, 4)))
    add("NEGTf", np.tile(negtf, (1, 4)))
    add("NEGTb", np.tile(negtb, (1, 4)))
    pi = np.where((P % 64) < 32, P + 32, P - 32)
    rp = np.zeros((128, 128))
    rp[pi, P] = 1.0
    add("Rperm", rp)
    kc = P % 64
    par = P // 64
    qc = np.arange(64)
    cs = np.clip(qc - 8, 0, 48)
    colmask = np.where((kc[:, None] >= cs[None, :]) & (kc[:, None] < cs[None, :] + 16), 0.0, -BIG)
    add("colmask", colmask)
    m9 = np.zeros((128, 9, 64))
    m9[par == 1, 0, :] = -BIG
    m9[par == 0, 8, :] = -BIG
    add("M9", m9.reshape(128, 576))
    dg32 = (P[:, None] // 32 == P[None, :] // 32)
    off1 = (P[:, None] // 64 == P[None, :] // 64) & (P[:, None] // 32 > P[None, :] // 32)
    off2 = (P[:, None] // 64 > P[None, :] // 64)
    add("Dg32", dg32)
    add("Off1", off1)
    add("Off1T", off1.T)
    add("Off2", off2)
    add("Off2T", off2.T)
    return np.concatenate(parts, axis=1).astype(np.float32)


def make_rope():
    t = np.arange(T_LAT)
    row = (t // 64).astype(np.float32)
    col = (t % 64).astype(np.float32)
    inv = (10000.0 ** (-np.arange(0, 64, 2, dtype=np.float32) / 64)).astype(np.float32)
    i = np.arange(128)
    f = i % 32
    pos = np.where((i < 64)[:, None], row[None, :], col[None, :]).astype(np.float32)
    ang = (pos * inv[f][:, None]).astype(np.float32)
    sgn = np.where((i % 64) < 32, -1.0, 1.0)[:, None]
    out = np.stack([np.cos(ang), sgn * np.sin(ang)], axis=1)
    return out.astype(np.float32)


def make_rpb_gather(rpb):
    P = np.arange(128)
    kc = (P % 64)[:, None, None]
    par = (P // 64)[:, None, None]
    ep = np.arange(16)[None, :, None]
    qc = np.arange(64)[None, None, :]
    dr = 7 - ep + par + 0 * qc
    dc = kc - qc + 0 * ep
    valid = (np.abs(dr) <= 7) & (np.abs(dc) <= 15)
    dri = np.clip(dr + 7, 0, 14)
    dci = np.clip(dc + 15, 0, 30)
    g = rpb[:, :, dri, dci]
    g = np.where(valid[None, None], g, np.float32(0.0))
    return np.ascontiguousarray(g.reshape(rpb.shape[0], rpb.shape[1], 128, 1024).astype(np.float32))


def build(n_layers=DEPTH, dbg=(), stages=None):
    make_consts()
    nc = bass.Bass("TRN2", target_bir_lowering=False)
    D = {}

    def din(name, shape, dt=F32):
        D[name] = nc.dram_tensor(name, list(shape), dt, kind="ExternalInput").ap()

    def dscr(name, shape, dt):
        kind = "ExternalOutput" if name in dbg else "Internal"
        D[name] = nc.dram_tensor(name, list(shape), dt, kind=kind).ap()

    ncst = sum(v[1] for v in CONST_LAYOUT.values())
    din("xin", [T_ALL, DM]); din("cvec", [2, DM]); din("consts", [128, ncst]); din("rope", [128, 2, T_LAT])
    din("Gp", [DEPTH, 8, 128, 1024])
    din("w_ada", [DEPTH, DM, 6 * DM]); din("b_ada", [DEPTH, 6 * DM]); din("w_in", [DEPTH, DM, IN_COLS])
    din("conv_w", [DEPTH, 5, 1536]); din("a_log", [DEPTH, 8]); din("dt_bias", [DEPTH, 8]); din("gdn_norm_w", [DEPTH, 128])
    din("w_out", [DEPTH, DM, DM]); din("ln1_g", [DEPTH, DM]); din("ln1_b", [DEPTH, DM])
    din("w_mlp1", [DEPTH, DM, 4 * DM]); din("w_mlp2", [DEPTH, 4 * DM, DM]); din("ln2_g", [DEPTH, DM]); din("ln2_b", [DEPTH, DM])
    D["y"] = nc.dram_tensor("y", [T_LAT, DM], F32, kind="ExternalOutput").ap()

    dscr("winb", [DEPTH, DM, IN_COLS], BF16); dscr("woutb", [DEPTH, DM, DM], BF16)
    dscr("w1b", [DEPTH, DM, 4 * DM], BF16); dscr("w2b", [DEPTH, 4 * DM, DM], BF16)
    dscr("mod_d", [DEPTH, 2, 6 * DM], F32)
    dscr("FM_d", [3072, T_ALL], BF16)
    dscr("v_d", [T_ALL, 512], BF16); dscr("ba_d", [T_ALL, 16], F32)
    dscr("GQK_d", [1024, T_ALL], BF16)
    dscr("Ktok_d", [T_ALL, 512], BF16); dscr("Vtok_d", [T_ALL, 512], BF16)
    dscr("oT_d", [2, 512, T_ALL], F32)
    dscr("aT_d", [1024, T_ALL], BF16)
    dscr("X_d", [T_ALL, DM], F32); dscr("X1_d", [T_ALL, DM], F32); dscr("h2T_d", [DM, T_ALL], BF16)
    dscr("m1T_d", [4 * DM, T_ALL], BF16)

    with ExitStack() as st:
        kb = KB(nc, st)
        bg = (stages is None) or ("g2" in stages)
        if stages is None or "cast" in stages:
            items = [("w_in", "winb", 0, None)]
            if not bg:
                for l_ in range(n_layers):
                    items += layer_cast_items(l_, n_layers)
            cast_weights(kb, D, items)
        if stages is None or "mod" in stages:
            mod_stage(kb, D, n_layers)
        for l in range(n_layers):
            last = (l == DEPTH - 1)
            xsrc = D["xin"] if l == 0 else D["X_d"]
            if stages is None or "s1" in stages:
                stage1(kb, D, l, xsrc)
            if stages is None or "na" in stages:
                stage_na(kb, D, l, last)
            if stages is None or "g1" in stages:
                stage_g1(kb, D, l)
            if stages is None or "g2" in stages:
                stage_g2(kb, D, l, layer_cast_items(l, n_layers))
            if stages is None or "g3" in stages:
                stage_g3(kb, D, l, last)
            if stages is None or "s4a" in stages:
                stage4a(kb, D, l, xsrc, last)
            if stages is None or "s4b" in stages:
                stage4b(kb, D, l, last)
        kb.barrier()
    nc._n_ins = kb.n_ins
    return nc


def run_gens(gens):
    gens = list(gens)
    while gens:
        for g_ in list(gens):
            try:
                next(g_)
            except StopIteration:
                gens.remove(g_)


def cload(kb, D, name, dt=F32, eng="dve", rows=128):
    off, w = CONST_LAYOUT[name]
    t = kb.sb("c_" + name, [128, w], F32)
    kb.dma("sp", t[:], D["consts"][:, off:off + w])
    if dt == F32:
        return t
    tb = kb.sb("cb_" + name, [128, w], dt)
    kb.copy(eng, tb[:], t[:])
    return tb


def cast_jobs(D, items):
    jobs = []
    for src, dst, l, w in items:
        s2 = D[src][l]
        d2 = D[dst][l]
        if w:
            s2 = s2.rearrange("(a b) c -> a (b c)", b=w)
            d2 = d2.rearrange("(a b) c -> a (b c)", b=w)
        R, C = s2.shape
        for r0 in range(0, R, 128):
            for c0 in range(0, C, 2048):
                cw = min(2048, C - c0)
                jobs.append((s2[r0:r0 + 128, c0:c0 + cw], d2[r0:r0 + 128, c0:c0 + cw], cw))
    return jobs


def layer_cast_items(l, n_layers):
    it = [("w_out", "woutb", l, 2), ("w_mlp1", "w1b", l, None), ("w_mlp2", "w2b", l, 2)]
    if l + 1 < n_layers:
        it.append(("w_in", "winb", l + 1, None))
    return it


def cast_weights(kb, D, items):
    with kb.stage():
        NB = 3
        fb = [kb.sb("cf", [128, 2048], F32) for _ in range(NB)]
        bb = [kb.sb("cb", [128, 2048], BF16) for _ in range(NB)]
        for i, (s, d, cw) in enumerate(cast_jobs(D, items)):
            b = i % NB
            kb.dma("sp", fb[b][:, :cw], s)
            kb.copy("dve" if i % 2 == 0 else "act", bb[b][:, :cw], fb[b][:, :cw])
            kb.dma("pool", d, bb[b][:, :cw])


def cast_gen(kb, D, items, fb, bb):
    NB = len(fb)
    for i, (s, d, cw) in enumerate(cast_jobs(D, items)):
        b = i % NB
        kb.dma("sp", fb[b][:, :cw], s)
        yield
        kb.copy("act", bb[b][:, :cw], fb[b][:, :cw])
        kb.dma("act", d, bb[b][:, :cw])
        yield


def mod_stage(kb, D, n_layers):
    with kb.stage():
        cT = kb.sb("cT", [128, 2, 8], F32)
        kb.dma("sp", cT[:], D["cvec"].rearrange("r (k p) -> p r k", p=128), allow_slow_non_contiguous=True)
        scl = kb.sb("scl", [128, 8, 33], F32)
        kb.memset("dve", scl[:], 0.0)
        kb.act(scl[:, :, 0], cT[:, 0, :], AF.Silu)
        kb.act(scl[:, :, 32], cT[:, 1, :], AF.Silu)
        wt = [kb.sb("wada", [128, 8, 512], F32) for _ in range(2)]
        bt = [kb.sb("bada", [33, 512], F32) for _ in range(2)]
        mr = [kb.sb("mrow", [33, 512], F32) for _ in range(2)]
        pm = [kb.ps("pm", [128, 512]) for _ in range(2)]
        for b in bt:
            kb.memset("dve", b[:], 0.0)
        i = 0
        for l in range(n_layers):
            for cb in range(12):
                j = i % 2
                i += 1
                cols = slice(cb * 512, (cb + 1) * 512)
                kb.dma("sp", wt[j][:], D["w_ada"][l, :, cols].rearrange("(k p) c -> p k c", p=128))
                kb.dma("sp", bt[j][0:1, :], D["b_ada"][l:l + 1, cols])
                kb.dma("sp", bt[j][32:33, :], D["b_ada"][l:l + 1, cols])
                for kc in range(8):
                    kb.mm(pm[j][0:33, :], scl[:, kc, :], wt[j][:, kc, :], start=(kc == 0), stop=(kc == 7))
                kb.tt("dve", mr[j][:], pm[j][0:33, :], bt[j][:], ALU.add)
                if cb in (2, 3, 8, 9):
                    kb.ts("dve", mr[j][:], mr[j][:], 1.0, ALU.add)
                kb.dma("pool", D["mod_d"][l, 0:1, cols], mr[j][0:1, :])
                kb.dma("pool", D["mod_d"][l, 1:2, cols], mr[j][32:33, :])


def bvec(kb, name, src):
    n = src.shape[-1]
    t = kb.sb(name, [128, n], F32)
    kb.dma("sp", t[:], src.partition_broadcast(128))
    return t


def stage1(kb, D, l, xsrc):
    with kb.stage():
        w = kb.sb("win", [128, 8, IN_COLS], BF16)
        for kc in range(8):
            kb.dma("sp", w[:, kc, :], D["winb"][l, kc * 128:(kc + 1) * 128, :])
        mods = {}
        for r, nm in ((0, "l"), (1, "c")):
            mods[nm] = (bvec(kb, "sc1", D["mod_d"][l, r, 1024:2048]), bvec(kb, "sh1", D["mod_d"][l, r, 0:1024]))
        idb = cload(kb, D, "ident", BF16)
        xt = [kb.sb("xt", [128, DM], F32) for _ in range(2)]
        hb = [kb.sb("hb", [128, DM], BF16) for _ in range(2)]
        hT = [kb.sb("hT", [128, 8, 512], BF16) for _ in range(2)]
        fm = [kb.sb("fm", [128, 6, 512], BF16) for _ in range(4)]
        vt = [kb.sb("vt", [128, 512], BF16) for _ in range(2)]
        bat = [kb.sb("bat", [128, 16], F32) for _ in range(2)]
        pT = [kb.ps("pT", [128, 1024], BF16) for _ in range(2)]
        pF = [kb.ps("pF", [128, 512]) for _ in range(3)]
        pV = kb.ps("pV", [128, 512])
        pB = kb.ps("pB", [128, 512])
        FMv = D["FM_d"].rearrange("(b p) t -> p b t", p=128)
        ti = 0
        ei = 0
        fi = 0
        for g in range(9):
            ntile = 4 if g < 8 else 2
            ntok = ntile * 128
            tok0 = g * 512
            sc, sh = mods["l"] if g < 8 else mods["c"]
            hTg = hT[g % 2]
            for tt_ in range(ntile):
                j = ti % 2
                ti += 1
                rows = slice(tok0 + tt_ * 128, tok0 + (tt_ + 1) * 128)
                kb.dma("sp", xt[j][:], xsrc[rows, :])
                kb.tt("dve", xt[j][:], xt[j][:], sc[:], ALU.mult)
                kb.tt("dve", hb[j][:], xt[j][:], sh[:], ALU.add)
                for kc in range(8):
                    kb.tr(pT[j][:, kc * 128:(kc + 1) * 128], hb[j][:, kc * 128:(kc + 1) * 128], idb[:])
                kb.act(hTg[:, :, tt_ * 128:(tt_ + 1) * 128], pT[j][:].rearrange("p (k t) -> p k t", k=8), AF.Copy)
                for kc in range(8):
                    kb.mm(pV[:], hTg[:, kc, tt_ * 128:(tt_ + 1) * 128], w[:, kc, 1024:1536], start=(kc == 0), stop=(kc == 7))
                for kc in range(8):
                    kb.mm(pB[:, 0:16], hTg[:, kc, tt_ * 128:(tt_ + 1) * 128], w[:, kc, 3584:3600], start=(kc == 0), stop=(kc == 7))
                kb.copy("dve", vt[j][:], pV[:])
                kb.copy("dve", bat[j][:], pB[:, 0:16])
                kb.dma("pool", D["v_d"][rows, :], vt[j][:])
                kb.dma("pool", D["ba_d"][rows, :], bat[j][:])
            for blk in range(24):
                col0 = blk * 128 if blk < 8 else 1536 + (blk - 8) * 128
                p = pF[ei % 3]
                for kc in range(8):
                    kb.mm(p[:, :ntok], w[:, kc, col0:col0 + 128], hTg[:, kc, :ntok], start=(kc == 0), stop=(kc == 7))
                if blk % 6 == 0:
                    fcur = fm[fi % 4]
                    fi += 1
                if blk < 4:
                    kb.act(fcur[:, blk % 6, :ntok], p[:, :ntok], AF.Copy, scale=0.125)
                elif ei % 2 == 0:
                    kb.act(fcur[:, blk % 6, :ntok], p[:, :ntok], AF.Copy)
                else:
                    kb.copy("dve", fcur[:, blk % 6, :ntok], p[:, :ntok])
                ei += 1
                if blk % 6 == 5:
                    b0 = blk - 5
                    kb.dma("pool", FMv[:, b0:b0 + 6, tok0:tok0 + ntok], fcur[:, :, :ntok])


def na_pieces(c):
    out = []
    r0 = 8 * c
    for j in range(32):
        runs = []
        for r in range(r0, r0 + 8):
            rs = min(max(r - 4, 0), 56)
            if 2 * j + 1 < rs or 2 * j > rs + 7:
                continue
            edge = (r < 4) or (r > 60)
            kind = "G" if edge else "S"
            var = (7 - 2 * j + r) if edge else (r - 2 * j + 3)
            if runs and runs[-1][0] == kind and runs[-1][2] == r and runs[-1][1] + (r - runs[-1][3]) == var:
                runs[-1][2] = r + 1
            else:
                runs.append([kind, var, r + 1, r])
        for kind, var, rend, rstart in runs:
            out.append((j, (rstart - r0) * 64, (rend - rstart) * 64, (kind, var)))
    return out


def stage_na(kb, D, l, last):
    with kb.stage():
        colmask = cload(kb, D, "colmask")
        m9 = cload(kb, D, "M9")
        idf = cload(kb, D, "ident")
        vall = kb.sb("vall", [128, NT, 512], BF16)
        v3 = D["v_d"].rearrange("(t p) c -> p t c", p=128)
        for t0 in range(0, NT, 6):
            t1 = min(NT, t0 + 6)
            kb.dma("sp", vall[:, t0:t1, :], v3[:, t0:t1, :])
        vaug = [kb.sb("vaug", [128, NT, 128], BF16) for _ in range(2)]
        qk = [kb.sb("qk", [64, 2, T_ALL], BF16) for _ in range(2)]
        Gt = [kb.sb("Gt", [128, 1024], F32) for _ in range(2)]
        GM = [kb.sb("GM", [128, 1024], BF16) for _ in range(2)]
        S9 = [kb.sb("S9", [128, 576], BF16) for _ in range(2)]
        rcs = [kb.sb("rcs", [128, 512], F32) for _ in range(2)]
        for sid in range(2):
            kb.memset("pool", vaug[sid][:, :, 64:128], 1.0)
            kb.memset("dve", rcs[sid][:], 0.0)

        def mk2(name, shape, dt):
            return [[kb.sb(name, shape, dt) for _ in range(2)] for _ in range(2)]

        def mk3(name, shape, dt):
            return [[kb.sb(name, shape, dt) for _ in range(3)] for _ in range(2)]

        sc = mk3("sc", [128, 512], F32)
        pt = mk3("pt", [128, 512], BF16)
        osb = mk2("osb", [64, 512], F32)
        ob = mk2("ob", [64, 512], BF16)
        pS = [[kb.ps("pS", [128, 512]) for _ in range(3)] for _ in range(2)]
        pO = [[kb.ps("pO", [128, 512]) for _ in range(1)] for _ in range(2)]

        def headgen(h, sid):
            q_ = qk[sid]
            va = vaug[sid]
            kb.dma("sp", q_[:, 0, :], D["FM_d"][h * 64:(h + 1) * 64, :])
            kb.dma("sp", q_[:, 1, :], D["FM_d"][512 + h * 64:512 + (h + 1) * 64, :])
            kb.dma("sp", Gt[sid][:], D["Gp"][l, h])
            kb.copy("pool", va[:, :, 0:64], vall[:, :, h * 64:(h + 1) * 64])
            kb.tt("dve", GM[sid][:].rearrange("p (e q) -> p e q", e=16), Gt[sid][:].rearrange("p (e q) -> p e q", e=16),
                  colmask[:].unsqueeze(1).to_broadcast([128, 16, 64]), ALU.add)
            kb.tt("dve", S9[sid][:], GM[sid][:, 4 * 64:13 * 64], m9[:], ALU.add)
            yield
            pi = 0
            for c in range(9):
                if c == 8 and last:
                    continue
                ncols = 512 if c < 8 else 256
                q0 = c * 512
                pieces = [(32, 0, ncols, None), (33, 0, ncols, None)]
                if c < 8:
                    pieces += na_pieces(c)
                po = pO[sid][0]
                pend = []

                def pv(item):
                    k_, j_, lo_, n_, e_ = item
                    kb.mm(po[:, lo_:lo_ + n_], va[:, j_, :], e_[:, :n_], start=(k_ == 0), stop=(k_ == len(pieces) - 1))

                for k, (j, lo, n, bias) in enumerate(pieces):
                    p = pS[sid][pi % 3]
                    e = pt[sid][pi % 3]
                    s_ = sc[sid][pi % 3]
                    pi += 1
                    kb.mm(p[:, :n], q_[:, 1, j * 128:(j + 1) * 128], q_[:, 0, q0 + lo:q0 + lo + n], start=True, stop=True)
                    yield
                    if bias is not None:
                        src = S9[sid] if bias[0] == "S" else GM[sid]
                        kb.tt("dve", s_[:, :n], p[:, :n], src[:, bias[1] * 64:bias[1] * 64 + n], ALU.add)
                        yield
                        kb.act(e[:, :n], s_[:, :n], AF.Exp)
                    else:
                        kb.act(e[:, :n], p[:, :n], AF.Exp)
                    pend.append((k, j, lo, n, e))
                    if len(pend) > 2:
                        pv(pend.pop(0))
                    yield
                while pend:
                    pv(pend.pop(0))
                yield
                ci = c % 2
                r_ = rcs[sid]
                os_ = osb[sid][ci]
                o_ = ob[sid][ci]
                pb = pS[sid][pi % 3]
                pi += 1
                kb.act(r_[64:128, :ncols], po[64:128, :ncols], AF.Ln)
                kb.act(r_[64:128, :ncols], r_[64:128, :ncols], AF.Exp, scale=-1.0)
                kb.copy("act", os_[:, :ncols], po[0:64, :ncols])
                yield
                kb.mm(pb[0:64, :ncols], idf[:, 64:128], r_[:, :ncols], start=True, stop=True)
                yield
                kb.tt("dve", o_[:, :ncols], os_[:, :ncols], pb[0:64, :ncols], ALU.mult)
                kb.dma("pool", D["aT_d"][h * 64:(h + 1) * 64, q0:q0 + ncols], o_[:, :ncols])
                yield

        for h0 in range(0, 8, 2):
            run_gens([headgen(h0, 0), headgen(h0 + 1, 1)])


def stage_g1(kb, D, l):
    with kb.stage():
        idf = cload(kb, D, "ident")
        idb = kb.sb("idb", [128, 128], BF16)
        kb.copy("dve", idb[:], idf[:])
        onesb = cload(kb, D, "ones", BF16)
        rpb_ = cload(kb, D, "Rperm", BF16)
        rope = kb.sb("rope", [128, 2, T_LAT], F32)
        kb.dma("sp", rope[:, 0, :], D["rope"][:, 0, :])
        kb.dma("sp", rope[:, 1, :], D["rope"][:, 1, :])
        cw = kb.sb("cw", [128, 5, 12], F32)
        for j in range(5):
            kb.dma("sp", cw[:, j, :], D["conv_w"][l, j].rearrange("(f p) -> p f", p=128), allow_slow_non_contiguous=True)
        XW = 4360
        xin = [kb.sb("gxin", [128, XW], BF16) for _ in range(2)]
        xsh = [kb.sb("gxsh", [128, XW], BF16) for _ in range(2)]
        for x_ in xin + xsh:
            kb.memset("dve", x_[:], 0.0)
        sall = [kb.sb("sall", [128, T_ALL], F32) for _ in range(2)]
        dg = [kb.sb("dg", [128, 5, 128], BF16) for _ in range(2)]

        def mk2(name, shape, dt):
            return [[kb.sb(name, shape, dt) for _ in range(2)] for _ in range(2)]

        sq = mk2("sq", [128, 512], BF16)
        lnb = mk2("lnb", [128, 512], F32)
        qn = mk2("qn", [128, 512], BF16)
        t1 = mk2("t1", [128, 512], F32)
        t2 = mk2("t2", [128, 512], F32)
        obq = mk2("obq", [128, 512], BF16)
        tk_ = mk2("tkst", [128, 4, 128], BF16)
        pc = [kb.ps("pc", [128, 512]) for _ in range(2)]
        pn = [kb.ps("pn", [128, 512]) for _ in range(2)]
        pr = [kb.ps("pr", [128, 512]) for _ in range(2)]
        pt = [kb.ps("ptr", [128, 8, 128], BF16) for _ in range(2)]
        blocks = [(b * 512, 512, 2 + b * 512) for b in range(8)] + [(T_LAT, 256, 4102)]

        def fcgen(fc, sid):
            x_, xs_, d_, sa = xin[sid], xsh[sid], dg[sid], sall[sid]
            src = D["FM_d"][1024 + fc * 128:1024 + (fc + 1) * 128, :]
            kb.dma("sp", x_[:, 2:2 + T_LAT], src[:, 0:T_LAT])
            kb.dma("sp", x_[:, 4102:4102 + T_CTX], src[:, T_LAT:T_ALL])
            kb.dma("sp", xs_[:, 1:1 + T_LAT], src[:, 0:T_LAT])
            kb.dma("sp", xs_[:, 4101:4101 + T_CTX], src[:, T_LAT:T_ALL])
            for j in range(5):
                kb.ts("dve", d_[:, j, :], idf[:], cw[:, j, fc:fc + 1], ALU.mult)
            yield
            for (tok0, n, xc) in blocks:
                p = pc[sid]
                for j in range(5):
                    if j % 2 == 0:
                        rhs_ = x_[:, xc + j - 2:xc + j - 2 + n]
                    else:
                        rhs_ = xs_[:, xc + j - 3:xc + j - 3 + n]
                    kb.mm(p[:, :n], d_[:, j, :], rhs_, start=(j == 0), stop=(j == 4))
                yield
                kb.act(sa[:, tok0:tok0 + n], p[:, :n], AF.Silu)
                yield
            isq = fc < 4
            isv = fc >= 8
            hh = fc % 4
            for bi, (tok0, n, xc) in enumerate(blocks):
                i = bi % 2
                o_ = obq[sid][i]
                if isv:
                    kb.copy("dve", o_[:, :n], sa[:, tok0:tok0 + n])
                    yield
                else:
                    kb.tt("pool", sq[sid][i][:, :n], sa[:, tok0:tok0 + n], sa[:, tok0:tok0 + n], ALU.mult)
                    yield
                    kb.mm(pn[sid][:, :n], onesb[:], sq[sid][i][:, :n])
                    yield
                    kb.act(lnb[sid][i][:, :n], pn[sid][:, :n], AF.Ln, bias=NORM_EPS)
                    kb.act(lnb[sid][i][:, :n], lnb[sid][i][:, :n], AF.Exp, scale=-0.5, bias=(-0.5 * math.log(128.0) if isq else 0.0))
                    yield
                    if tok0 < T_LAT:
                        kb.tt("dve", qn[sid][i][:, :n], sa[:, tok0:tok0 + n], lnb[sid][i][:, :n], ALU.mult)
                        yield
                        kb.mm(pr[sid][:, :n], rpb_[:], qn[sid][i][:, :n])
                        kb.tt("pool", t1[sid][i][:, :n], qn[sid][i][:, :n], rope[:, 0, tok0:tok0 + n], ALU.mult)
                        yield
                        kb.tt("dve", t2[sid][i][:, :n], pr[sid][:, :n], rope[:, 1, tok0:tok0 + n], ALU.mult)
                        yield
                        kb.tt("dve", o_[:, :n], t1[sid][i][:, :n], t2[sid][i][:, :n], ALU.add)
                        yield
                    else:
                        kb.tt("dve", o_[:, :n], sa[:, tok0:tok0 + n], lnb[sid][i][:, :n], ALU.mult)
                        yield
                    kb.dma("pool", D["GQK_d"][fc * 128:(fc + 1) * 128, tok0:tok0 + n], o_[:, :n])
                if fc >= 4:
                    nt_ = n // 128
                    for a in range(nt_):
                        kb.tr(pt[sid][:, a, :], o_[:, a * 128:(a + 1) * 128], idb[:])
                    yield
                    kb.copy("act", tk_[sid][i][:, :nt_, :], pt[sid][:, :nt_, :])
                    dst = D["Vtok_d"] if isv else D["Ktok_d"]
                    kb.dma("pool", dst[tok0:tok0 + n, hh * 128:(hh + 1) * 128].rearrange("(a p) c -> p a c", p=128), tk_[sid][i][:, :nt_, :])
                    yield

        for fc0 in range(0, 12, 2):
            run_gens([fcgen(fc0, 0), fcgen(fc0 + 1, 1)])


def stage_g2(kb, D, l, cast_items=()):
    with kb.stage():
        cfb = [kb.sb("cf", [128, 2048], F32) for _ in range(3)]
        cbb = [kb.sb("cb", [128, 2048], BF16) for _ in range(3)]
        castg = cast_gen(kb, D, cast_items, cfb, cbb)
        idf = cload(kb, D, "ident")
        idb = kb.sb("idb", [128, 128], BF16)
        kb.copy("dve", idb[:], idf[:])
        onesf = cload(kb, D, "ones")
        Ua = [cload(kb, D, "U1"), cload(kb, D, "U1T")]
        Sa = [cload(kb, D, "SL"), cload(kb, D, "SU")]
        NEGs = [cload(kb, D, "NEGf", BF16), cload(kb, D, "NEGb", BF16)]
        NEGT = [cload(kb, D, "NEGTf", BF16), cload(kb, D, "NEGTb", BF16)]
        ba = kb.sb("ba", [128, NT, 16], F32)
        kb.dma("sp", ba[:], D["ba_d"].rearrange("(t p) c -> p t c", p=128))
        al = bvec(kb, "alog", D["a_log"][l])
        dtb = bvec(kb, "dtb", D["dt_bias"][l])
        nA = kb.sb("nA", [128, 8], F32)
        kb.act(nA[:], al[:], AF.Exp)
        kb.ts("dve", nA[:], nA[:], -1.0, ALU.mult)
        gx = kb.sb("gx", [128, NT, 8], F32)
        g_all = kb.sb("g_all", [128, NT, 8], F32)
        beta = kb.sb("beta", [128, NT, 8], F32)
        negb = kb.sb("negb", [128, NT, 8], F32)
        kb.tt("dve", gx[:], ba[:, :, 8:16], dtb[:].unsqueeze(1).to_broadcast([128, NT, 8]), ALU.add)
        kb.act(gx[:], gx[:], AF.Exp)
        kb.act(gx[:], gx[:], AF.Ln, bias=1.0)
        kb.tt("dve", g_all[:], gx[:], nA[:].unsqueeze(1).to_broadcast([128, NT, 8]), ALU.mult)
        kb.act(beta[:], ba[:, :, 0:8], AF.Sigmoid)
        kb.ts("dve", negb[:], beta[:], -1.0, ALU.mult)

        def mk(name, shape, dt, n=2):
            return [[kb.sb(name, shape, dt) for _ in range(n)] for _ in range(2)]

        QK = mk("QK", [128, 8, 128], BF16)
        KV = mk("KV", [128, 2, 512], BF16)
        gS = mk("gS", [128, 4, 128], F32, 1)
        gU = mk("gU", [128, 4, 128], F32, 1)
        Ds = mk("Ds", [128, 4, 128], F32, 1)
        DT = mk("DT", [128, 4, 128], F32, 1)
        Er = mk("Er", [128, 4, 128], F32, 1)
        E3 = mk("E3", [128, 3, 4], F32)
        be = mk("be", [128, 4], F32, 1)
        negA = mk("negA", [128, 4, 128], F32, 1)
        MP = mk("MP", [128, 4, 2, 128], BF16)
        Rm = mk("Rm", [128, 4, 128], BF16)
        Tm = mk("Tm", [128, 4, 128], BF16)
        N1 = mk("N1", [128, 4, 128], BF16, 1)
        N1T = mk("N1T", [128, 4, 128], BF16, 1)
        N2 = mk("N2", [128, 4, 128], BF16, 1)
        Yb = mk("Yb", [128, 4, 128], BF16, 1)
        Xb = mk("Xb", [128, 4, 128], BF16, 1)
        X2b = mk("X2b", [128, 4, 128], BF16, 1)
        Rb = mk("Rb", [128, 4, 128], BF16, 1)
        Dg = cload(kb, D, "Dg32")
        O1 = [cload(kb, D, "Off1"), cload(kb, D, "Off1T")]
        O1T = [O1[1], O1[0]]
        O2 = [cload(kb, D, "Off2"), cload(kb, D, "Off2T")]
        attnT = mk("attnT", [128, 4, 128], BF16)
        QdT = mk("QdT", [128, 4, 128], BF16)
        Kd = mk("Kd", [128, 4, 128], BF16)
        Kbe = mk("Kbe", [128, 4, 128], BF16, 1)
        bV = mk("bV", [128, 4, 128], BF16, 1)
        U = mk("U", [128, 4, 128], F32)
        WT = mk("WT", [128, 4, 128], BF16)
        vnew = mk("vnew", [128, 4, 128], BF16, 1)
        osb = mk("osb", [128, 4, 128], F32)
        S = [kb.sb("S", [128, 4, 128], F32) for _ in range(2)]
        Sb = [kb.sb("Sb", [128, 4, 128], BF16) for _ in range(2)]
        for d in range(2):
            kb.memset("dve", S[d][:], 0.0)
            kb.memset("pool", Sb[d][:], 0.0)
        pY = [kb.ps("pY", [128, 4, 2, 128]) for _ in range(2)]
        pZ = [kb.ps("pZ", [128, 4, 128]) for _ in range(2)]
        pW = kb.ps("pW", [128, 4, 128])
        pOo = kb.ps("pOo", [128, 4, 128])
        GQ = D["GQK_d"].rearrange("(a p) t -> p a t", p=128)
        OT = [D["oT_d"][d].rearrange("(h v) t -> v h t", v=128) for d in range(2)]

        order = [[32, 33] + list(range(32)), [33, 32] + list(range(31, -1, -1))]
        B4 = [128, 4, 128]
        f2 = lambda a: a[:].rearrange("p h t -> p (h t)")

        def pre(s, d):
            t = order[d][s]
            b = s % 2
            tok = slice(t * 128, (t + 1) * 128)
            qk = QK[d][b]
            kv = KV[d][b]
            PY = pY[d]
            PZ = pZ[d]
            PYf = PY[:].rearrange("p h w t -> p (h w t)")
            A0f = PYf[:, 0:512]
            A1f = PYf[:, 512:1024]
            A0 = A0f.rearrange("p (h t) -> p h t", h=4)
            A1 = A1f.rearrange("p (h t) -> p h t", h=4)
            kb.dma("sp", qk[:], GQ[:, :, tok])
            kb.dma("sp", kv[:, 0, :], D["Ktok_d"][tok, :])
            kb.dma("sp", kv[:, 1, :], D["Vtok_d"][tok, :])
            gcol = g_all[:, t, d * 4:(d + 1) * 4]
            bcol = beta[:, t, d * 4:(d + 1) * 4]
            nbcol = negb[:, t, d * 4:(d + 1) * 4]
            kb.tt("dve", gS[d][0][:], gcol.unsqueeze(2).to_broadcast(B4), Sa[d][:].unsqueeze(1).to_broadcast(B4), ALU.mult)
            kb.tt("pool", gU[d][0][:], gcol.unsqueeze(2).to_broadcast(B4), Ua[d][:].unsqueeze(1).to_broadcast(B4), ALU.mult)
            yield
            kb.mm(A0f, Ua[d][:], f2(gS[d][0]), start=True, stop=False)
            kb.mm(A0f, idb[:], NEGs[d][:], start=False, stop=True)
            kb.mm(A1f, Sa[d][:], f2(gU[d][0]), start=True, stop=False)
            kb.mm(A1f, idb[:], NEGT[d][:], start=False, stop=True)
            kb.mm(f2(PZ), onesf[:], f2(gU[d][0]), start=True, stop=True)
            yield
            kb.act(f2(Ds[d][0]), A0f, AF.Exp)
            kb.act(f2(DT[d][0]), A1f, AF.Exp)
            kb.act(Er[d][0][:], PZ[:], AF.Exp)
            yield
            kb.mm(PZ[:, 0, 0:4], Ua[d][:], gcol, start=True, stop=True)
            kb.mm(PZ[:, 0, 4:8], Sa[d][:], gcol, start=True, stop=True)
            kb.mm(PZ[:, 0, 8:12], onesf[:], gcol, start=True, stop=True)
            e3 = E3[d][b]
            kb.act(e3[:].rearrange("p a h -> p (a h)"), PZ[:, 0, 0:12], AF.Exp)
            kb.tt("dve", be[d][0][:], bcol, e3[:, 0, :], ALU.mult)
            yield
            for h in range(4):
                kb.mm(A0[:, h, :], qk[:, 4 + h, :], qk[:, 4 + h, :], start=True, stop=True)
            for h in range(4):
                kb.mm(A1[:, h, :], qk[:, 4 + h, :], qk[:, h, :], start=True, stop=True)
            yield
            na_ = negA[d][0]
            for h in range(4):
                kb.stt("dve", na_[:, h, :], A0[:, h, :], nbcol[:, h:h + 1], Ds[d][0][:, h, :], ALU.mult, ALU.mult)
            kb.tt("dve", attnT[d][b][:], A1, DT[d][0][:], ALU.mult)
            yield
            bc = lambda m: m[:].unsqueeze(1).to_broadcast(B4)
            mp = MP[d][0]
            for h in range(4):
                kb.tr(PZ[:, h, :], na_[:, h, :], idf[:])
            kb.tt("pool", mp[:, :, 1, :], na_[:], bc(Dg), ALU.mult)
            kb.tt("pool", N1[d][0][:], na_[:], bc(O1[d]), ALU.mult)
            kb.tt("pool", N2[d][0][:], na_[:], bc(O2[d]), ALU.mult)
            yield
            kb.tt("dve", mp[:, :, 0, :], PZ[:], bc(Dg), ALU.mult)
            kb.tt("dve", N1T[d][0][:], PZ[:], bc(O1T[d]), ALU.mult)
            yield
            kb.tt("pool", Tm[d][0][:], mp[:, :, 1, :], bc(idf), ALU.add)
            kb.tt("pool", Rm[d][0][:], mp[:, :, 0, :], bc(idf), ALU.add)
            yield
            cur = 0
            for it in range(1, 5):
                mpn = MP[d][1 - cur]
                mpc = MP[d][cur]
                for h in range(4):
                    kb.mm(PY[:, h, 0, :], mpc[:, h, 1, :], mpc[:, h, 0, :], start=True, stop=True)
                    kb.mm(PY[:, h, 1, :], mpc[:, h, 0, :], mpc[:, h, 1, :], start=True, stop=True)
                yield
                kb.copy("act", mpn[:, 0:2], PY[:, 0:2])
                kb.copy("act", mpn[:, 2:4], PY[:, 2:4])
                yield
                rn, rc_ = Rm[d][1 - cur], Rm[d][cur]
                tn, tc_ = Tm[d][1 - cur], Tm[d][cur]
                for h in range(4):
                    kb.mm(PZ[:, h, :], mpn[:, h, 1, :], rc_[:, h, :], start=True, stop=True)
                for h in range(4):
                    kb.mm(PY[:, h, 0, :], mpn[:, h, 0, :], tc_[:, h, :], start=True, stop=True)
                yield
                kb.tt("dve", rn[:], PZ[:], rc_[:], ALU.add)
                kb.tt("dve", tn[:], PY[:, :, 0, :], tc_[:], ALU.add)
                yield
                cur = 1 - cur
            R0_, T0_ = Rm[d][cur], Tm[d][cur]
            R1_, T1_ = Rm[d][1 - cur], Tm[d][1 - cur]
            for h in range(4):
                kb.mm(PZ[:, h, :], N1T[d][0][:, h, :], T0_[:, h, :], start=True, stop=True)
            for h in range(4):
                kb.mm(PY[:, h, 0, :], N1[d][0][:, h, :], R0_[:, h, :], start=True, stop=True)
            yield
            kb.copy("act", Yb[d][0][:], PZ[:])
            kb.copy("act", Xb[d][0][:], PY[:, :, 0, :])
            yield
            for h in range(4):
                kb.mm(PZ[:, h, :], R0_[:, h, :], Yb[d][0][:, h, :], start=True, stop=True)
            for h in range(4):
                kb.mm(PY[:, h, 0, :], T0_[:, h, :], Xb[d][0][:, h, :], start=True, stop=True)
            yield
            kb.tt("dve", T1_[:], PZ[:], T0_[:], ALU.add)
            kb.tt("dve", R1_[:], PY[:, :, 0, :], R0_[:], ALU.add)
            yield
            for h in range(4):
                kb.mm(PZ[:, h, :], N2[d][0][:, h, :], R1_[:, h, :], start=True, stop=True)
            yield
            kb.copy("act", X2b[d][0][:], PZ[:])
            yield
            for h in range(4):
                kb.mm(PZ[:, h, :], T1_[:, h, :], X2b[d][0][:, h, :], start=True, stop=True)
            yield
            r = Rb[d][0]
            kb.tt("dve", r[:], PZ[:], R1_[:], ALU.add)
            kb.tt("pool", bV[d][0][:], kv[:, 1, :].rearrange("p (h v) -> p h v", h=4), bcol.unsqueeze(2).to_broadcast(B4), ALU.mult)
            kb.tt("pool", Kbe[d][0][:], kv[:, 0, :].rearrange("p (h v) -> p h v", h=4), be[d][0][:].unsqueeze(2).to_broadcast(B4), ALU.mult)
            kb.tt("pool", Kd[d][b][:], kv[:, 0, :].rearrange("p (h v) -> p h v", h=4), e3[:, 1, :].unsqueeze(2).to_broadcast(B4), ALU.mult)
            kb.tt("dve", QdT[d][b][:], qk[:, 0:4, :], Er[d][0][:], ALU.mult)
            yield
            for h in range(4):
                kb.mm(PZ[:, h, :], r[:, h, :], bV[d][0][:, h, :], start=True, stop=True)
            for h in range(4):
                kb.mm(A0[:, h, :], Kbe[d][0][:, h, :], r[:, h, :], start=True, stop=True)
            yield
            kb.copy("act", U[d][b][:], PZ[:])
            kb.copy("act", WT[d][b][:], A0)
            yield

        def scan(s, d):
            t = order[d][s]
            b = s % 2
            tok = slice(t * 128, (t + 1) * 128)
            for h in range(4):
                kb.mm(pW[:, h, :], WT[d][b][:, h, :], Sb[d][:, h, :], start=True, stop=True)
            yield
            kb.tt("dve", vnew[d][0][:], U[d][b][:], pW[:], ALU.subtract)
            yield
            for h in range(4):
                kb.mm(pOo[:, h, :], Sb[d][:, h, :], QdT[d][b][:, h, :], start=True, stop=False)
                kb.mm(pOo[:, h, :], vnew[d][0][:, h, :], attnT[d][b][:, h, :], start=False, stop=True)
            for h in range(4):
                kb.mm(pW[:, h, :], Kd[d][b][:, h, :], vnew[d][0][:, h, :], start=True, stop=True)
            yield
            kb.copy("act", osb[d][b][:], pOo[:])
            kb.dma("pool", OT[d][:, :, tok], osb[d][b][:])
            for h in range(4):
                kb.stt("dve", S[d][:, h, :], S[d][:, h, :], E3[d][b][:, 2, h:h + 1], pW[:, h, :], ALU.mult, ALU.add)
            yield
            kb.copy("act", Sb[d][:], S[d][:])
            yield

        def run(gens):
            gens = list(gens)
            while gens:
                for g_ in list(gens):
                    try:
                        next(g_)
                    except StopIteration:
                        gens.remove(g_)

        run([pre(0, 0), pre(0, 1)])
        for s in range(NT):
            def scans(s_):
                yield from scan(s_, 0)
                yield from scan(s_, 1)
            def limited(g_, n_):
                for _ in range(n_):
                    try:
                        next(g_)
                    except StopIteration:
                        return
                    yield
            gl = [scans(s), limited(castg, 6)]
            if s + 1 < NT:
                gl += [pre(s + 1, 0), pre(s + 1, 1)]
            run(gl)
        for _ in castg:
            pass


def stage_g3(kb, D, l, last):
    with kb.stage():
        meanb = kb.sb("meanb", [128, 128], BF16)
        kb.memset("dve", meanb[:], 1.0 / 128.0)
        gw = kb.sb("gw", [128, 1], F32)
        kb.dma("sp", gw[:], D["gdn_norm_w"][l].rearrange("(p o) -> p o", o=1))
        ntok_all = T_LAT if last else T_ALL
        zt = [kb.sb("zt", [128, T_ALL], BF16) for _ in range(2)]
        sz = [kb.sb("sz", [128, T_ALL], F32) for _ in range(2)]

        def mk2(name, shape, dt):
            return [[kb.sb(name, shape, dt) for _ in range(2)] for _ in range(2)]

        of = mk2("of", [128, 512], F32)
        obk = mk2("obk", [128, 512], F32)
        sq = mk2("sq", [128, 512], BF16)
        rs = mk2("rs", [128, 512], F32)
        yo = mk2("yo", [128, 512], BF16)
        pn = [kb.ps("pn", [128, 512]) for _ in range(2)]
        blocks = [(b * 512, 512) for b in range(8)] + ([] if last else [(T_LAT, 256)])

        def headgen(h, sid):
            kb.dma("sp", zt[sid][:, :ntok_all], D["FM_d"][2560 + h * 128:2560 + (h + 1) * 128, 0:ntok_all])
            yield
            for c0 in range(0, ntok_all, 1088):
                c1 = min(ntok_all, c0 + 1088)
                kb.act(sz[sid][:, c0:c1], zt[sid][:, c0:c1], AF.Silu)
                yield
            for bi, (tok0, n) in enumerate(blocks):
                j = bi % 2
                o_, ob_, sq_, rs_, yo_ = of[sid][j], obk[sid][j], sq[sid][j], rs[sid][j], yo[sid][j]
                kb.dma("sp", o_[:, :n], D["oT_d"][0, h * 128:(h + 1) * 128, tok0:tok0 + n])
                kb.dma("sp", ob_[:, :n], D["oT_d"][1, h * 128:(h + 1) * 128, tok0:tok0 + n])
                yield
                kb.tt("pool", o_[:, :n], o_[:, :n], ob_[:, :n], ALU.add)
                yield
                kb.tt("pool", sq_[:, :n], o_[:, :n], o_[:, :n], ALU.mult)
                yield
                kb.mm(pn[sid][:, :n], meanb[:], sq_[:, :n])
                yield
                kb.act(rs_[:, :n], pn[sid][:, :n], AF.Ln, bias=NORM_EPS)
                kb.act(rs_[:, :n], rs_[:, :n], AF.Exp, scale=-0.5)
                yield
                kb.tt("dve", o_[:, :n], o_[:, :n], rs_[:, :n], ALU.mult)
                yield
                kb.stt("dve", yo_[:, :n], o_[:, :n], gw[:, 0:1], sz[sid][:, tok0:tok0 + n], ALU.mult, ALU.mult)
                kb.dma("pool", D["aT_d"][512 + h * 128:512 + (h + 1) * 128, tok0:tok0 + n], yo_[:, :n])
                yield

        for h0 in range(0, 4, 2):
            run_gens([headgen(h0, 0), headgen(h0 + 1, 1)])


def layer_norm_gen(kb, r, out, stt_, mv, rs, gam, bet):
    for hh in range(2):
        kb.op("dve", lambda g: g.bn_stats(out=stt_[:, hh, :], in_=r[:, hh * 512:(hh + 1) * 512]), [r[:]], [stt_[:]])
    kb.op("dve", lambda g: g.bn_aggr(out=mv[:], in_=stt_[:]), [stt_[:]], [mv[:]])
    yield
    kb.act(rs[:, 0:1], mv[:, 1:2], AF.Ln, bias=LN_EPS)
    kb.act(rs[:, 0:1], rs[:, 0:1], AF.Exp, scale=-0.5)
    yield
    kb.stt("dve", rs[:, 1:2], mv[:, 0:1], -1.0, rs[:, 0:1], ALU.mult, ALU.mult)
    yield
    kb.act(out[:], r[:], AF.Identity, scale=rs[:, 0:1], bias=rs[:, 1:2])
    yield
    kb.tt("pool", out[:], out[:], gam[:], ALU.mult)
    kb.tt("pool", out[:], out[:], bet[:], ALU.add)
    yield


def layer_norm_tile(kb, r, out, stt_, mv, rs, gam, bet):
    for _ in layer_norm_gen(kb, r, out, stt_, mv, rs, gam, bet):
        pass


def stage4a(kb, D, l, xsrc, last):
    with kb.stage():
        wo = kb.sb("wo", [128, 8, DM], BF16)
        for kc in range(8):
            kb.dma("sp", wo[:, kc, :], D["woutb"][l, kc * 128:(kc + 1) * 128, :])
        idb = cload(kb, D, "ident", BF16)
        lg = bvec(kb, "ln1g", D["ln1_g"][l])
        lb = bvec(kb, "ln1b", D["ln1_b"][l])
        mods = {}
        for r_, nm in ((0, "l"), (1, "c")):
            if nm == "c" and last:
                continue
            mods[nm] = (bvec(kb, "g1", D["mod_d"][l, r_, 2048:3072]), bvec(kb, "sc2", D["mod_d"][l, r_, 4096:5120]),
                        bvec(kb, "sh2", D["mod_d"][l, r_, 3072:4096]))
        aT = [kb.sb("aT", [128, 8, 512], BF16) for _ in range(2)]
        NS = 4
        xt = [kb.sb("xt", [128, DM], F32) for _ in range(NS)]
        rr = [kb.sb("rr", [128, DM], F32) for _ in range(NS)]
        x1 = [kb.sb("x1", [128, DM], F32) for _ in range(NS)]
        hb = [kb.sb("hb", [128, DM], BF16) for _ in range(NS)]
        hT = [kb.sb("hT", [128, 8, 512], BF16) for _ in range(2)]
        stt_ = [kb.sb("stt", [128, 2, 6], F32) for _ in range(NS)]
        mv = [kb.sb("mv", [128, 2], F32) for _ in range(NS)]
        rs = [kb.sb("rs", [128, 2], F32) for _ in range(NS)]
        py = [kb.ps("py", [128, 512]) for _ in range(NS)]
        pT = [kb.ps("pT", [128, 1024], BF16) for _ in range(NS)]
        aTv = D["aT_d"].rearrange("(k p) t -> p k t", p=128)
        hTv = D["h2T_d"].rearrange("(k p) t -> p k t", p=128)
        for g in range(9):
            if g == 8 and last:
                continue
            ntile = 4 if g < 8 else 2
            ntok = ntile * 128
            tok0 = g * 512
            g1, sc2, sh2 = mods["l"] if g < 8 else mods["c"]
            a_ = aT[g % 2]
            hTg = hT[g % 2]
            kb.dma("sp", a_[:, :, :ntok], aTv[:, :, tok0:tok0 + ntok])

            def tilegen(tt_, j):
                rows = slice(tok0 + tt_ * 128, tok0 + (tt_ + 1) * 128)
                kb.dma("sp", xt[j][:], xsrc[rows, :])
                for hh in range(2):
                    cs_ = slice(hh * 512, (hh + 1) * 512)
                    for kc in range(8):
                        kb.mm(py[j][:], a_[:, kc, tt_ * 128:(tt_ + 1) * 128], wo[:, kc, cs_], start=(kc == 0), stop=(kc == 7))
                    yield
                    kb.tt("dve", rr[j][:, cs_], py[j][:], g1[:, cs_], ALU.mult)
                    yield
                kb.stt("dve", rr[j][:], xt[j][:], ALPHA, rr[j][:], ALU.mult, ALU.add)
                yield
                yield from layer_norm_gen(kb, rr[j], x1[j], stt_[j], mv[j], rs[j], lg, lb)
                kb.dma("pool", D["X1_d"][rows, :], x1[j][:])
                kb.tt("dve", rr[j][:], x1[j][:], sc2[:], ALU.mult)
                kb.tt("dve", hb[j][:], rr[j][:], sh2[:], ALU.add)
                yield
                for kc in range(8):
                    kb.tr(pT[j][:, kc * 128:(kc + 1) * 128], hb[j][:, kc * 128:(kc + 1) * 128], idb[:])
                yield
                kb.act(hTg[:, :, tt_ * 128:(tt_ + 1) * 128], pT[j][:].rearrange("p (k t) -> p k t", k=8), AF.Copy)
                yield

            run_gens([tilegen(t_, t_) for t_ in range(ntile)])
            kb.dma("pool", hTv[:, :, tok0:tok0 + ntok], hTg[:, :, :ntok])


def stage4b(kb, D, l, last):
    ngroups = 8 if last else 9
    m1v = D["m1T_d"].rearrange("(j p) t -> p j t", p=128)
    hTv = D["h2T_d"].rearrange("(k p) t -> p k t", p=128)
    with kb.stage():
        w1 = kb.sb("w1", [128, 8, 4 * DM], BF16)
        for kc in range(8):
            kb.dma("sp", w1[:, kc, :], D["w1b"][l, kc * 128:(kc + 1) * 128, :])
        hT = [kb.sb("hT", [128, 8, 512], BF16) for _ in range(2)]
        rl = [kb.sb("rl", [128, 512], F32) for _ in range(4)]
        mo = [kb.sb("mo", [128, 8, 512], BF16) for _ in range(3)]
        pm = [kb.ps("pm", [128, 512]) for _ in range(4)]
        mi = 0
        oi = 0
        for g in range(ngroups):
            ntok = 512 if g < 8 else 256
            tok0 = g * 512
            h_ = hT[g % 2]
            kb.dma("sp", h_[:, :, :ntok], hTv[:, :, tok0:tok0 + ntok])
            for j in range(32):
                p = pm[mi % 4]
                r_ = rl[mi % 4]
                if j % 8 == 0:
                    m_ = mo[oi % 3]
                    oi += 1
                for kc in range(8):
                    kb.mm(p[:, :ntok], w1[:, kc, j * 128:(j + 1) * 128], h_[:, kc, :ntok], start=(kc == 0), stop=(kc == 7))
                kb.act(r_[:, :ntok], p[:, :ntok], AF.Relu)
                kb.tt("dve" if mi % 2 == 0 else "pool", m_[:, j % 8, :ntok], r_[:, :ntok], r_[:, :ntok], ALU.mult)
                mi += 1
                if j % 8 == 7:
                    kb.dma("pool", m1v[:, j - 7:j + 1, tok0:tok0 + ntok], m_[:, :, :ntok])
    with kb.stage():
        w2 = kb.sb("w2", [128, 32, DM], BF16)
        w2v = D["w2b"][l].rearrange("(j p) c -> p j c", p=128)
        for j0 in range(0, 32, 8):
            kb.dma("sp", w2[:, j0:j0 + 8, :], w2v[:, j0:j0 + 8, :])
        lg = bvec(kb, "ln2g", D["ln2_g"][l])
        lb = bvec(kb, "ln2b", D["ln2_b"][l])
        g2 = {"l": bvec(kb, "g2", D["mod_d"][l, 0, 5120:6144])}
        if not last:
            g2["c"] = bvec(kb, "g2c", D["mod_d"][l, 1, 5120:6144])
        mt = [kb.sb("mt", [128, 32, 512], BF16) for _ in range(2)]
        x1 = [kb.sb("x1", [128, DM], F32) for _ in range(2)]
        rr = [kb.sb("rr", [128, DM], F32) for _ in range(2)]
        x2 = [kb.sb("x2", [128, DM], F32) for _ in range(2)]
        stt_ = [kb.sb("stt", [128, 2, 6], F32) for _ in range(2)]
        mv = [kb.sb("mv", [128, 2], F32) for _ in range(2)]
        rs = [kb.sb("rs", [128, 2], F32) for _ in range(2)]
        po = [kb.ps("po", [128, 512]) for _ in range(4)]
        dst = D["y"] if last else D["X_d"]
        ui = 0
        ti = 0
        for g in range(ngroups):
            ntile = 4 if g < 8 else 2
            ntok = ntile * 128
            tok0 = g * 512
            m_ = mt[g % 2]
            for j0 in range(0, 32, 8):
                kb.dma("sp", m_[:, j0:j0 + 8, :ntok], m1v[:, j0:j0 + 8, tok0:tok0 + ntok])
            gg = g2["l"] if g < 8 else g2["c"]
            for t in range(ntile):
                i = ti % 2
                ti += 1
                rows = slice(tok0 + t * 128, tok0 + (t + 1) * 128)
                kb.dma("sp", x1[i][:], D["X1_d"][rows, :])
                for hh in range(2):
                    p = po[ui % 4]
                    ui += 1
                    cs_ = slice(hh * 512, (hh + 1) * 512)
                    for j in range(32):
                        kb.mm(p[:], m_[:, j, t * 128:(t + 1) * 128], w2[:, j, cs_], start=(j == 0), stop=(j == 31))
                    kb.tt("dve", rr[i][:, cs_], p[:], gg[:, cs_], ALU.mult)
                kb.stt("dve", rr[i][:], x1[i][:], ALPHA, rr[i][:], ALU.mult, ALU.add)
                layer_norm_tile(kb, rr[i], x2[i], stt_[i], mv[i], rs[i], lg, lb)
                kb.dma("pool", dst[rows, :], x2[i][:])


_CACHE = {}


def host_inputs(b, inp, shared):
    m = dict(shared)
    m["xin"] = np.ascontiguousarray(np.concatenate([inp["x"][b], inp["ctx"][b]], axis=0), dtype=np.float32)
    m["cvec"] = np.ascontiguousarray(np.stack([inp["c"][b], inp["c_ctx"]], axis=0), dtype=np.float32)
    return m


def shared_inputs(inp):
    f = lambda a: np.ascontiguousarray(a, dtype=np.float32)
    sh = {"consts": make_consts(), "rope": make_rope(), "Gp": make_rpb_gather(np.asarray(inp["rpb"], np.float32))}
    for k in ("w_ada", "b_ada", "w_in", "conv_w", "gdn_norm_w", "w_out", "ln1_g", "ln1_b", "w_mlp1", "w_mlp2", "ln2_g", "ln2_b"):
        sh[k] = f(inp[k])
    sh["a_log"] = f(inp["a_log"]).reshape(DEPTH, 8)
    sh["dt_bias"] = f(inp["dt_bias"]).reshape(DEPTH, 8)
    return sh


def kernel(**inputs):
    inp = {k: np.asarray(v) for k, v in inputs.items()}
    if "nc" not in _CACHE:
        _CACHE["nc"] = build()
    nc = _CACHE["nc"]
    shared = shared_inputs(inp)
    in_maps = [host_inputs(b, inp, shared) for b in range(8)]
    res = run_bass_kernel_spmd(nc, in_maps, core_ids=list(range(8)))
    return np.stack([np.asarray(r["y"], dtype=np.float32) for r in res.results], axis=0)
```

```python
import math
from contextlib import ExitStack, contextmanager
import numpy as np
import concourse.bass as bass
import concourse.mybir as mybir
from concourse.bass_utils import run_bass_kernel_spmd

F32 = mybir.dt.float32
BF16 = mybir.dt.bfloat16
AF = mybir.ActivationFunctionType
ALU = mybir.AluOpType

N_DMA_SEMS = 24
SAME_ENGINE_SYNC = {"pe": False, "act": True, "dve": True, "pool": True, "sp": True}

DEPTH = 4
T_LAT = 4096
T_CTX = 256
T_ALL = T_LAT + T_CTX
NT = T_ALL // 128
DM = 1024
IN_COLS = 3600
BIG = 30000.0
ALPHA = (2 * DEPTH) ** 0.25
LN_EPS = 1e-5
NORM_EPS = 1e-6


class Tk:
    __slots__ = ("w", "r")

    def __init__(self):
        self.w = None
        self.r = {}


class KB:
    def __init__(self, nc, st):
        self.nc = nc
        self.E = {"pe": nc.tensor, "act": nc.scalar, "dve": nc.vector, "pool": nc.gpsimd, "sp": nc.sync}
        self.sem = {}
        self.cnt = {}
        for e in self.E:
            self.sem[e] = st.enter_context(nc.semaphore("s_" + e))
            self.cnt[e] = 0
        for q in ("sp", "pool", "act"):
            for i in range(N_DMA_SEMS):
                k = "d%s%d" % (q, i)
                self.sem[k] = st.enter_context(nc.semaphore("s_" + k))
                self.cnt[k] = 0
        self.known = {e: {} for e in self.E}
        self.dma_rr = {"sp": 0, "pool": 0, "act": 0}
        self.n_ins = 0
        self.tk = {}
        self.psum = set()
        self.cur = None
        self.uid = 0

    @contextmanager
    def stage(self):
        prev = self.cur
        with ExitStack() as s:
            self.cur = s
            yield s
            self.barrier()
        self.cur = prev

    def sb(self, name, shape, dt):
        self.uid += 1
        nm = "%s_%d" % (name, self.uid)
        t = self.cur.enter_context(self.nc.sbuf_tensor(nm, list(shape), dt))
        self.tk[nm] = Tk()
        return t

    def ps(self, name, shape, dt=F32):
        self.uid += 1
        nm = "%s_%d" % (name, self.uid)
        t = self.cur.enter_context(self.nc.psum_tensor(nm, list(shape), dt))
        self.tk[nm] = Tk()
        self.psum.add(nm)
        return t

    def _split(self, ins, outs):
        reads, writes = [], []
        for a in ins:
            n = a.name
            if n in self.tk:
                (writes if n in self.psum else reads).append(self.tk[n])
        for a in outs:
            n = a.name
            if n in self.tk:
                writes.append(self.tk[n])
        return reads, writes

    def _deps(self, reads, writes):
        deps = {}
        for t in reads:
            if t.w is not None and t.w[1] > deps.get(t.w[0], 0):
                deps[t.w[0]] = t.w[1]
        for t in writes:
            if t.w is not None and t.w[1] > deps.get(t.w[0], 0):
                deps[t.w[0]] = t.w[1]
            for s, c in t.r.items():
                if c > deps.get(s, 0):
                    deps[s] = c
        return deps

    def _wait(self, e, deps):
        eng = self.E[e]
        kn = self.known[e]
        for s, c in deps.items():
            if kn.get(s, 0) >= c:
                continue
            if s == e and not SAME_ENGINE_SYNC[e]:
                continue
            eng.wait_ge(self.sem[s], c)
            kn[s] = c

    def _mark(self, tag, reads, writes):
        s, c = tag
        for t in reads:
            if t.r.get(s, 0) < c:
                t.r[s] = c
        for t in writes:
            t.w = tag
            t.r = {}

    def op(self, e, fn, ins=(), outs=()):
        reads, writes = self._split(ins, outs)
        self._wait(e, self._deps(reads, writes))
        ins_ = fn(self.E[e])
        self.cnt[e] += 1
        ins_.then_inc(self.sem[e], 1)
        self._mark((e, self.cnt[e]), reads, writes)
        self.n_ins += 1

    def dma(self, q, out, in_, **kw):
        reads, writes = self._split([in_], [out])
        k = "d%s%d" % (q, self.dma_rr[q])
        self.dma_rr[q] = (self.dma_rr[q] + 1) % N_DMA_SEMS
        deps = self._deps(reads, writes)
        if self.cnt[k] > deps.get(k, 0):
            deps[k] = self.cnt[k]
        self._wait(q, deps)
        ins_ = self.E[q].dma_start(out=out, in_=in_, **kw)
        self.cnt[k] += 16
        ins_.then_inc(self.sem[k], 16)
        self._mark((k, self.cnt[k]), reads, writes)
        self.n_ins += 1

    def barrier(self):
        for e in self.E:
            self._wait(e, {s: c for s, c in self.cnt.items() if c > 0 and s != e})

    def mm(self, out, lhsT, rhs, start=True, stop=True):
        self.op("pe", lambda e: e.matmul(out, lhsT=lhsT, rhs=rhs, start=start, stop=stop), [lhsT, rhs], [out])

    def tr(self, out, in_, ident):
        self.op("pe", lambda e: e.transpose(out, in_, ident), [in_, ident], [out])

    def act(self, out, in_, func, scale=None, bias=None):
        kw = {}
        ins = [in_]
        if scale is not None:
            kw["scale"] = scale
            if not isinstance(scale, (int, float)):
                ins.append(scale)
        if bias is not None:
            kw["bias"] = bias
            if not isinstance(bias, (int, float)):
                ins.append(bias)
        self.op("act", lambda e: e.activation(out=out, in_=in_, func=func, **kw), ins, [out])

    def copy(self, e, out, in_):
        if e == "act":
            self.act(out, in_, AF.Copy)
        else:
            self.op(e, lambda g: g.tensor_copy(out=out, in_=in_), [in_], [out])

    def tt(self, e, out, a, b, op):
        self.op(e, lambda g: g.tensor_tensor(out=out, in0=a, in1=b, op=op), [a, b], [out])

    def ts(self, e, out, a, s1, op0, s2=None, op1=None):
        ins = [a] + [s for s in (s1, s2) if s is not None and not isinstance(s, (int, float))]
        if op1 is None:
            self.op(e, lambda g: g.tensor_scalar(out=out, in0=a, scalar1=s1, scalar2=None, op0=op0), ins, [out])
        else:
            self.op(e, lambda g: g.tensor_scalar(out=out, in0=a, scalar1=s1, scalar2=s2, op0=op0, op1=op1), ins, [out])

    def stt(self, e, out, a, scalar, b, op0, op1):
        ins = [a, b] + ([] if isinstance(scalar, (int, float)) else [scalar])
        self.op(e, lambda g: g.scalar_tensor_tensor(out=out, in0=a, scalar=scalar, in1=b, op0=op0, op1=op1), ins, [out])

    def memset(self, e, out, val):
        self.op(e, lambda g: g.memset(out, val), [], [out])


CONST_LAYOUT = {}


def make_consts():
    P = np.arange(128)
    parts = []
    off = 0

    def add(name, arr):
        nonlocal off
        arr = np.asarray(arr, np.float32)
        CONST_LAYOUT[name] = (off, arr.shape[1])
        parts.append(arr)
        off += arr.shape[1]

    add("ident", np.eye(128))
    add("U1", P[:, None] <= P[None, :])
    add("U1T", P[:, None] >= P[None, :])
    add("SL", P[:, None] > P[None, :])
    add("SU", P[:, None] < P[None, :])
    add("ones", np.ones((128, 128)))
    negf = np.where(P[:, None] > P[None, :], 0.0, -BIG)
    negb = np.where(P[:, None] < P[None, :], 0.0, -BIG)
    negtf = np.where(P[None, :] >= P[:, None], 0.0, -BIG)
    negtb = np.where(P[None, :] <= P[:, None], 0.0, -BIG)
    add("NEGf", np.tile(negf, (1, 4)))
    add("NEGb", np.tile(negb, (1, 4)))
    add("NEGTf", np.tile(negtf, (1, 4)))
    add("NEGTb", np.tile(negtb, (1, 4)))
    pi = np.where((P % 64) < 32, P + 32, P - 32)
    rp = np.zeros((128, 128))
    rp[pi, P] = 1.0
    add("Rperm", rp)
    kc = P % 64
    par = P // 64
    qc = np.arange(64)
    cs = np.clip(qc - 8, 0, 48)
    colmask = np.where((kc[:, None] >= cs[None, :]) & (kc[:, None] < cs[None, :] + 16), 0.0, -BIG)
    add("colmask", colmask)
    m9 = np.zeros((128, 9, 64))
    m9[par == 1, 0, :] = -BIG
    m9[par == 0, 8, :] = -BIG
    add("M9", m9.reshape(128, 576))
    dg32 = (P[:, None] // 32 == P[None, :] // 32)
    off1 = (P[:, None] // 64 == P[None, :] // 64) & (P[:, None] // 32 > P[None, :] // 32)
    off2 = (P[:, None] // 64 > P[None, :] // 64)
    add("Dg32", dg32)
    add("Off1", off1)
    add("Off1T", off1.T)
    add("Off2", off2)
    add("Off2T", off2.T)
    return np.concatenate(parts, axis=1).astype(np.float32)


def make_rope():
    t = np.arange(T_LAT)
    row = (t // 64).astype(np.float32)
    col = (t % 64).astype(np.float32)
    inv = (10000.0 ** (-np.arange(0, 64, 2, dtype=np.float32) / 64)).astype(np.float32)
    i = np.arange(128)
    f = i % 32
    pos = np.where((i < 64)[:, None], row[None, :], col[None, :]).astype(np.float32)
    ang = (pos * inv[f][:, None]).astype(np.float32)
    sgn = np.where((i % 64) < 32, -1.0, 1.0)[:, None]
    out = np.stack([np.cos(ang), sgn * np.sin(ang)], axis=1)
    return out.astype(np.float32)


def make_rpb_gather(rpb):
    P = np.arange(128)
    kc = (P % 64)[:, None, None]
    par = (P // 64)[:, None, None]
    ep = np.arange(16)[None, :, None]
    qc = np.arange(64)[None, None, :]
    dr = 7 - ep + par + 0 * qc
    dc = kc - qc + 0 * ep
    valid = (np.abs(dr) <= 7) & (np.abs(dc) <= 15)
    dri = np.clip(dr + 7, 0, 14)
    dci = np.clip(dc + 15, 0, 30)
    g = rpb[:, :, dri, dci]
    g = np.where(valid[None, None], g, np.float32(0.0))
    return np.ascontiguousarray(g.reshape(rpb.shape[0], rpb.shape[1], 128, 1024).astype(np.float32))


def build(n_layers=DEPTH, dbg=(), stages=None):
    make_consts()
    nc = bass.Bass("TRN2", target_bir_lowering=False)
    D = {}

    def din(name, shape, dt=F32):
        D[name] = nc.dram_tensor(name, list(shape), dt, kind="ExternalInput").ap()

    def dscr(name, shape, dt):
        kind = "ExternalOutput" if name in dbg else "Internal"
        D[name] = nc.dram_tensor(name, list(shape), dt, kind=kind).ap()

    ncst = sum(v[1] for v in CONST_LAYOUT.values())
    din("xin", [T_ALL, DM]); din("cvec", [2, DM]); din("consts", [128, ncst]); din("rope", [128, 2, T_LAT])
    din("Gp", [DEPTH, 8, 128, 1024])
    din("w_ada", [DEPTH, DM, 6 * DM]); din("b_ada", [DEPTH, 6 * DM]); din("w_in", [DEPTH, DM, IN_COLS])
    din("conv_w", [DEPTH, 5, 1536]); din("a_log", [DEPTH, 8]); din("dt_bias", [DEPTH, 8]); din("gdn_norm_w", [DEPTH, 128])
    din("w_out", [DEPTH, DM, DM]); din("ln1_g", [DEPTH, DM]); din("ln1_b", [DEPTH, DM])
    din("w_mlp1", [DEPTH, DM, 4 * DM]); din("w_mlp2", [DEPTH, 4 * DM, DM]); din("ln2_g", [DEPTH, DM]); din("ln2_b", [DEPTH, DM])
    D["y"] = nc.dram_tensor("y", [T_LAT, DM], F32, kind="ExternalOutput").ap()

    dscr("winb", [DEPTH, DM, IN_COLS], BF16); dscr("woutb", [DEPTH, DM, DM], BF16)
    dscr("w1b", [DEPTH, DM, 4 * DM], BF16); dscr("w2b", [DEPTH, 4 * DM, DM], BF16)
    dscr("mod_d", [DEPTH, 2, 6 * DM], F32)
    dscr("FM_d", [3072, T_ALL], BF16)
    dscr("v_d", [T_ALL, 512], BF16); dscr("ba_d", [T_ALL, 16], F32)
    dscr("GQK_d", [1024, T_ALL], BF16)
    dscr("Ktok_d", [T_ALL, 512], BF16); dscr("Vtok_d", [T_ALL, 512], BF16)
    dscr("oT_d", [2, 512, T_ALL], F32)
    dscr("aT_d", [1024, T_ALL], BF16)
    dscr("X_d", [T_ALL, DM], F32); dscr("X1_d", [T_ALL, DM], F32); dscr("h2T_d", [DM, T_ALL], BF16)
    dscr("m1T_d", [4 * DM, T_ALL], BF16)

    with ExitStack() as st:
        kb = KB(nc, st)
        bg = (stages is None) or ("g2" in stages)
        if stages is None or "cast" in stages:
            items = [("w_in", "winb", 0, None)]
            if not bg:
                for l_ in range(n_layers):
                    items += layer_cast_items(l_, n_layers)
            cast_weights(kb, D, items)
        if stages is None or "mod" in stages:
            mod_stage(kb, D, n_layers)
        for l in range(n_layers):
            last = (l == DEPTH - 1)
            xsrc = D["xin"] if l == 0 else D["X_d"]
            if stages is None or "s1" in stages:
                stage1(kb, D, l, xsrc)
            if stages is None or "na" in stages:
                stage_na(kb, D, l, last)
            if stages is None or "g1" in stages:
                stage_g1(kb, D, l)
            if stages is None or "g2" in stages:
                stage_g2(kb, D, l, layer_cast_items(l, n_layers))
            if stages is None or "g3" in stages:
                stage_g3(kb, D, l, last)
            if stages is None or "s4a" in stages:
                stage4a(kb, D, l, xsrc, last)
            if stages is None or "s4b" in stages:
                stage4b(kb, D, l, last)
        kb.barrier()
    nc._n_ins = kb.n_ins
    return nc


def run_gens(gens):
    gens = list(gens)
    while gens:
        for g_ in list(gens):
            try:
                next(g_)
            except StopIteration:
                gens.remove(g_)


def cload(kb, D, name, dt=F32, eng="dve", rows=128):
    off, w = CONST_LAYOUT[name]
    t = kb.sb("c_" + name, [128, w], F32)
    kb.dma("sp", t[:], D["consts"][:, off:off + w])
    if dt == F32:
        return t
    tb = kb.sb("cb_" + name, [128, w], dt)
    kb.copy(eng, tb[:], t[:])
    return tb


def cast_jobs(D, items):
    jobs = []
    for src, dst, l, w in items:
        s2 = D[src][l]
        d2 = D[dst][l]
        if w:
            s2 = s2.rearrange("(a b) c -> a (b c)", b=w)
            d2 = d2.rearrange("(a b) c -> a (b c)", b=w)
        R, C = s2.shape
        for r0 in range(0, R, 128):
            for c0 in range(0, C, 2048):
                cw = min(2048, C - c0)
                jobs.append((s2[r0:r0 + 128, c0:c0 + cw], d2[r0:r0 + 128, c0:c0 + cw], cw))
    return jobs


def layer_cast_items(l, n_layers):
    it = [("w_out", "woutb", l, 2), ("w_mlp1", "w1b", l, None), ("w_mlp2", "w2b", l, 2)]
    if l + 1 < n_layers:
        it.append(("w_in", "winb", l + 1, None))
    return it


def cast_weights(kb, D, items):
    with kb.stage():
        NB = 3
        fb = [kb.sb("cf", [128, 2048], F32) for _ in range(NB)]
        bb = [kb.sb("cb", [128, 2048], BF16) for _ in range(NB)]
        for i, (s, d, cw) in enumerate(cast_jobs(D, items)):
            b = i % NB
            kb.dma("sp", fb[b][:, :cw], s)
            kb.copy("dve" if i % 2 == 0 else "act", bb[b][:, :cw], fb[b][:, :cw])
            kb.dma("pool", d, bb[b][:, :cw])


def cast_gen(kb, D, items, fb, bb):
    NB = len(fb)
    for i, (s, d, cw) in enumerate(cast_jobs(D, items)):
        b = i % NB
        kb.dma("sp", fb[b][:, :cw], s)
        yield
        kb.copy("act", bb[b][:, :cw], fb[b][:, :cw])
        kb.dma("act", d, bb[b][:, :cw])
        yield


def mod_stage(kb, D, n_layers):
    with kb.stage():
        cT = kb.sb("cT", [128, 2, 8], F32)
        kb.dma("sp", cT[:], D["cvec"].rearrange("r (k p) -> p r k", p=128), allow_slow_non_contiguous=True)
        scl = kb.sb("scl", [128, 8, 33], F32)
        kb.memset("dve", scl[:], 0.0)
        kb.act(scl[:, :, 0], cT[:, 0, :], AF.Silu)
        kb.act(scl[:, :, 32], cT[:, 1, :], AF.Silu)
        wt = [kb.sb("wada", [128, 8, 512], F32) for _ in range(2)]
        bt = [kb.sb("bada", [33, 512], F32) for _ in range(2)]
        mr = [kb.sb("mrow", [33, 512], F32) for _ in range(2)]
        pm = [kb.ps("pm", [128, 512]) for _ in range(2)]
        for b in bt:
            kb.memset("dve", b[:], 0.0)
        i = 0
        for l in range(n_layers):
            for cb in range(12):
                j = i % 2
                i += 1
                cols = slice(cb * 512, (cb + 1) * 512)
                kb.dma("sp", wt[j][:], D["w_ada"][l, :, cols].rearrange("(k p) c -> p k c", p=128))
                kb.dma("sp", bt[j][0:1, :], D["b_ada"][l:l + 1, cols])
                kb.dma("sp", bt[j][32:33, :], D["b_ada"][l:l + 1, cols])
                for kc in range(8):
                    kb.mm(pm[j][0:33, :], scl[:, kc, :], wt[j][:, kc, :], start=(kc == 0), stop=(kc == 7))
                kb.tt("dve", mr[j][:], pm[j][0:33, :], bt[j][:], ALU.add)
                if cb in (2, 3, 8, 9):
                    kb.ts("dve", mr[j][:], mr[j][:], 1.0, ALU.add)
                kb.dma("pool", D["mod_d"][l, 0:1, cols], mr[j][0:1, :])
                kb.dma("pool", D["mod_d"][l, 1:2, cols], mr[j][32:33, :])


def bvec(kb, name, src):
    n = src.shape[-1]
    t = kb.sb(name, [128, n], F32)
    kb.dma("sp", t[:], src.partition_broadcast(128))
    return t


def stage1(kb, D, l, xsrc):
    with kb.stage():
        w = kb.sb("win", [128, 8, IN_COLS], BF16)
        for kc in range(8):
            kb.dma("sp", w[:, kc, :], D["winb"][l, kc * 128:(kc + 1) * 128, :])
        mods = {}
        for r, nm in ((0, "l"), (1, "c")):
            mods[nm] = (bvec(kb, "sc1", D["mod_d"][l, r, 1024:2048]), bvec(kb, "sh1", D["mod_d"][l, r, 0:1024]))
        idb = cload(kb, D, "ident", BF16)
        xt = [kb.sb("xt", [128, DM], F32) for _ in range(2)]
        hb = [kb.sb("hb", [128, DM], BF16) for _ in range(2)]
        hT = [kb.sb("hT", [128, 8, 512], BF16) for _ in range(2)]
        fm = [kb.sb("fm", [128, 6, 512], BF16) for _ in range(4)]
        vt = [kb.sb("vt", [128, 512], BF16) for _ in range(2)]
        bat = [kb.sb("bat", [128, 16], F32) for _ in range(2)]
        pT = [kb.ps("pT", [128, 1024], BF16) for _ in range(2)]
        pF = [kb.ps("pF", [128, 512]) for _ in range(3)]
        pV = kb.ps("pV", [128, 512])
        pB = kb.ps("pB", [128, 512])
        FMv = D["FM_d"].rearrange("(b p) t -> p b t", p=128)
        ti = 0
        ei = 0
        fi = 0
        for g in range(9):
            ntile = 4 if g < 8 else 2
            ntok = ntile * 128
            tok0 = g * 512
            sc, sh = mods["l"] if g < 8 else mods["c"]
            hTg = hT[g % 2]
            pend_tm = []
            for tt_ in range(ntile):
                j = ti % 2
                ti += 1
                rows = slice(tok0 + tt_ * 128, tok0 + (tt_ + 1) * 128)
                kb.dma("sp", xt[j][:], xsrc[rows, :])
                kb.tt("dve", xt[j][:], xt[j][:], sc[:], ALU.mult)
                kb.tt("dve", hb[j][:], xt[j][:], sh[:], ALU.add)
                for kc in range(8):
                    kb.tr(pT[j][:, kc * 128:(kc + 1) * 128], hb[j][:, kc * 128:(kc + 1) * 128], idb[:])
                kb.act(hTg[:, :, tt_ * 128:(tt_ + 1) * 128], pT[j][:].rearrange("p (k t) -> p k t", k=8), AF.Copy)

                def tokmajor(tt_=tt_, j=j, rows=rows):
                    for kc in range(8):
                        kb.mm(pV[:], hTg[:, kc, tt_ * 128:(tt_ + 1) * 128], w[:, kc, 1024:1536], start=(kc == 0), stop=(kc == 7))
                    for kc in range(8):
                        kb.mm(pB[:, 0:16], hTg[:, kc, tt_ * 128:(tt_ + 1) * 128], w[:, kc, 3584:3600], start=(kc == 0), stop=(kc == 7))
                    kb.copy("dve", vt[j][:], pV[:])
                    kb.copy("dve", bat[j][:], pB[:, 0:16])
                    kb.dma("pool", D["v_d"][rows, :], vt[j][:])
                    kb.dma("pool", D["ba_d"][rows, :], bat[j][:])

                if pend_tm:
                    pend_tm.pop(0)()
                pend_tm.append(tokmajor)
            while pend_tm:
                pend_tm.pop(0)()
            for blk in range(24):
                col0 = blk * 128 if blk < 8 else 1536 + (blk - 8) * 128
                p = pF[ei % 3]
                for kc in range(8):
                    kb.mm(p[:, :ntok], w[:, kc, col0:col0 + 128], hTg[:, kc, :ntok], start=(kc == 0), stop=(kc == 7))
                if blk % 6 == 0:
                    fcur = fm[fi % 4]
                    fi += 1
                if blk < 4:
                    kb.act(fcur[:, blk % 6, :ntok], p[:, :ntok], AF.Copy, scale=0.125)
                elif ei % 2 == 0:
                    kb.act(fcur[:, blk % 6, :ntok], p[:, :ntok], AF.Copy)
                else:
                    kb.copy("dve", fcur[:, blk % 6, :ntok], p[:, :ntok])
                ei += 1
                if blk % 6 == 5:
                    b0 = blk - 5
                    kb.dma("pool", FMv[:, b0:b0 + 6, tok0:tok0 + ntok], fcur[:, :, :ntok])


def na_pieces(c):
    out = []
    r0 = 8 * c
    for j in range(32):
        runs = []
        for r in range(r0, r0 + 8):
            rs = min(max(r - 4, 0), 56)
            if 2 * j + 1 < rs or 2 * j > rs + 7:
                continue
            edge = (r < 4) or (r > 60)
            kind = "G" if edge else "S"
            var = (7 - 2 * j + r) if edge else (r - 2 * j + 3)
            if runs and runs[-1][0] == kind and runs[-1][2] == r and runs[-1][1] + (r - runs[-1][3]) == var:
                runs[-1][2] = r + 1
            else:
                runs.append([kind, var, r + 1, r])
        for kind, var, rend, rstart in runs:
            out.append((j, (rstart - r0) * 64, (rend - rstart) * 64, (kind, var)))
    return out


def stage_na(kb, D, l, last):
    with kb.stage():
        colmask = cload(kb, D, "colmask")
        m9 = cload(kb, D, "M9")
        idf = cload(kb, D, "ident")
        vall = kb.sb("vall", [128, NT, 512], BF16)
        v3 = D["v_d"].rearrange("(t p) c -> p t c", p=128)
        for t0 in range(0, NT, 6):
            t1 = min(NT, t0 + 6)
            kb.dma("sp", vall[:, t0:t1, :], v3[:, t0:t1, :])
        vaug = [kb.sb("vaug", [128, NT, 128], BF16) for _ in range(2)]
        qk = [kb.sb("qk", [64, 2, T_ALL], BF16) for _ in range(2)]
        Gt = [kb.sb("Gt", [128, 1024], F32) for _ in range(2)]
        GM = [kb.sb("GM", [128, 1024], BF16) for _ in range(2)]
        S9 = [kb.sb("S9", [128, 576], BF16) for _ in range(2)]
        rcs = [kb.sb("rcs", [128, 512], F32) for _ in range(2)]
        for sid in range(2):
            kb.memset("pool", vaug[sid][:, :, 64:128], 1.0)
            kb.memset("dve", rcs[sid][:], 0.0)

        def mk2(name, shape, dt):
            return [[kb.sb(name, shape, dt) for _ in range(2)] for _ in range(2)]

        def mk3(name, shape, dt):
            return [[kb.sb(name, shape, dt) for _ in range(3)] for _ in range(2)]

        sc = mk3("sc", [128, 512], F32)
        pt = mk3("pt", [128, 512], BF16)
        osb = mk2("osb", [64, 512], F32)
        ob = mk2("ob", [64, 512], BF16)
        pS = [[kb.ps("pS", [128, 512]) for _ in range(3)] for _ in range(2)]
        pO = [[kb.ps("pO", [128, 512]) for _ in range(1)] for _ in range(2)]

        def headgen(h, sid):
            q_ = qk[sid]
            va = vaug[sid]
            kb.dma("sp", q_[:, 0, :], D["FM_d"][h * 64:(h + 1) * 64, :])
            kb.dma("sp", q_[:, 1, :], D["FM_d"][512 + h * 64:512 + (h + 1) * 64, :])
            kb.dma("sp", Gt[sid][:], D["Gp"][l, h])
            kb.copy("pool", va[:, :, 0:64], vall[:, :, h * 64:(h + 1) * 64])
            kb.tt("dve", GM[sid][:].rearrange("p (e q) -> p e q", e=16), Gt[sid][:].rearrange("p (e q) -> p e q", e=16),
                  colmask[:].unsqueeze(1).to_broadcast([128, 16, 64]), ALU.add)
            kb.tt("dve", S9[sid][:], GM[sid][:, 4 * 64:13 * 64], m9[:], ALU.add)
            yield
            pi = 0
            for c in range(9):
                if c == 8 and last:
                    continue
                ncols = 512 if c < 8 else 256
                q0 = c * 512
                pieces = [(32, 0, ncols, None), (33, 0, ncols, None)]
                if c < 8:
                    pieces += na_pieces(c)
                po = pO[sid][0]
                pend = []

                def pv(item):
                    k_, j_, lo_, n_, e_ = item
                    kb.mm(po[:, lo_:lo_ + n_], va[:, j_, :], e_[:, :n_], start=(k_ == 0), stop=(k_ == len(pieces) - 1))

                for k, (j, lo, n, bias) in enumerate(pieces):
                    p = pS[sid][pi % 3]
                    e = pt[sid][pi % 3]
                    s_ = sc[sid][pi % 3]
                    pi += 1
                    kb.mm(p[:, :n], q_[:, 1, j * 128:(j + 1) * 128], q_[:, 0, q0 + lo:q0 + lo + n], start=True, stop=True)
                    yield
                    if bias is not None:
                        src = S9[sid] if bias[0] == "S" else GM[sid]
                        kb.tt("dve", s_[:, :n], p[:, :n], src[:, bias[1] * 64:bias[1] * 64 + n], ALU.add)
                        yield
                        kb.act(e[:, :n], s_[:, :n], AF.Exp)
                    else:
                        kb.act(e[:, :n], p[:, :n], AF.Exp)
                    pend.append((k, j, lo, n, e))
                    if len(pend) > 2:
                        pv(pend.pop(0))
                    yield
                while pend:
                    pv(pend.pop(0))
                yield
                ci = c % 2
                r_ = rcs[sid]
                os_ = osb[sid][ci]
                o_ = ob[sid][ci]
                pb = pS[sid][pi % 3]
                pi += 1
                kb.act(r_[64:128, :ncols], po[64:128, :ncols], AF.Ln)
                kb.act(r_[64:128, :ncols], r_[64:128, :ncols], AF.Exp, scale=-1.0)
                kb.copy("act", os_[:, :ncols], po[0:64, :ncols])
                yield
                kb.mm(pb[0:64, :ncols], idf[:, 64:128], r_[:, :ncols], start=True, stop=True)
                yield
                kb.tt("dve", o_[:, :ncols], os_[:, :ncols], pb[0:64, :ncols], ALU.mult)
                kb.dma("pool", D["aT_d"][h * 64:(h + 1) * 64, q0:q0 + ncols], o_[:, :ncols])
                yield

        for h0 in range(0, 8, 2):
            run_gens([headgen(h0, 0), headgen(h0 + 1, 1)])


def stage_g1(kb, D, l):
    with kb.stage():
        idf = cload(kb, D, "ident")
        idb = kb.sb("idb", [128, 128], BF16)
        kb.copy("dve", idb[:], idf[:])
        onesb = cload(kb, D, "ones", BF16)
        rpb_ = cload(kb, D, "Rperm", BF16)
        rope = kb.sb("rope", [128, 2, T_LAT], F32)
        kb.dma("sp", rope[:, 0, :], D["rope"][:, 0, :])
        kb.dma("sp", rope[:, 1, :], D["rope"][:, 1, :])
        cw = kb.sb("cw", [128, 5, 12], F32)
        for j in range(5):
            kb.dma("sp", cw[:, j, :], D["conv_w"][l, j].rearrange("(f p) -> p f", p=128), allow_slow_non_contiguous=True)
        XW = 4360
        xin = [kb.sb("gxin", [128, XW], BF16) for _ in range(2)]
        xsh = [kb.sb("gxsh", [128, XW], BF16) for _ in range(2)]
        for x_ in xin + xsh:
            kb.memset("dve", x_[:], 0.0)
        sall = [kb.sb("sall", [128, T_ALL], F32) for _ in range(2)]
        dg = [kb.sb("dg", [128, 5, 128], BF16) for _ in range(2)]

        def mk2(name, shape, dt):
            return [[kb.sb(name, shape, dt) for _ in range(2)] for _ in range(2)]

        sq = mk2("sq", [128, 512], BF16)
        lnb = mk2("lnb", [128, 512], F32)
        qn = mk2("qn", [128, 512], BF16)
        t1 = mk2("t1", [128, 512], F32)
        t2 = mk2("t2", [128, 512], F32)
        obq = mk2("obq", [128, 512], BF16)
        tk_ = mk2("tkst", [128, 4, 128], BF16)
        pc = [kb.ps("pc", [128, 512]) for _ in range(2)]
        pn = [kb.ps("pn", [128, 512]) for _ in range(2)]
        pr = [kb.ps("pr", [128, 512]) for _ in range(2)]
        pt = [kb.ps("ptr", [128, 8, 128], BF16) for _ in range(2)]
        blocks = [(b * 512, 512, 2 + b * 512) for b in range(8)] + [(T_LAT, 256, 4102)]

        def fcgen(fc, sid):
            x_, xs_, d_, sa = xin[sid], xsh[sid], dg[sid], sall[sid]
            src = D["FM_d"][1024 + fc * 128:1024 + (fc + 1) * 128, :]
            kb.dma("sp", x_[:, 2:2 + T_LAT], src[:, 0:T_LAT])
            kb.dma("sp", x_[:, 4102:4102 + T_CTX], src[:, T_LAT:T_ALL])
            kb.dma("sp", xs_[:, 1:1 + T_LAT], src[:, 0:T_LAT])
            kb.dma("sp", xs_[:, 4101:4101 + T_CTX], src[:, T_LAT:T_ALL])
            for j in range(5):
                kb.ts("dve", d_[:, j, :], idf[:], cw[:, j, fc:fc + 1], ALU.mult)
            yield
            for (tok0, n, xc) in blocks:
                p = pc[sid]
                for j in range(5):
                    if j % 2 == 0:
                        rhs_ = x_[:, xc + j - 2:xc + j - 2 + n]
                    else:
                        rhs_ = xs_[:, xc + j - 3:xc + j - 3 + n]
                    kb.mm(p[:, :n], d_[:, j, :], rhs_, start=(j == 0), stop=(j == 4))
                yield
                kb.act(sa[:, tok0:tok0 + n], p[:, :n], AF.Silu)
                yield
            isq = fc < 4
            isv = fc >= 8
            hh = fc % 4
            for bi, (tok0, n, xc) in enumerate(blocks):
                i = bi % 2
                o_ = obq[sid][i]
                if isv:
                    kb.copy("dve", o_[:, :n], sa[:, tok0:tok0 + n])
                    yield
                else:
                    kb.tt("pool", sq[sid][i][:, :n], sa[:, tok0:tok0 + n], sa[:, tok0:tok0 + n], ALU.mult)
                    yield
                    kb.mm(pn[sid][:, :n], onesb[:], sq[sid][i][:, :n])
                    yield
                    kb.act(lnb[sid][i][:, :n], pn[sid][:, :n], AF.Ln, bias=NORM_EPS)
                    kb.act(lnb[sid][i][:, :n], lnb[sid][i][:, :n], AF.Exp, scale=-0.5, bias=(-0.5 * math.log(128.0) if isq else 0.0))
                    yield
                    if tok0 < T_LAT:
                        kb.tt("dve", qn[sid][i][:, :n], sa[:, tok0:tok0 + n], lnb[sid][i][:, :n], ALU.mult)
                        yield
                        kb.mm(pr[sid][:, :n], rpb_[:], qn[sid][i][:, :n])
                        kb.tt("pool", t1[sid][i][:, :n], qn[sid][i][:, :n], rope[:, 0, tok0:tok0 + n], ALU.mult)
                        yield
                        kb.tt("dve", t2[sid][i][:, :n], pr[sid][:, :n], rope[:, 1, tok0:tok0 + n], ALU.mult)
                        yield
                        kb.tt("dve", o_[:, :n], t1[sid][i][:, :n], t2[sid][i][:, :n], ALU.add)
                        yield
                    else:
                        kb.tt("dve", o_[:, :n], sa[:, tok0:tok0 + n], lnb[sid][i][:, :n], ALU.mult)
                        yield
                    kb.dma("pool", D["GQK_d"][fc * 128:(fc + 1) * 128, tok0:tok0 + n], o_[:, :n])
                if fc >= 4:
                    nt_ = n // 128
                    for a in range(nt_):
                        kb.tr(pt[sid][:, a, :], o_[:, a * 128:(a + 1) * 128], idb[:])
                    yield
                    kb.copy("act", tk_[sid][i][:, :nt_, :], pt[sid][:, :nt_, :])
                    dst = D["Vtok_d"] if isv else D["Ktok_d"]
                    kb.dma("pool", dst[tok0:tok0 + n, hh * 128:(hh + 1) * 128].rearrange("(a p) c -> p a c", p=128), tk_[sid][i][:, :nt_, :])
                    yield

        for fc0 in range(0, 12, 2):
            run_gens([fcgen(fc0, 0), fcgen(fc0 + 1, 1)])


def stage_g2(kb, D, l, cast_items=()):
    with kb.stage():
        cfb = [kb.sb("cf", [128, 2048], F32) for _ in range(3)]
        cbb = [kb.sb("cb", [128, 2048], BF16) for _ in range(3)]
        castg = cast_gen(kb, D, cast_items, cfb, cbb)
        idf = cload(kb, D, "ident")
        idb = kb.sb("idb", [128, 128], BF16)
        kb.copy("dve", idb[:], idf[:])
        onesf = cload(kb, D, "ones")
        Ua = [cload(kb, D, "U1"), cload(kb, D, "U1T")]
        Sa = [cload(kb, D, "SL"), cload(kb, D, "SU")]
        NEGs = [cload(kb, D, "NEGf", BF16), cload(kb, D, "NEGb", BF16)]
        NEGT = [cload(kb, D, "NEGTf", BF16), cload(kb, D, "NEGTb", BF16)]
        ba = kb.sb("ba", [128, NT, 16], F32)
        kb.dma("sp", ba[:], D["ba_d"].rearrange("(t p) c -> p t c", p=128))
        al = bvec(kb, "alog", D["a_log"][l])
        dtb = bvec(kb, "dtb", D["dt_bias"][l])
        nA = kb.sb("nA", [128, 8], F32)
        kb.act(nA[:], al[:], AF.Exp)
        kb.ts("dve", nA[:], nA[:], -1.0, ALU.mult)
        gx = kb.sb("gx", [128, NT, 8], F32)
        g_all = kb.sb("g_all", [128, NT, 8], F32)
        beta = kb.sb("beta", [128, NT, 8], F32)
        negb = kb.sb("negb", [128, NT, 8], F32)
        kb.tt("dve", gx[:], ba[:, :, 8:16], dtb[:].unsqueeze(1).to_broadcast([128, NT, 8]), ALU.add)
        kb.act(gx[:], gx[:], AF.Exp)
        kb.act(gx[:], gx[:], AF.Ln, bias=1.0)
        kb.tt("dve", g_all[:], gx[:], nA[:].unsqueeze(1).to_broadcast([128, NT, 8]), ALU.mult)
        kb.act(beta[:], ba[:, :, 0:8], AF.Sigmoid)
        kb.ts("dve", negb[:], beta[:], -1.0, ALU.mult)

        def mk(name, shape, dt, n=2):
            return [[kb.sb(name, shape, dt) for _ in range(n)] for _ in range(2)]

        QK = mk("QK", [128, 8, 128], BF16)
        KV = mk("KV", [128, 2, 512], BF16)
        gS = mk("gS", [128, 4, 128], F32, 1)
        gU = mk("gU", [128, 4, 128], F32, 1)
        Ds = mk("Ds", [128, 4, 128], F32, 1)
        DT = mk("DT", [128, 4, 128], F32, 1)
        Er = mk("Er", [128, 4, 128], F32, 1)
        E3 = mk("E3", [128, 3, 4], F32)
        be = mk("be", [128, 4], F32, 1)
        negA = mk("negA", [128, 4, 128], F32, 1)
        MP = mk("MP", [128, 4, 2, 128], BF16)
        Rm = mk("Rm", [128, 4, 128], BF16)
        Tm = mk("Tm", [128, 4, 128], BF16)
        N1 = mk("N1", [128, 4, 128], BF16, 1)
        N1T = mk("N1T", [128, 4, 128], BF16, 1)
        N2 = mk("N2", [128, 4, 128], BF16, 1)
        Yb = mk("Yb", [128, 4, 128], BF16, 1)
        Xb = mk("Xb", [128, 4, 128], BF16, 1)
        X2b = mk("X2b", [128, 4, 128], BF16, 1)
        Rb = mk("Rb", [128, 4, 128], BF16, 1)
        Dg = cload(kb, D, "Dg32")
        O1 = [cload(kb, D, "Off1"), cload(kb, D, "Off1T")]
        O1T = [O1[1], O1[0]]
        O2 = [cload(kb, D, "Off2"), cload(kb, D, "Off2T")]
        attnT = mk("attnT", [128, 4, 128], BF16)
        QdT = mk("QdT", [128, 4, 128], BF16)
        Kd = mk("Kd", [128, 4, 128], BF16)
        Kbe = mk("Kbe", [128, 4, 128], BF16, 1)
        bV = mk("bV", [128, 4, 128], BF16, 1)
        U = mk("U", [128, 4, 128], F32)
        WT = mk("WT", [128, 4, 128], BF16)
        vnew = mk("vnew", [128, 4, 128], BF16, 1)
        osb = mk("osb", [128, 4, 128], F32)
        S = [kb.sb("S", [128, 4, 128], F32) for _ in range(2)]
        Sb = [kb.sb("Sb", [128, 4, 128], BF16) for _ in range(2)]
        for d in range(2):
            kb.memset("dve", S[d][:], 0.0)
            kb.memset("pool", Sb[d][:], 0.0)
        pY = [kb.ps("pY", [128, 4, 2, 128]) for _ in range(2)]
        pZ = [kb.ps("pZ", [128, 4, 128]) for _ in range(2)]
        pW = kb.ps("pW", [128, 4, 128])
        pOo = kb.ps("pOo", [128, 4, 128])
        GQ = D["GQK_d"].rearrange("(a p) t -> p a t", p=128)
        OT = [D["oT_d"][d].rearrange("(h v) t -> v h t", v=128) for d in range(2)]

        order = [[32, 33] + list(range(32)), [33, 32] + list(range(31, -1, -1))]
        B4 = [128, 4, 128]
        f2 = lambda a: a[:].rearrange("p h t -> p (h t)")

        def pre(s, d):
            t = order[d][s]
            b = s % 2
            tok = slice(t * 128, (t + 1) * 128)
            qk = QK[d][b]
            kv = KV[d][b]
            PY = pY[d]
            PZ = pZ[d]
            PYf = PY[:].rearrange("p h w t -> p (h w t)")
            A0f = PYf[:, 0:512]
            A1f = PYf[:, 512:1024]
            A0 = A0f.rearrange("p (h t) -> p h t", h=4)
            A1 = A1f.rearrange("p (h t) -> p h t", h=4)
            kb.dma("sp", qk[:], GQ[:, :, tok])
            kb.dma("sp", kv[:, 0, :], D["Ktok_d"][tok, :])
            kb.dma("sp", kv[:, 1, :], D["Vtok_d"][tok, :])
            gcol = g_all[:, t, d * 4:(d + 1) * 4]
            bcol = beta[:, t, d * 4:(d + 1) * 4]
            nbcol = negb[:, t, d * 4:(d + 1) * 4]
            kb.tt("dve", gS[d][0][:], gcol.unsqueeze(2).to_broadcast(B4), Sa[d][:].unsqueeze(1).to_broadcast(B4), ALU.mult)
            kb.tt("pool", gU[d][0][:], gcol.unsqueeze(2).to_broadcast(B4), Ua[d][:].unsqueeze(1).to_broadcast(B4), ALU.mult)
            yield
            kb.mm(A0f, Ua[d][:], f2(gS[d][0]), start=True, stop=False)
            kb.mm(A0f, idb[:], NEGs[d][:], start=False, stop=True)
            kb.mm(A1f, Sa[d][:], f2(gU[d][0]), start=True, stop=False)
            kb.mm(A1f, idb[:], NEGT[d][:], start=False, stop=True)
            kb.mm(f2(PZ), onesf[:], f2(gU[d][0]), start=True, stop=True)
            yield
            kb.act(f2(Ds[d][0]), A0f, AF.Exp)
            kb.act(f2(DT[d][0]), A1f, AF.Exp)
            kb.act(Er[d][0][:], PZ[:], AF.Exp)
            yield
            kb.mm(PZ[:, 0, 0:4], Ua[d][:], gcol, start=True, stop=True)
            kb.mm(PZ[:, 0, 4:8], Sa[d][:], gcol, start=True, stop=True)
            kb.mm(PZ[:, 0, 8:12], onesf[:], gcol, start=True, stop=True)
            e3 = E3[d][b]
            kb.act(e3[:].rearrange("p a h -> p (a h)"), PZ[:, 0, 0:12], AF.Exp)
            kb.tt("dve", be[d][0][:], bcol, e3[:, 0, :], ALU.mult)
            yield
            for h in range(4):
                kb.mm(A0[:, h, :], qk[:, 4 + h, :], qk[:, 4 + h, :], start=True, stop=True)
            for h in range(4):
                kb.mm(A1[:, h, :], qk[:, 4 + h, :], qk[:, h, :], start=True, stop=True)
            yield
            na_ = negA[d][0]
            for h in range(4):
                kb.stt("dve", na_[:, h, :], A0[:, h, :], nbcol[:, h:h + 1], Ds[d][0][:, h, :], ALU.mult, ALU.mult)
            kb.tt("dve", attnT[d][b][:], A1, DT[d][0][:], ALU.mult)
            yield
            bc = lambda m: m[:].unsqueeze(1).to_broadcast(B4)
            mp = MP[d][0]
            for h in range(4):
                kb.tr(PZ[:, h, :], na_[:, h, :], idf[:])
            kb.tt("pool", mp[:, :, 1, :], na_[:], bc(Dg), ALU.mult)
            kb.tt("pool", N1[d][0][:], na_[:], bc(O1[d]), ALU.mult)
            kb.tt("pool", N2[d][0][:], na_[:], bc(O2[d]), ALU.mult)
            yield
            kb.tt("dve", mp[:, :, 0, :], PZ[:], bc(Dg), ALU.mult)
            kb.tt("dve", N1T[d][0][:], PZ[:], bc(O1T[d]), ALU.mult)
            yield
            kb.tt("pool", Tm[d][0][:], mp[:, :, 1, :], bc(idf), ALU.add)
            kb.tt("pool", Rm[d][0][:], mp[:, :, 0, :], bc(idf), ALU.add)
            yield
            cur = 0
            for it in range(1, 5):
                mpn = MP[d][1 - cur]
                mpc = MP[d][cur]
                for h in range(4):
                    kb.mm(PY[:, h, 0, :], mpc[:, h, 1, :], mpc[:, h, 0, :], start=True, stop=True)
                    kb.mm(PY[:, h, 1, :], mpc[:, h, 0, :], mpc[:, h, 1, :], start=True, stop=True)
                yield
                kb.copy("act", mpn[:, 0:2], PY[:, 0:2])
                kb.copy("act", mpn[:, 2:4], PY[:, 2:4])
                yield
                rn, rc_ = Rm[d][1 - cur], Rm[d][cur]
                tn, tc_ = Tm[d][1 - cur], Tm[d][cur]
                for h in range(4):
                    kb.mm(PZ[:, h, :], mpn[:, h, 1, :], rc_[:, h, :], start=True, stop=True)
                for h in range(4):
                    kb.mm(PY[:, h, 0, :], mpn[:, h, 0, :], tc_[:, h, :], start=True, stop=True)
                yield
                kb.tt("dve", rn[:], PZ[:], rc_[:], ALU.add)
                kb.tt("dve", tn[:], PY[:, :, 0, :], tc_[:], ALU.add)
                yield
                cur = 1 - cur
            R0_, T0_ = Rm[d][cur], Tm[d][cur]
            R1_, T1_ = Rm[d][1 - cur], Tm[d][1 - cur]
            for h in range(4):
                kb.mm(PZ[:, h, :], N1T[d][0][:, h, :], T0_[:, h, :], start=True, stop=True)
            for h in range(4):
                kb.mm(PY[:, h, 0, :], N1[d][0][:, h, :], R0_[:, h, :], start=True, stop=True)
            yield
            kb.copy("act", Yb[d][0][:], PZ[:])
            kb.copy("act", Xb[d][0][:], PY[:, :, 0, :])
            yield
            for h in range(4):
                kb.mm(PZ[:, h, :], R0_[:, h, :], Yb[d][0][:, h, :], start=True, stop=True)
            for h in range(4):
                kb.mm(PY[:, h, 0, :], T0_[:, h, :], Xb[d][0][:, h, :], start=True, stop=True)
            yield
            kb.tt("dve", T1_[:], PZ[:], T0_[:], ALU.add)
            kb.tt("dve", R1_[:], PY[:, :, 0, :], R0_[:], ALU.add)
            yield
            for h in range(4):
                kb.mm(PZ[:, h, :], N2[d][0][:, h, :], R1_[:, h, :], start=True, stop=True)
            yield
            kb.copy("act", X2b[d][0][:], PZ[:])
            yield
            for h in range(4):
                kb.mm(PZ[:, h, :], T1_[:, h, :], X2b[d][0][:, h, :], start=True, stop=True)
            yield
            r = Rb[d][0]
            kb.tt("dve", r[:], PZ[:], R1_[:], ALU.add)
            kb.tt("pool", bV[d][0][:], kv[:, 1, :].rearrange("p (h v) -> p h v", h=4), bcol.unsqueeze(2).to_broadcast(B4), ALU.mult)
            kb.tt("pool", Kbe[d][0][:], kv[:, 0, :].rearrange("p (h v) -> p h v", h=4), be[d][0][:].unsqueeze(2).to_broadcast(B4), ALU.mult)
            kb.tt("pool", Kd[d][b][:], kv[:, 0, :].rearrange("p (h v) -> p h v", h=4), e3[:, 1, :].unsqueeze(2).to_broadcast(B4), ALU.mult)
            kb.tt("dve", QdT[d][b][:], qk[:, 0:4, :], Er[d][0][:], ALU.mult)
            yield
            for h in range(4):
                kb.mm(PZ[:, h, :], r[:, h, :], bV[d][0][:, h, :], start=True, stop=True)
            for h in range(4):
                kb.mm(A0[:, h, :], Kbe[d][0][:, h, :], r[:, h, :], start=True, stop=True)
            yield
            kb.copy("act", U[d][b][:], PZ[:])
            kb.copy("act", WT[d][b][:], A0)
            yield

        def scan(s, d):
            t = order[d][s]
            b = s % 2
            tok = slice(t * 128, (t + 1) * 128)
            for h in range(4):
                kb.mm(pW[:, h, :], WT[d][b][:, h, :], Sb[d][:, h, :], start=True, stop=True)
            yield
            kb.tt("dve", vnew[d][0][:], U[d][b][:], pW[:], ALU.subtract)
            yield
            for h in range(4):
                kb.mm(pOo[:, h, :], Sb[d][:, h, :], QdT[d][b][:, h, :], start=True, stop=False)
                kb.mm(pOo[:, h, :], vnew[d][0][:, h, :], attnT[d][b][:, h, :], start=False, stop=True)
            for h in range(4):
                kb.mm(pW[:, h, :], Kd[d][b][:, h, :], vnew[d][0][:, h, :], start=True, stop=True)
            yield
            kb.copy("act", osb[d][b][:], pOo[:])
            kb.dma("pool", OT[d][:, :, tok], osb[d][b][:])
            for h in range(4):
                kb.stt("dve", S[d][:, h, :], S[d][:, h, :], E3[d][b][:, 2, h:h + 1], pW[:, h, :], ALU.mult, ALU.add)
            yield
            kb.copy("act", Sb[d][:], S[d][:])
            yield

        def run(gens):
            gens = list(gens)
            while gens:
                for g_ in list(gens):
                    try:
                        next(g_)
                    except StopIteration:
                        gens.remove(g_)

        run([pre(0, 0), pre(0, 1)])
        for s in range(NT):
            def scans(s_):
                yield from scan(s_, 0)
                yield from scan(s_, 1)
            def limited(g_, n_):
                for _ in range(n_):
                    try:
                        next(g_)
                    except StopIteration:
                        return
                    yield
            gl = [scans(s), limited(castg, 6)]
            if s + 1 < NT:
                gl += [pre(s + 1, 0), pre(s + 1, 1)]
            run(gl)
        for _ in castg:
            pass


def stage_g3(kb, D, l, last):
    with kb.stage():
        meanb = kb.sb("meanb", [128, 128], BF16)
        kb.memset("dve", meanb[:], 1.0 / 128.0)
        gw = kb.sb("gw", [128, 1], F32)
        kb.dma("sp", gw[:], D["gdn_norm_w"][l].rearrange("(p o) -> p o", o=1))
        ntok_all = T_LAT if last else T_ALL
        zt = [kb.sb("zt", [128, T_ALL], BF16) for _ in range(2)]
        sz = [kb.sb("sz", [128, T_ALL], F32) for _ in range(2)]

        def mk2(name, shape, dt):
            return [[kb.sb(name, shape, dt) for _ in range(2)] for _ in range(2)]

        of = mk2("of", [128, 512], F32)
        obk = mk2("obk", [128, 512], F32)
        sq = mk2("sq", [128, 512], BF16)
        rs = mk2("rs", [128, 512], F32)
        yo = mk2("yo", [128, 512], BF16)
        pn = [kb.ps("pn", [128, 512]) for _ in range(2)]
        blocks = [(b * 512, 512) for b in range(8)] + ([] if last else [(T_LAT, 256)])

        def headgen(h, sid):
            kb.dma("sp", zt[sid][:, :ntok_all], D["FM_d"][2560 + h * 128:2560 + (h + 1) * 128, 0:ntok_all])
            yield
            for c0 in range(0, ntok_all, 1088):
                c1 = min(ntok_all, c0 + 1088)
                kb.act(sz[sid][:, c0:c1], zt[sid][:, c0:c1], AF.Silu)
                yield
            for bi, (tok0, n) in enumerate(blocks):
                j = bi % 2
                o_, ob_, sq_, rs_, yo_ = of[sid][j], obk[sid][j], sq[sid][j], rs[sid][j], yo[sid][j]
                kb.dma("sp", o_[:, :n], D["oT_d"][0, h * 128:(h + 1) * 128, tok0:tok0 + n])
                kb.dma("sp", ob_[:, :n], D["oT_d"][1, h * 128:(h + 1) * 128, tok0:tok0 + n])
                yield
                kb.tt("pool", o_[:, :n], o_[:, :n], ob_[:, :n], ALU.add)
                yield
                kb.tt("pool", sq_[:, :n], o_[:, :n], o_[:, :n], ALU.mult)
                yield
                kb.mm(pn[sid][:, :n], meanb[:], sq_[:, :n])
                yield
                kb.act(rs_[:, :n], pn[sid][:, :n], AF.Ln, bias=NORM_EPS)
                kb.act(rs_[:, :n], rs_[:, :n], AF.Exp, scale=-0.5)
                yield
                kb.tt("dve", o_[:, :n], o_[:, :n], rs_[:, :n], ALU.mult)
                yield
                kb.stt("dve", yo_[:, :n], o_[:, :n], gw[:, 0:1], sz[sid][:, tok0:tok0 + n], ALU.mult, ALU.mult)
                kb.dma("pool", D["aT_d"][512 + h * 128:512 + (h + 1) * 128, tok0:tok0 + n], yo_[:, :n])
                yield

        for h0 in range(0, 4, 2):
            run_gens([headgen(h0, 0), headgen(h0 + 1, 1)])


def layer_norm_gen(kb, r, out, stt_, mv, rs, gam, bet):
    for hh in range(2):
        kb.op("dve", lambda g: g.bn_stats(out=stt_[:, hh, :], in_=r[:, hh * 512:(hh + 1) * 512]), [r[:]], [stt_[:]])
    kb.op("dve", lambda g: g.bn_aggr(out=mv[:], in_=stt_[:]), [stt_[:]], [mv[:]])
    yield
    kb.act(rs[:, 0:1], mv[:, 1:2], AF.Ln, bias=LN_EPS)
    kb.act(rs[:, 0:1], rs[:, 0:1], AF.Exp, scale=-0.5)
    yield
    kb.stt("dve", rs[:, 1:2], mv[:, 0:1], -1.0, rs[:, 0:1], ALU.mult, ALU.mult)
    yield
    kb.act(out[:], r[:], AF.Identity, scale=rs[:, 0:1], bias=rs[:, 1:2])
    yield
    kb.tt("pool", out[:], out[:], gam[:], ALU.mult)
    kb.tt("pool", out[:], out[:], bet[:], ALU.add)
    yield


def layer_norm_tile(kb, r, out, stt_, mv, rs, gam, bet):
    for _ in layer_norm_gen(kb, r, out, stt_, mv, rs, gam, bet):
        pass


def stage4a(kb, D, l, xsrc, last):
    with kb.stage():
        wo = kb.sb("wo", [128, 8, DM], BF16)
        for kc in range(8):
            kb.dma("sp", wo[:, kc, :], D["woutb"][l, kc * 128:(kc + 1) * 128, :])
        idb = cload(kb, D, "ident", BF16)
        lg = bvec(kb, "ln1g", D["ln1_g"][l])
        lb = bvec(kb, "ln1b", D["ln1_b"][l])
        mods = {}
        for r_, nm in ((0, "l"), (1, "c")):
            if nm == "c" and last:
                continue
            mods[nm] = (bvec(kb, "g1", D["mod_d"][l, r_, 2048:3072]), bvec(kb, "sc2", D["mod_d"][l, r_, 4096:5120]),
                        bvec(kb, "sh2", D["mod_d"][l, r_, 3072:4096]))
        aT = [kb.sb("aT", [128, 8, 512], BF16) for _ in range(2)]
        NS = 4
        xt = [kb.sb("xt", [128, DM], F32) for _ in range(NS)]
        rr = [kb.sb("rr", [128, DM], F32) for _ in range(NS)]
        x1 = [kb.sb("x1", [128, DM], F32) for _ in range(NS)]
        hb = [kb.sb("hb", [128, DM], BF16) for _ in range(NS)]
        hT = [kb.sb("hT", [128, 8, 512], BF16) for _ in range(2)]
        stt_ = [kb.sb("stt", [128, 2, 6], F32) for _ in range(NS)]
        mv = [kb.sb("mv", [128, 2], F32) for _ in range(NS)]
        rs = [kb.sb("rs", [128, 2], F32) for _ in range(NS)]
        py = [kb.ps("py", [128, 512]) for _ in range(NS)]
        pT = [kb.ps("pT", [128, 1024], BF16) for _ in range(NS)]
        aTv = D["aT_d"].rearrange("(k p) t -> p k t", p=128)
        hTv = D["h2T_d"].rearrange("(k p) t -> p k t", p=128)
        for g in range(9):
            if g == 8 and last:
                continue
            ntile = 4 if g < 8 else 2
            ntok = ntile * 128
            tok0 = g * 512
            g1, sc2, sh2 = mods["l"] if g < 8 else mods["c"]
            a_ = aT[g % 2]
            hTg = hT[g % 2]
            kb.dma("sp", a_[:, :, :ntok], aTv[:, :, tok0:tok0 + ntok])

            def tilegen(tt_, j):
                rows = slice(tok0 + tt_ * 128, tok0 + (tt_ + 1) * 128)
                kb.dma("sp", xt[j][:], xsrc[rows, :])
                for hh in range(2):
                    cs_ = slice(hh * 512, (hh + 1) * 512)
                    for kc in range(8):
                        kb.mm(py[j][:], a_[:, kc, tt_ * 128:(tt_ + 1) * 128], wo[:, kc, cs_], start=(kc == 0), stop=(kc == 7))
                    yield
                    kb.tt("dve", rr[j][:, cs_], py[j][:], g1[:, cs_], ALU.mult)
                    yield
                kb.stt("dve", rr[j][:], xt[j][:], ALPHA, rr[j][:], ALU.mult, ALU.add)
                yield
                yield from layer_norm_gen(kb, rr[j], x1[j], stt_[j], mv[j], rs[j], lg, lb)
                kb.dma("pool", D["X1_d"][rows, :], x1[j][:])
                kb.tt("dve", rr[j][:], x1[j][:], sc2[:], ALU.mult)
                kb.tt("dve", hb[j][:], rr[j][:], sh2[:], ALU.add)
                yield
                for kc in range(8):
                    kb.tr(pT[j][:, kc * 128:(kc + 1) * 128], hb[j][:, kc * 128:(kc + 1) * 128], idb[:])
                yield
                kb.act(hTg[:, :, tt_ * 128:(tt_ + 1) * 128], pT[j][:].rearrange("p (k t) -> p k t", k=8), AF.Copy)
                yield

            run_gens([tilegen(t_, t_) for t_ in range(ntile)])
            kb.dma("pool", hTv[:, :, tok0:tok0 + ntok], hTg[:, :, :ntok])


def stage4b(kb, D, l, last):
    ngroups = 8 if last else 9
    m1v = D["m1T_d"].rearrange("(j p) t -> p j t", p=128)
    hTv = D["h2T_d"].rearrange("(k p) t -> p k t", p=128)
    with kb.stage():
        w1 = kb.sb("w1", [128, 8, 4 * DM], BF16)
        for kc in range(8):
            kb.dma("sp", w1[:, kc, :], D["w1b"][l, kc * 128:(kc + 1) * 128, :])
        hT = [kb.sb("hT", [128, 8, 512], BF16) for _ in range(2)]
        rl = [kb.sb("rl", [128, 512], F32) for _ in range(4)]
        mo = [kb.sb("mo", [128, 8, 512], BF16) for _ in range(3)]
        pm = [kb.ps("pm", [128, 512]) for _ in range(4)]
        mi = 0
        oi = 0
        for g in range(ngroups):
            ntok = 512 if g < 8 else 256
            tok0 = g * 512
            h_ = hT[g % 2]
            kb.dma("sp", h_[:, :, :ntok], hTv[:, :, tok0:tok0 + ntok])
            for j in range(32):
                p = pm[mi % 4]
                r_ = rl[mi % 4]
                if j % 8 == 0:
                    m_ = mo[oi % 3]
                    oi += 1
                for kc in range(8):
                    kb.mm(p[:, :ntok], w1[:, kc, j * 128:(j + 1) * 128], h_[:, kc, :ntok], start=(kc == 0), stop=(kc == 7))
                kb.act(r_[:, :ntok], p[:, :ntok], AF.Relu)
                kb.tt("dve" if mi % 2 == 0 else "pool", m_[:, j % 8, :ntok], r_[:, :ntok], r_[:, :ntok], ALU.mult)
                mi += 1
                if j % 8 == 7:
                    kb.dma("pool", m1v[:, j - 7:j + 1, tok0:tok0 + ntok], m_[:, :, :ntok])
    with kb.stage():
        w2 = kb.sb("w2", [128, 32, DM], BF16)
        w2v = D["w2b"][l].rearrange("(j p) c -> p j c", p=128)
        for j0 in range(0, 32, 8):
            kb.dma("sp", w2[:, j0:j0 + 8, :], w2v[:, j0:j0 + 8, :])
        lg = bvec(kb, "ln2g", D["ln2_g"][l])
        lb = bvec(kb, "ln2b", D["ln2_b"][l])
        g2 = {"l": bvec(kb, "g2", D["mod_d"][l, 0, 5120:6144])}
        if not last:
            g2["c"] = bvec(kb, "g2c", D["mod_d"][l, 1, 5120:6144])
        mt = [kb.sb("mt", [128, 32, 512], BF16) for _ in range(2)]
        x1 = [kb.sb("x1", [128, DM], F32) for _ in range(2)]
        rr = [kb.sb("rr", [128, DM], F32) for _ in range(2)]
        x2 = [kb.sb("x2", [128, DM], F32) for _ in range(2)]
        stt_ = [kb.sb("stt", [128, 2, 6], F32) for _ in range(2)]
        mv = [kb.sb("mv", [128, 2], F32) for _ in range(2)]
        rs = [kb.sb("rs", [128, 2], F32) for _ in range(2)]
        po = [kb.ps("po", [128, 512]) for _ in range(4)]
        dst = D["y"] if last else D["X_d"]
        ui = 0
        ti = 0
        for g in range(ngroups):
            ntile = 4 if g < 8 else 2
            ntok = ntile * 128
            tok0 = g * 512
            m_ = mt[g % 2]
            for j0 in range(0, 32, 8):
                kb.dma("sp", m_[:, j0:j0 + 8, :ntok], m1v[:, j0:j0 + 8, tok0:tok0 + ntok])
            gg = g2["l"] if g < 8 else g2["c"]
            for t in range(ntile):
                i = ti % 2
                ti += 1
                rows = slice(tok0 + t * 128, tok0 + (t + 1) * 128)
                kb.dma("sp", x1[i][:], D["X1_d"][rows, :])
                for hh in range(2):
                    p = po[ui % 4]
                    ui += 1
                    cs_ = slice(hh * 512, (hh + 1) * 512)
                    for j in range(32):
                        kb.mm(p[:], m_[:, j, t * 128:(t + 1) * 128], w2[:, j, cs_], start=(j == 0), stop=(j == 31))
                    kb.tt("dve", rr[i][:, cs_], p[:], gg[:, cs_], ALU.mult)
                kb.stt("dve", rr[i][:], x1[i][:], ALPHA, rr[i][:], ALU.mult, ALU.add)
                layer_norm_tile(kb, rr[i], x2[i], stt_[i], mv[i], rs[i], lg, lb)
                kb.dma("pool", dst[rows, :], x2[i][:])


_CACHE = {}


def host_inputs(b, inp, shared):
    m = dict(shared)
    m["xin"] = np.ascontiguousarray(np.concatenate([inp["x"][b], inp["ctx"][b]], axis=0), dtype=np.float32)
    m["cvec"] = np.ascontiguousarray(np.stack([inp["c"][b], inp["c_ctx"]], axis=0), dtype=np.float32)
    return m


def shared_inputs(inp):
    f = lambda a: np.ascontiguousarray(a, dtype=np.float32)
    sh = {"consts": make_consts(), "rope": make_rope(), "Gp": make_rpb_gather(np.asarray(inp["rpb"], np.float32))}
    for k in ("w_ada", "b_ada", "w_in", "conv_w", "gdn_norm_w", "w_out", "ln1_g", "ln1_b", "w_mlp1", "w_mlp2", "ln2_g", "ln2_b"):
        sh[k] = f(inp[k])
    sh["a_log"] = f(inp["a_log"]).reshape(DEPTH, 8)
    sh["dt_bias"] = f(inp["dt_bias"]).reshape(DEPTH, 8)
    return sh


def kernel(**inputs):
    inp = {k: np.asarray(v) for k, v in inputs.items()}
    if "nc" not in _CACHE:
        _CACHE["nc"] = build()
    nc = _CACHE["nc"]
    shared = shared_inputs(inp)
    in_maps = [host_inputs(b, inp, shared) for b in range(8)]
    res = run_bass_kernel_spmd(nc, in_maps, core_ids=list(range(8)))
    return np.stack([np.asarray(r["y"], dtype=np.float32) for r in res.results], axis=0)
```
